# Optimizing a Trainium2 kernel written in Bass

```python
import math
import jax, jax.numpy as jnp
from jax import lax
import numpy as np

D_MODEL = 2048
BATCH = 16
SEQ = 256
DEPTH = 4
DEC_BATCH = 8
DEC_SEQ = 4096
PAST_LEN = 512

GRID_W = 64
N_MIXERS = 3
N_ATTN = (DEPTH + 2) // 3
N_SSD = (DEPTH + 1) // 3
N_LRU = DEPTH // 3
HEAD_DIM = 128
N_HEADS = D_MODEL // HEAD_DIM
N_KV_HEADS = 4
Q_PER_KV = N_HEADS // N_KV_HEADS
QKV_DIM = (N_HEADS + 2 * N_KV_HEADS) * HEAD_DIM
ROPE_THETA = 10000.0
Q_BLOCK = 128
SSD_D_INNER = 2 * D_MODEL
SSD_HEAD_DIM = 64
SSD_HEADS = SSD_D_INNER // SSD_HEAD_DIM
SSD_GROUPS = 8
SSD_HEADS_PER_GROUP = SSD_HEADS // SSD_GROUPS
SSD_STATE = 128
SSD_CHUNK = 128
SSD_CONV_DIM = SSD_D_INNER + 2 * SSD_GROUPS * SSD_STATE
SSD_IN_DIM = SSD_D_INNER + SSD_CONV_DIM + 2 * SSD_HEADS
CONV_WIDTH = 4
LRU_WIDTH = D_MODEL
LRU_BLOCKS = 16
LRU_BLOCK_W = LRU_WIDTH // LRU_BLOCKS
LRU_C = 8.0
D_FF = -(-8 * D_MODEL // (3 * 256)) * 256
EPS = 1e-6

kernel_name = 'hybrid_prefix_diffusion_step'

F32 = jnp.float32


def rms_norm(x, g):
    xf = x.astype(F32)
    y = xf * lax.rsqrt(jnp.mean(xf * xf, axis=-1, keepdims=True) + EPS)
    return (y * g.astype(F32)).astype(x.dtype)


def adaln(cond, w, b):
    m = (jax.nn.silu(cond) @ w + b)[:, None, :]
    return jnp.split(m, 6, axis=-1)


def modulate(h, shift, scale):
    return h * (1 + scale) + shift


def axial_rope(rows):
    row = jnp.repeat(jnp.arange(rows, dtype=F32), GRID_W)
    col = jnp.tile(jnp.arange(GRID_W, dtype=F32), rows)
    n_freq = HEAD_DIM // 4
    inv = ROPE_THETA ** (-jnp.arange(n_freq, dtype=F32) / n_freq)
    ang = jnp.concatenate([row[:, None] * inv, col[:, None] * inv], axis=-1)
    return jnp.cos(ang), jnp.sin(ang)


def apply_rope(x, cos, sin):
    xf = x.astype(F32)
    x1, x2 = jnp.split(xf, 2, axis=-1)
    c, s = cos[None, :, None, :], sin[None, :, None, :]
    return jnp.concatenate([x1 * c - x2 * s, x2 * c + x1 * s], axis=-1).astype(x.dtype)


def blocked_attention(q, k, v):
    b, lq = q.shape[:2]
    nb = lq // Q_BLOCK
    qb = jnp.moveaxis(q.reshape((b, nb, Q_BLOCK) + q.shape[2:]), 1, 0)
    scale = HEAD_DIM ** -0.5

    def one_block(q_blk):
        s = jnp.einsum('bqkgd,bskd->bkgqs', q_blk, k).astype(F32) * scale
        p = jax.nn.softmax(s, axis=-1).astype(v.dtype)
        return jnp.einsum('bkgqs,bskd->bqkgd', p, v)

    o = lax.map(one_block, qb)
    return jnp.moveaxis(o, 0, 1).reshape(b, lq, N_HEADS * HEAD_DIM)


def attention_mixer(h, w_qkv, q_norm, k_norm, w_o, rope=None, k_ctx=None, v_ctx=None):
    b, l, _ = h.shape
    q, k, v = jnp.split(h @ w_qkv, [N_HEADS * HEAD_DIM, (N_HEADS + N_KV_HEADS) * HEAD_DIM], axis=-1)
    q = rms_norm(q.reshape(b, l, N_HEADS, HEAD_DIM), q_norm)
    k = rms_norm(k.reshape(b, l, N_KV_HEADS, HEAD_DIM), k_norm)
    v = v.reshape(b, l, N_KV_HEADS, HEAD_DIM)
    if rope is None:
        k_all, v_all = k, v
    else:
        q = apply_rope(q, *rope)
        k = apply_rope(k, *rope)
        k_all = jnp.concatenate([k, k_ctx], axis=1)
        v_all = jnp.concatenate([v, v_ctx], axis=1)
    o = blocked_attention(q.reshape(b, l, N_KV_HEADS, Q_PER_KV, HEAD_DIM), k_all, v_all)
    return o @ w_o, k, v


def centred_dwconv(x, w, bias):
    l = x.shape[1]
    left = (CONV_WIDTH - 1) // 2
    xp = jnp.pad(x, ((0, 0), (left, CONV_WIDTH - 1 - left), (0, 0)))
    return sum(xp[:, t:t + l] * w[t] for t in range(CONV_WIDTH)) + bias


def ssd_chunk_scan(x, dt, a, bm, cm, h0):
    b, l = x.shape[:2]
    nc = l // SSD_CHUNK
    G, HG = SSD_GROUPS, SSD_HEADS_PER_GROUP

    def to_chunks(t):
        return jnp.swapaxes(t.reshape((b, nc, SSD_CHUNK) + t.shape[2:]), 0, 1)

    xc = to_chunks(x.reshape(b, l, G, HG, SSD_HEAD_DIM))
    dtc = to_chunks(dt.reshape(b, l, G, HG))
    bc, cc = to_chunks(bm), to_chunks(cm)
    ag = a.reshape(G, HG)
    causal = jnp.tril(jnp.ones((SSD_CHUNK, SSD_CHUNK), dtype=bool))[None, :, :, None, None]

    def step(state, inp):
        x_k, dt_k, b_k, c_k = inp
        cum = jnp.cumsum(dt_k * ag, axis=1)
        seg = cum[:, :, None] - cum[:, None, :]
        decay = jnp.exp(jnp.where(causal, seg, -jnp.inf))
        scores = jnp.einsum('bign,bjgn->bijg', c_k, b_k)
        xdt = x_k * dt_k[..., None]
        y_intra = jnp.einsum('bijg,bijgh,bjghp->bighp', scores, decay, xdt)
        y_inter = jnp.einsum('bign,bghpn->bighp', c_k, state) * jnp.exp(cum)[..., None]
        to_end = jnp.exp(cum[:, -1:] - cum)
        state = state * jnp.exp(cum[:, -1])[..., None, None] + jnp.einsum('bjgh,bjghp,bjgn->bghpn', to_end, xdt, b_k)
        return state, y_intra + y_inter

    state0 = h0.astype(F32).reshape(b, G, HG, SSD_HEAD_DIM, SSD_STATE)
    h_T, ys = lax.scan(step, state0, (xc, dtc, bc, cc))
    y = jnp.swapaxes(ys, 0, 1).reshape(b, l, SSD_HEADS, SSD_HEAD_DIM)
    return y, h_T.reshape(b, SSD_HEADS, SSD_HEAD_DIM, SSD_STATE)


def ssd_mixer(h, h0_f, h0_b, w_in, conv_w, conv_b, dt_bias, a_log, d_skip, norm_w, w_out):
    b, l, _ = h.shape
    z, xbc, dt_raw = jnp.split(h @ w_in, [SSD_D_INNER, SSD_D_INNER + SSD_CONV_DIM], axis=-1)
    xbc = jax.nn.silu(centred_dwconv(xbc, conv_w, conv_b))
    xv, bm, cm = jnp.split(xbc, [SSD_D_INNER, SSD_D_INNER + SSD_GROUPS * SSD_STATE], axis=-1)
    xv = xv.reshape(b, l, SSD_HEADS, SSD_HEAD_DIM).astype(F32)
    bm = bm.reshape(b, l, SSD_GROUPS, SSD_STATE).astype(F32)
    cm = cm.reshape(b, l, SSD_GROUPS, SSD_STATE).astype(F32)
    dt = jax.nn.softplus(dt_raw.reshape(b, l, 2, SSD_HEADS).astype(F32) + dt_bias.astype(F32))
    a = -jnp.exp(a_log.astype(F32))
    y_f, h_f = ssd_chunk_scan(xv, dt[:, :, 0], a[0], bm, cm, h0_f)
    flip = lambda t: jnp.flip(t, axis=1)
    y_b, h_b = ssd_chunk_scan(flip(xv), flip(dt[:, :, 1]), a[1], flip(bm), flip(cm), h0_b)
    y = y_f + flip(y_b) + xv * d_skip.astype(F32)[:, None]
    y = y.reshape(b, l, SSD_D_INNER).astype(h.dtype)
    y = rms_norm(y * jax.nn.silu(z), norm_w)
    return y @ w_out, h_f.astype(h.dtype), h_b.astype(h.dtype)


def linear_recurrence(a, u, h0, reverse):
    first = -1 if reverse else 0
    u = u.at[:, first].add(a[:, first] * h0)

    def combine(e1, e2):
        a1, u1 = e1
        a2, u2 = e2
        return a1 * a2, a2 * u1 + u2

    _, hs = lax.associative_scan(combine, (a, u), reverse=reverse, axis=1)
    return hs, hs[:, 0] if reverse else hs[:, -1]


def rglru_mixer(h, h0_f, h0_b, w_in, conv_w, conv_b, w_a, b_a, w_i, b_i, a_param, w_out):
    b, l, _ = h.shape
    gate_branch, rec = jnp.split(h @ w_in, 2, axis=-1)
    rec = centred_dwconv(rec, conv_w, conv_b)
    rec_blk = rec.reshape(b, l, LRU_BLOCKS, LRU_BLOCK_W)
    rec32 = rec.astype(F32)
    outs, finals = [], []
    for d, h0 in enumerate((h0_f, h0_b)):
        r = jax.nn.sigmoid((jnp.einsum('blnw,nwv->blnv', rec_blk, w_a[d]).reshape(b, l, LRU_WIDTH) + b_a[d]).astype(F32))
        ig = jax.nn.sigmoid((jnp.einsum('blnw,nwv->blnv', rec_blk, w_i[d]).reshape(b, l, LRU_WIDTH) + b_i[d]).astype(F32))
        log_a = -LRU_C * r * jax.nn.softplus(-a_param[d].astype(F32))
        a = jnp.exp(log_a)
        u = jnp.sqrt(-jnp.expm1(2.0 * log_a)) * (ig * rec32)
        hs, h_T = linear_recurrence(a, u, h0.astype(F32), reverse=(d == 1))
        outs.append(hs)
        finals.append(h_T.astype(h.dtype))
    y = (outs[0] + outs[1]).astype(h.dtype) * jax.nn.gelu(gate_branch)
    return y @ w_out, finals[0], finals[1]


def swiglu_ffn(h, w_in, w_out):
    g, u = jnp.split(h @ w_in, 2, axis=-1)
    return (jax.nn.silu(g) * u) @ w_out


def setup_inputs(seed: int = 0) -> dict:
    key = jax.random.key(seed)
    keys = iter(jax.random.split(key, 48))
    D = D_MODEL

    def normal(shape, scale):
        return jax.random.normal(next(keys), shape, F32) * scale

    def uniform(shape, lo, hi):
        return jax.random.uniform(next(keys), shape, F32, lo, hi)

    x_prompt = normal((BATCH, SEQ, D), 1.0)
    x_sample = normal((DEC_BATCH, DEC_SEQ, D), 1.0)
    cache_k = normal((DEC_BATCH, N_ATTN, PAST_LEN, N_KV_HEADS, HEAD_DIM), 1.0)
    cache_v = normal((DEC_BATCH, N_ATTN, PAST_LEN, N_KV_HEADS, HEAD_DIM), 1.0)
    state_ssm = normal((DEC_BATCH, N_SSD, 2, SSD_HEADS, SSD_HEAD_DIM, SSD_STATE), 0.1)
    state_lru = normal((DEC_BATCH, N_LRU, 2, LRU_WIDTH), 0.5)
    c = normal((DEC_BATCH, D), 1.0)
    c_ctx = normal((D,), 1.0)
    ada_w = normal((DEPTH, D, 6 * D), 0.5 * D ** -0.5)
    ada_b = normal((DEPTH, 6 * D), 0.02)
    norm_mix = 1.0 + normal((DEPTH, D), 0.02)
    norm_ffn = 1.0 + normal((DEPTH, D), 0.02)
    attn_w_qkv = normal((N_ATTN, D, QKV_DIM), D ** -0.5)
    attn_q_norm = 1.0 + normal((N_ATTN, HEAD_DIM), 0.02)
    attn_k_norm = 1.0 + normal((N_ATTN, HEAD_DIM), 0.02)
    attn_w_o = normal((N_ATTN, N_HEADS * HEAD_DIM, D), (N_HEADS * HEAD_DIM) ** -0.5)
    ssd_w_in = normal((N_SSD, D, SSD_IN_DIM), D ** -0.5)
    ssd_conv_w = normal((N_SSD, CONV_WIDTH, SSD_CONV_DIM), CONV_WIDTH ** -0.5)
    ssd_conv_b = normal((N_SSD, SSD_CONV_DIM), 0.02)
    dt0 = jnp.exp(uniform((N_SSD, 2, SSD_HEADS), math.log(1e-3), math.log(1e-1)))
    ssd_dt_bias = dt0 + jnp.log(-jnp.expm1(-dt0))
    ssd_a_log = jnp.log(uniform((N_SSD, 2, SSD_HEADS), 1.0, 16.0))
    ssd_d = 1.0 + normal((N_SSD, SSD_HEADS), 0.1)
    ssd_norm = 1.0 + normal((N_SSD, SSD_D_INNER), 0.02)
    ssd_w_out = normal((N_SSD, SSD_D_INNER, D), SSD_D_INNER ** -0.5)
    lru_w_in = normal((N_LRU, D, 2 * LRU_WIDTH), D ** -0.5)
    lru_conv_w = normal((N_LRU, CONV_WIDTH, LRU_WIDTH), CONV_WIDTH ** -0.5)
    lru_conv_b = normal((N_LRU, LRU_WIDTH), 0.02)
    lru_w_a = normal((N_LRU, 2, LRU_BLOCKS, LRU_BLOCK_W, LRU_BLOCK_W), LRU_BLOCK_W ** -0.5)
    lru_b_a = normal((N_LRU, 2, LRU_WIDTH), 0.02)
    lru_w_i = normal((N_LRU, 2, LRU_BLOCKS, LRU_BLOCK_W, LRU_BLOCK_W), LRU_BLOCK_W ** -0.5)
    lru_b_i = normal((N_LRU, 2, LRU_WIDTH), 0.02)
    a0 = uniform((N_LRU, 2, LRU_WIDTH), 0.9, 0.999)
    s0 = a0 ** (1.0 / LRU_C)
    lru_a_param = jnp.log(s0) - jnp.log1p(-s0)
    lru_w_out = normal((N_LRU, LRU_WIDTH, D), LRU_WIDTH ** -0.5)
    ffn_w_in = normal((DEPTH, D, 2 * D_FF), D ** -0.5)
    ffn_w_out = normal((DEPTH, D_FF, D), D_FF ** -0.5)
    final_norm = 1.0 + normal((D,), 0.02)
    return {'x_prompt': x_prompt, 'x_sample': x_sample, 'cache_k': cache_k, 'cache_v': cache_v,
            'state_ssm': state_ssm, 'state_lru': state_lru, 'c': c, 'c_ctx': c_ctx,
            'ada_w': ada_w, 'ada_b': ada_b, 'norm_mix': norm_mix, 'norm_ffn': norm_ffn,
            'attn_w_qkv': attn_w_qkv, 'attn_q_norm': attn_q_norm, 'attn_k_norm': attn_k_norm, 'attn_w_o': attn_w_o,
            'ssd_w_in': ssd_w_in, 'ssd_conv_w': ssd_conv_w, 'ssd_conv_b': ssd_conv_b, 'ssd_dt_bias': ssd_dt_bias,
            'ssd_a_log': ssd_a_log, 'ssd_d': ssd_d, 'ssd_norm': ssd_norm, 'ssd_w_out': ssd_w_out,
            'lru_w_in': lru_w_in, 'lru_conv_w': lru_conv_w, 'lru_conv_b': lru_conv_b, 'lru_w_a': lru_w_a,
            'lru_b_a': lru_b_a, 'lru_w_i': lru_w_i, 'lru_b_i': lru_b_i, 'lru_a_param': lru_a_param,
            'lru_w_out': lru_w_out, 'ffn_w_in': ffn_w_in, 'ffn_w_out': ffn_w_out, 'final_norm': final_norm}


def reference(x_prompt, x_sample, cache_k, cache_v, state_ssm, state_lru, c, c_ctx,
              ada_w, ada_b, norm_mix, norm_ffn,
              attn_w_qkv, attn_q_norm, attn_k_norm, attn_w_o,
              ssd_w_in, ssd_conv_w, ssd_conv_b, ssd_dt_bias, ssd_a_log, ssd_d, ssd_norm, ssd_w_out,
              lru_w_in, lru_conv_w, lru_conv_b, lru_w_a, lru_b_a, lru_w_i, lru_b_i, lru_a_param, lru_w_out,
              ffn_w_in, ffn_w_out, final_norm):
    xp, xs = x_prompt, x_sample
    bp = xp.shape[0]
    rows = xs.shape[1] // GRID_W
    rope = axial_rope(rows)
    new_k, new_v, new_ssm, new_lru = [], [], [], []
    for i in range(DEPTH):
        kind, j = i % N_MIXERS, i // N_MIXERS
        p_sh1, p_sc1, p_g1, p_sh2, p_sc2, p_g2 = adaln(c_ctx[None, :], ada_w[i], ada_b[i])
        s_sh1, s_sc1, s_g1, s_sh2, s_sc2, s_g2 = adaln(c, ada_w[i], ada_b[i])
        hp = modulate(rms_norm(xp, norm_mix[i]), p_sh1, p_sc1)
        hs = modulate(rms_norm(xs, norm_mix[i]), s_sh1, s_sc1)
        if kind == 0:
            op, k_ctx, v_ctx = attention_mixer(hp, attn_w_qkv[j], attn_q_norm[j], attn_k_norm[j], attn_w_o[j])
            os_, _, _ = attention_mixer(hs, attn_w_qkv[j], attn_q_norm[j], attn_k_norm[j], attn_w_o[j],
                                        rope=rope, k_ctx=cache_k[:, j], v_ctx=cache_v[:, j])
            new_k.append(k_ctx)
            new_v.append(v_ctx)
        elif kind == 1:
            zero = jnp.zeros((bp, SSD_HEADS, SSD_HEAD_DIM, SSD_STATE), F32)
            w = (ssd_w_in[j], ssd_conv_w[j], ssd_conv_b[j], ssd_dt_bias[j], ssd_a_log[j], ssd_d[j], ssd_norm[j], ssd_w_out[j])
            op, h_f, h_b = ssd_mixer(hp, zero, zero, *w)
            os_, _, _ = ssd_mixer(hs, state_ssm[:, j, 0], state_ssm[:, j, 1], *w)
            new_ssm.append(jnp.stack([h_f, h_b], axis=1))
        else:
            zero = jnp.zeros((bp, LRU_WIDTH), F32)
            w = (lru_w_in[j], lru_conv_w[j], lru_conv_b[j], lru_w_a[j], lru_b_a[j], lru_w_i[j], lru_b_i[j], lru_a_param[j], lru_w_out[j])
            op, h_f, h_b = rglru_mixer(hp, zero, zero, *w)
            os_, _, _ = rglru_mixer(hs, state_lru[:, j, 0], state_lru[:, j, 1], *w)
            new_lru.append(jnp.stack([h_f, h_b], axis=1))
        xp = xp + p_g1 * op
        xs = xs + s_g1 * os_
        xp = xp + p_g2 * swiglu_ffn(modulate(rms_norm(xp, norm_ffn[i]), p_sh2, p_sc2), ffn_w_in[i], ffn_w_out[i])
        xs = xs + s_g2 * swiglu_ffn(modulate(rms_norm(xs, norm_ffn[i]), s_sh2, s_sc2), ffn_w_in[i], ffn_w_out[i])
    y_prompt = rms_norm(xp, final_norm)
    y_sample = rms_norm(xs, final_norm)
    new_cache_k = jnp.stack(new_k, axis=1)
    new_cache_v = jnp.stack(new_v, axis=1)
    new_state_ssm = jnp.stack(new_ssm, axis=1)
    new_state_lru = jnp.stack(new_lru, axis=1)
    return (y_prompt, y_sample, new_cache_k, new_cache_v, new_state_ssm, new_state_lru)
```

```python
import numpy as np
import ml_dtypes
import concourse.bass as bass
import concourse.mybir as mybir
from concourse.bass_utils import run_bass_kernel_spmd

F32 = mybir.dt.float32
BF16 = mybir.dt.bfloat16
ALU = mybir.AluOpType
AF = mybir.ActivationFunctionType

D = 2048
DC = 16
S_LEN = 4096
P_LEN = 256
T = S_LEN + 2 * P_LEN
NBLK = T // 512
DEPTH = 4
DFF = 5632
EPS = 1e-6
PAST = 512
TILES = [(0, 1024), (1024, 1024), (2048, 1024), (3072, 1024), (4096, 512)]
SEGS = [(0, S_LEN, 0), (S_LEN, P_LEN, 1), (S_LEN + P_LEN, P_LEN, 1)]
PADT = T + 7
SEGQ = [(0, S_LEN, 1), (S_LEN, P_LEN, S_LEN + 3), (S_LEN + P_LEN, P_LEN, S_LEN + P_LEN + 5)]


class Buf:
    __slots__ = ("w", "r")

    def __init__(self):
        self.w = None
        self.r = {}


class Eng:
    def __init__(self, name):
        self.name = name
        self.sem = None
        self.count = 0
        self.waited = {}
        self.rec = []


class Pool:
    def __init__(self, items):
        self.items = items
        self.i = 0

    def next(self):
        it = self.items[self.i % len(self.items)]
        self.i += 1
        return it


class Phase:
    def __init__(self, k):
        self.k = k
        self.cms = []

    def __enter__(self):
        assert self.k.cur_phase is None
        self.k.cur_phase = self
        return self

    def __exit__(self, *a):
        self.k.barrier()
        for cm in reversed(self.cms):
            cm.__exit__(None, None, None)
        self.k.cur_phase = None
        return False


class K:
    def __init__(self, dbg=None):
        self.nc = bass.Bass("TRN2", target_bir_lowering=False)
        self.E = {n: Eng(n) for n in ("pe", "act", "dve", "pool", "sp")}
        self.semof = {}
        self.sem_ctx = []
        for n, e in self.E.items():
            e.sem = self._sem("e_" + n)
            self.semof[n] = e.sem
        self.dslots = {}
        for q, n in (("sp", 44), ("pool", 28)):
            sl = []
            for i in range(n):
                key = "d%s%d" % (q, i)
                s = self._sem(key)
                self.semof[key] = s
                sl.append([key, s, 0])
            self.dslots[q] = Pool(sl)
        self.nsb = 0
        self.bar_sem = self._sem("bar")
        self.bar_count = 0
        self.cur_phase = None

    def _sem(self, name):
        cm = self.nc.semaphore(name)
        s = cm.__enter__()
        self.sem_ctx.append(cm)
        return s

    def sb(self, shape, dt, name=None):
        self.nsb += 1
        name = (name or "sb") + "_%d" % self.nsb
        if self.cur_phase is not None:
            cm = self.nc.sbuf_tensor(name, list(shape), dt)
            t = cm.__enter__()
            self.cur_phase.cms.append(cm)
            return t
        return self.nc.alloc_sbuf_tensor(name, list(shape), dt)

    def sbpool(self, n, shape, dt, name):
        return Pool([(self.sb(shape, dt, "%s%d" % (name, i)), Buf()) for i in range(n)])

    def dram(self, name, shape, dt):
        return self.nc.dram_tensor(name, list(shape), dt).ap()

    def _waits(self, e, reads, writes):
        need = {}
        for b in reads:
            w = b.w
            if w is not None and need.get(w[0], 0) < w[1]:
                need[w[0]] = w[1]
        for b in writes:
            w = b.w
            if w is not None and need.get(w[0], 0) < w[1]:
                need[w[0]] = w[1]
            for k, v in b.r.items():
                if need.get(k, 0) < v:
                    need[k] = v
        for k, v in need.items():
            if k == "pe" and e.name == "pe":
                continue
            if e.waited.get(k, 0) >= v:
                continue
            e.waited[k] = v
            sem = self.semof[k]
            e.rec.append(lambda h, sem=sem, v=v: h.wait_ge(sem, v))

    def _mark(self, ev, reads, writes):
        k, v = ev
        for b in reads:
            if b.r.get(k, 0) < v:
                b.r[k] = v
        for b in writes:
            b.w = ev
            b.r = {}

    def op(self, en, fn, reads=(), writes=()):
        e = self.E[en]
        self._waits(e, reads, writes)
        e.count += 1
        sem = e.sem
        e.rec.append(lambda h, fn=fn, sem=sem: fn(h).then_inc(sem, 1))
        self._mark((en, e.count), reads, writes)

    def mm(self, fns, reads, writes):
        e = self.E["pe"]
        self._waits(e, reads, writes)
        for fn in fns[:-1]:
            e.rec.append(fn)
        e.count += 1
        sem = e.sem
        fn = fns[-1]
        e.rec.append(lambda h, fn=fn, sem=sem: fn(h).then_inc(sem, 1))
        self._mark(("pe", e.count), reads, writes)

    def dma(self, q, out, in_, reads=(), writes=(), nc_ok=False):
        e = self.E[q]
        self._waits(e, reads, writes)
        slot = self.dslots[q].next()
        key, sem, val = slot
        if val > 0 and e.waited.get(key, 0) < val:
            e.waited[key] = val
            e.rec.append(lambda h, sem=sem, v=val: h.wait_ge(sem, v))
        slot[2] = val + 16
        if nc_ok:
            e.rec.append(lambda h, out=out, in_=in_, sem=sem: h.dma_start(out=out, in_=in_, allow_slow_non_contiguous=True).then_inc(sem, 16))
        else:
            e.rec.append(lambda h, out=out, in_=in_, sem=sem: h.dma_start(out=out, in_=in_).then_inc(sem, 16))
        self._mark((key, val + 16), reads, writes)

    def barrier(self):
        sp = self.E["sp"]
        allk = {}
        for q in ("sp", "pool"):
            for key, sem, val in self.dslots[q].items:
                if val > 0:
                    allk[key] = val
        for n in ("pe", "act", "dve", "pool"):
            if self.E[n].count > 0:
                allk[n] = self.E[n].count
        for key, val in allk.items():
            if sp.waited.get(key, 0) < val:
                sp.waited[key] = val
                sp.rec.append(lambda h, sem=self.semof[key], v=val: h.wait_ge(sem, v))
        self.bar_count += 1
        bc = self.bar_count
        sp.rec.append(lambda h, sem=self.bar_sem: h.sem_inc(sem, 1))
        for n in ("pe", "act", "dve", "pool"):
            e = self.E[n]
            e.rec.append(lambda h, sem=self.bar_sem, v=bc: h.wait_ge(sem, v))
            for key, val in allk.items():
                if e.waited.get(key, 0) < val:
                    e.waited[key] = val

    def phase(self):
        return Phase(self)

    def finish(self):
        e = self.E["sp"]
        for q in ("sp", "pool"):
            for key, sem, val in self.dslots[q].items:
                if val > 0:
                    e.rec.append(lambda h, sem=sem, v=val: h.wait_ge(sem, v))
        for n in ("pe", "act", "dve"):
            en = self.E[n]
            if en.count > 0:
                e.rec.append(lambda h, sem=en.sem, v=en.count: h.wait_ge(sem, v))
        nc = self.nc
        E = self.E
        with nc.Block() as block:
            @block.tensor
            def _(h):
                for f in E["pe"].rec:
                    f(h)

            @block.scalar
            def _(h):
                for f in E["act"].rec:
                    f(h)

            @block.vector
            def _(h):
                for f in E["dve"].rec:
                    f(h)

            @block.gpsimd
            def _(h):
                for f in E["pool"].rec:
                    f(h)

            @block.sync
            def _(h):
                for f in E["sp"].rec:
                    f(h)
        return nc


def _act(out, in_, func, **kw):
    return lambda h: h.activation(out=out, in_=in_, func=func, **kw)


def _tt(out, a, b, op):
    return lambda h: h.tensor_tensor(out=out, in0=a, in1=b, op=op)


def _stt(out, in0, scalar, in1, op0, op1):
    return lambda h: h.scalar_tensor_tensor(out=out, in0=in0, scalar=scalar, in1=in1, op0=op0, op1=op1)


def _ts(out, in0, s1, s2, op0, op1=None):
    if op1 is None:
        return lambda h: h.tensor_scalar(out=out, in0=in0, scalar1=s1, scalar2=None, op0=op0)
    return lambda h: h.tensor_scalar(out=out, in0=in0, scalar1=s1, scalar2=s2, op0=op0, op1=op1)


def _cp(out, in_):
    return lambda h: h.tensor_copy(out=out, in_=in_)


def _mmf(out, lhsT, rhs, start=True, stop=True):
    return lambda h: h.matmul(out, lhsT=lhsT, rhs=rhs, start=start, stop=stop)


def _tp(out, in_, ident):
    return lambda h: h.transpose(out, in_, ident)


class Net:
    def __init__(self, cfg):
        self.cfg = cfg
        k = self.k = K()
        nc = k.nc
        self.ins = {}
        self.outs = {}

        def inp(name, shape, dt=F32):
            self.ins[name] = nc.dram_tensor(name, list(shape), dt, kind="ExternalInput").ap()
            return self.ins[name]

        def outp(name, shape):
            self.outs[name] = nc.dram_tensor(name, list(shape), F32, kind="ExternalOutput").ap()
            return self.outs[name]

        inp("x_all", [T, D])
        inp("cache_k", [2, PAST, 512]); inp("cache_v", [2, PAST, 512])
        inp("state_ssm", [2, 4096, 128]); inp("state_lru", [2, D])
        inp("cond", [2, D])
        inp("ada_w", [DEPTH, D, 6 * D]); inp("ada_b", [DEPTH, 6 * D])
        inp("norm_mix", [DEPTH, D]); inp("norm_ffn", [DEPTH, D])
        inp("attn_w_qkv", [2, D, 3072]); inp("attn_q_norm", [2, 128]); inp("attn_k_norm", [2, 128]); inp("attn_w_o", [2, D, D])
        inp("ssd_w_in", [1, D, 10368]); inp("ssd_conv_w", [1, 4, 6144]); inp("ssd_conv_b", [1, 6144])
        inp("ssd_dt_bias", [1, 128]); inp("ssd_a_log", [1, 128]); inp("ssd_d", [1, 64]); inp("ssd_norm", [1, 4096])
        inp("ssd_w_out", [1, 4096, D])
        inp("lru_w_in", [1, D, 4096]); inp("lru_conv_w", [1, 4, D]); inp("lru_conv_b", [1, D])
        inp("lru_w_a", [1, 2, 16, 128, 128]); inp("lru_b_a", [1, 2, D]); inp("lru_w_i", [1, 2, 16, 128, 128]); inp("lru_b_i", [1, 2, D])
        inp("lru_a_param", [1, 2, D]); inp("lru_w_out", [1, D, D])
        inp("ffn_w_in", [DEPTH, D, 2 * DFF]); inp("ffn_w_out", [DEPTH, DFF, D])
        inp("final_norm", [D])
        inp("c_ident", [128, 128]); inp("c_rot", [128, 128]); inp("c_cos", [128, S_LEN]); inp("c_sin", [128, S_LEN])
        inp("c_tri", [4, 128, 128])
        outp("y_all", [T, D])
        outp("nk", [2, 2, P_LEN, 512]); outp("nv", [2, 2, P_LEN, 512])
        outp("nssm", [2, 2, 4096, 128]); outp("nlru", [2, 2, D])
        if cfg.get("dbg"):
            outp("dbg_xT", [D, T])

        self.xT = k.dram("xT", [D, T], F32)
        self.xT_b = [[Buf() for _ in range(NBLK)] for _ in range(DC)]
        self.qT = k.dram("qT", [D, T], BF16)
        self.kT = k.dram("kT", [512, T], BF16)
        self.vtok = k.dram("vtok", [T, 512], BF16)
        self.oT = k.dram("oT", [4096, T], BF16)
        self.xbcT = k.dram("xbcT", [6144, T], F32)
        self.zsT = k.dram("zsT", [4096, T], BF16)
        self.dtk = k.dram("dtk", [T, 2, 128], F32)
        self.xtok = k.dram("xtok", [T, 4096], BF16)
        self.btok = k.dram("btok", [T, 1024], BF16)
        self.bcT = k.dram("bcT", [2048, T], BF16)
        self.ytok = k.dram("ytok", [2, T, 4096], F32)
        self.ident = k.sb([128, 128], F32, "ident"); self.ident_b = Buf()
        self.identh = k.sb([128, 128], BF16, "identh")
        self.ones_h = k.sb([128, 128], BF16, "ones_h")
        self.ones_f = k.sb([128, 128], F32, "ones_f")
        self.rot_h = k.sb([128, 128], BF16, "rot_h")
        self.cb = Buf()
        k.dma("sp", self.ident[:, :], self.ins["c_ident"][:, :], writes=[self.cb])
        k.dma("pool", self.identh[:, :], self.ins["c_ident"][:, :], writes=[self.cb])
        k.dma("pool", self.rot_h[:, :], self.ins["c_rot"][:, :], writes=[self.cb])
        k.op("dve", lambda h: h.memset(self.ones_h[:, :], 1.0), writes=[self.cb])
        k.op("dve", lambda h: h.memset(self.ones_f[:, :], 1.0), writes=[self.cb])
        self.epsb = k.sb([128, 1], F32, "epsb")
        k.op("dve", lambda h: h.memset(self.epsb[:, :], EPS), writes=[self.cb])
        self.ps = Pool([(nc.alloc_psum_tensor("ps%d" % i, [128, 512], F32), Buf()) for i in range(4)])
        self.psacc = Pool([(nc.alloc_psum_tensor("pa%d" % i, [128, 512], F32), Buf()) for i in range(4)])
        self.zcol = k.sb([128, 1], F32, "zcol")
        k.op("dve", lambda h: h.memset(self.zcol[:, :], 0.0), writes=[self.cb])
        self.t512 = k.sbpool(6, [128, 512], F32, "t512")
        self.rsp = k.sbpool(3, [128, 512], F32, "rsp")
        self.sgp = k.sbpool(3, [128, 512], F32, "sgp")
        self.h512 = k.sbpool(4, [128, 512], BF16, "h512")
        self.xpc = k.sbpool(3, [128, 512], F32, "xpc")
        self.xo = k.sbpool(3, [128, 512], F32, "xo")
        self.modT = k.sb([128, 96, 2], F32, "modT"); self.mod_b = Buf()
        self.gmix = k.sb([128, DEPTH, DC], F32, "gmix")
        self.gffn = k.sb([128, DEPTH, DC], F32, "gffn")
        self.gfin = k.sb([128, DC], F32, "gfin")
        self.AB = k.sb([128, 2, 2, 2, DC], F32, "AB"); self.AB_b = Buf()
        self.gate = k.sb([128, 2, 2, DC], F32, "gate")
        self.adabT = k.sb([128, DEPTH, 96], F32, "adabT")
        k.dma("sp", self.gmix[:, :, :], self.ins["norm_mix"].rearrange("l (c p) -> p l c", p=128), writes=[self.cb], nc_ok=True)
        k.dma("sp", self.gffn[:, :, :], self.ins["norm_ffn"].rearrange("l (c p) -> p l c", p=128), writes=[self.cb], nc_ok=True)
        k.dma("sp", self.gfin[:, :], self.ins["final_norm"].rearrange("(c p) -> p c", p=128), writes=[self.cb], nc_ok=True)
        k.dma("sp", self.adabT[:, :, :], self.ins["ada_b"].rearrange("l (j p) -> p l j", p=128), writes=[self.cb], nc_ok=True)
        self.condT = k.sb([128, DC, 2], F32, "condT")
        self.condh = k.sb([128, DC, 2], BF16, "condh")
        for r in range(2):
            k.dma("sp", self.condT[:, :, r], self.ins["cond"][r].rearrange("(c p) -> p c", p=128), writes=[self.cb], nc_ok=True)
        k.op("act", _act(self.condh[:, :, :], self.condT[:, :, :], AF.Silu), [self.cb], [self.cb])

    def alloc_gemm(self, nw=3, hT=True):
        k = self.k
        self.wpool = k.sbpool(nw, [128, 8192], BF16, "w")
        if hT:
            self.hTf = k.sb([128, 16384], BF16, "hT"); self.hT_b = Buf()
            self.hT = self.hTf[:, :].rearrange("p (k t) -> p k t", k=DC)
            self.hT32 = self.hTf[:, :].rearrange("p (k t) -> p k t", k=32)

    def xbufs(self, t0, n):
        b0 = t0 // 512
        return [self.xT_b[c][b0 + i] for c in range(DC) for i in range(n // 512)]

    def load_x_T(self):
        k = self.k
        with k.phase():
            stage = k.sbpool(2, [128, D], F32, "p0in")
            for tb in range(T // 128):
                st, sbuf = stage.next()
                k.dma("sp", st[:, :], self.ins["x_all"][tb * 128:(tb + 1) * 128, :], writes=[sbuf])
                for cg in range(4):
                    ps, pb = self.ps.next()
                    fns = [_tp(ps[:, j * 128:(j + 1) * 128], st[:, (cg * 4 + j) * 128:(cg * 4 + j + 1) * 128], self.ident[:, :]) for j in range(4)]
                    k.mm(fns, [sbuf, self.cb], [pb])
                    o, ob = self.xo.next()
                    if cg % 2 == 0:
                        k.op("act", _act(o[:, :], ps[:, :], AF.Copy), [pb], [ob])
                    else:
                        k.op("dve", _cp(o[:, :], ps[:, :]), [pb], [ob])
                    dst = self.xT[cg * 512:(cg + 1) * 512, tb * 128:(tb + 1) * 128].rearrange("(c p) t -> p c t", p=128)
                    wb = [self.xT_b[cg * 4 + j][tb // 4] for j in range(4)]
                    k.dma("sp", dst, o[:, :].rearrange("p (c t) -> p c t", c=4), reads=[ob], writes=wb)

    def rstd_from_sumsq(self, ps, pb, n, inv_n):
        k = self.k
        sd, sdb = self.t512.next()
        k.op("act", _act(sd[:, :n], ps[:, :n], AF.Sqrt, scale=inv_n, bias=self.epsb[:, 0:1]), [pb, self.cb], [sdb])
        rs, rsb = self.rsp.next()
        k.op("dve", lambda h: h.reciprocal(out=rs[:, :n], in_=sd[:, :n]), [sdb], [rsb])
        return rs, rsb

    def load_stats(self, tt0):
        k = self.k
        xi, xib = self.xin.next()
        src = self.xT[:, tt0:tt0 + 512].rearrange("(c p) t -> p c t", p=128)
        k.dma("sp", xi[:, :, :], src, reads=self.xbufs(tt0, 512), writes=[xib])
        ps, pb = self.ps.next()
        for c in range(DC):
            sq, sqb = self.h512.next()
            k.op("act", _act(sq[:, :], xi[:, c, :], AF.Square), [xib], [sqb])
            k.mm([_mmf(ps[:, :], self.ones_h[:, :], sq[:, :], start=(c == 0), stop=(c == DC - 1))], [sqb, self.cb], [pb])
        rs, rsb = self.rstd_from_sumsq(ps, pb, 512, 1.0 / D)
        return xi, xib, rs, rsb

    def norm_mod(self, t0, TT, which, r):
        k = self.k
        for blk in range(TT // 512):
            xi, xib, rs, rsb = self.load_stats(t0 + blk * 512)
            for c in range(DC):
                tm, tmb = self.t512.next()
                k.op("dve", _stt(tm[:, :], xi[:, c, :], self.AB[:, which, r, 0, c:c + 1], rs[:, :], ALU.mult, ALU.mult),
                     [xib, rsb, self.AB_b], [tmb])
                k.op("act", _act(self.hT[:, c, blk * 512:(blk + 1) * 512], tm[:, :], AF.Identity, bias=self.AB[:, which, r, 1, c:c + 1], scale=1.0),
                     [tmb, self.AB_b], [self.hT_b])

    def gemm_fm(self, act, act_b, KC, TT, W, groups, cb):
        k = self.k
        Wv = W.rearrange("(k p) n -> p k n", p=128)
        for gi, grp in enumerate(groups):
            G = len(grp)
            assert KC * G * 128 <= 8192
            wt, wb = self.wpool.next()
            wv = wt[:, 0:KC * G * 128].rearrange("p (k n) -> p k n", k=KC)
            s = 0
            while s < G:
                e = s
                while e + 1 < G and grp[e + 1] == grp[e] + 128:
                    e += 1
                n = (e - s + 1) * 128
                k.dma("pool", wv[:, :, s * 128:s * 128 + n], Wv[:, :, grp[s]:grp[s] + n], writes=[wb])
                s = e + 1
            for blk in range(TT // 512):
                for s, col in enumerate(grp):
                    ps, pb = self.ps.next()
                    fns = [_mmf(ps[:, :], wv[:, kk, s * 128:(s + 1) * 128], act[:, kk, blk * 512:(blk + 1) * 512],
                                start=(kk == 0), stop=(kk == KC - 1)) for kk in range(KC)]
                    k.mm(fns, [wb, act_b], [pb])
                    cb(gi, s, col, blk, ps, pb)

    def resid_cb(self, t0, gate_ap_fn):
        k = self.k

        def cb(gi, s, col, blk, ps, pb):
            c = col // 128
            tt0 = t0 + blk * 512
            xb = self.xT_b[c][tt0 // 512]
            xp, xpb = self.xpc.next()
            k.dma("sp", xp[:, :], self.xT[c * 128:(c + 1) * 128, tt0:tt0 + 512], reads=[xb], writes=[xpb])
            o, ob = self.xo.next()
            k.op("dve", _stt(o[:, :], ps[:, :], gate_ap_fn(c), xp[:, :], ALU.mult, ALU.add), [pb, xpb, self.AB_b], [ob])
            k.dma("sp", self.xT[c * 128:(c + 1) * 128, tt0:tt0 + 512], o[:, :], reads=[ob], writes=[xb])
        return cb

    def adaln(self, l):
        k = self.k
        Wv = self.ins["ada_w"][l].rearrange("(k p) n -> p k n", p=128)
        psT, psTb = self.psacc.next()
        ph = k.phase(); ph.__enter__()
        self.alloc_gemm(hT=False)
        for cb_ in range(24):
            wt, wb = self.wpool.next()
            wv = wt[:, :].rearrange("p (k n) -> p k n", k=DC)
            k.dma("pool", wv[:, :, :], Wv[:, :, cb_ * 512:(cb_ + 1) * 512], writes=[wb])
            ps, pb = self.ps.next()
            fns = [_mmf(ps[0:2, :], self.condh[:, kk, :], wv[:, kk, :], start=(kk == 0), stop=(kk == DC - 1)) for kk in range(DC)]
            k.mm(fns, [wb, self.cb], [pb])
            row, rowb = self.t512.next()
            k.op("dve", _cp(row[0:2, :], ps[0:2, :]), [pb], [rowb])
            fns = [_tp(psT[:, (cb_ * 4 + j) * 2:(cb_ * 4 + j + 1) * 2], row[0:2, j * 128:(j + 1) * 128], self.ident[0:2, 0:2]) for j in range(4)]
            k.mm(fns, [rowb, self.cb], [psTb])
        for r in range(2):
            k.op("dve", _tt(self.modT[:, :, r], psT[:, 0:192].rearrange("p (j r) -> p j r", r=2)[:, :, r], self.adabT[:, l, :], ALU.add),
                 [psTb, self.cb], [self.mod_b, self.AB_b])
        for which, (gt, o_sh, o_sc, o_g) in enumerate(((self.gmix, 0, 16, 32), (self.gffn, 48, 64, 80))):
            for r in range(2):
                k.op("dve", _stt(self.AB[:, which, r, 0, :], self.modT[:, o_sc:o_sc + 16, r], 1.0, gt[:, l, :], ALU.add, ALU.mult),
                     [self.mod_b, self.cb], [self.AB_b])
                k.op("dve", _cp(self.AB[:, which, r, 1, :], self.modT[:, o_sh:o_sh + 16, r]), [self.mod_b], [self.AB_b])
                k.op("dve", _cp(self.gate[:, which, r, :], self.modT[:, o_g:o_g + 16, r]), [self.mod_b], [self.AB_b])
        ph.__exit__(None, None, None)

    def ffn(self, l):
        k = self.k
        Win = self.ins["ffn_w_in"][l]
        Wout = self.ins["ffn_w_out"][l]
        with k.phase():
            self.alloc_gemm()
            self.xin = k.sbpool(1, [128, DC, 512], F32, "xin")
            big = k.sb([128, 22 * 1024], BF16, "big")
            big_b = Buf()
            for (t0, TT) in TILES:
                r = 1 if t0 >= S_LEN else 0
                self.norm_mod(t0, TT, 1, r)
                if self.cfg.get("dbg_hT"):
                    for c in range(DC):
                        tm, tmb = self.t512.next()
                        k.op("dve", _cp(tm[:, :], self.hT[:, c, 0:512]), [self.hT_b], [tmb])
                        k.dma("sp", self.outs["dbg_xT"][c * 128:(c + 1) * 128, t0:t0 + 512], tm[:, :], reads=[tmb])
                    continue
                hid = big[:, 0:22 * TT].rearrange("p (j t) -> p j t", j=22)
                for half in range(2):
                    state = {}

                    def cb_in(gi, s, col, blk, ps, pb, state=state, hid=hid):
                        if s == 0:
                            sg, sgb = self.sgp.next()
                            k.op("act", _act(sg[:, :], ps[:, :], AF.Silu), [pb], [sgb])
                            state["sg"] = (sg, sgb)
                        else:
                            sg, sgb = state["sg"]
                            k.op("dve", _tt(hid[:, gi, blk * 512:(blk + 1) * 512], ps[:, :], sg[:, :], ALU.mult), [pb, sgb], [big_b])
                    groups = [[(half * 22 + j) * 128, DFF + (half * 22 + j) * 128] for j in range(22)]
                    self.gemm_fm(self.hT, self.hT_b, DC, TT, Win, groups, cb_in)
                    Wo = Wout[half * 2816:(half + 1) * 2816, :]
                    groups = [[c * 128, (c + 1) * 128] for c in range(0, DC, 2)]
                    self.gemm_fm(hid, big_b, 22, TT, Wo, groups, self.resid_cb(t0, lambda c, r=r: self.gate[:, 1, r, c:c + 1]))

    def final(self):
        k = self.k
        with k.phase():
            self.xin = k.sbpool(1, [128, DC, 512], F32, "xin")
            p0out = k.sbpool(2, [128, D], F32, "p0out")
            for blk in range(NBLK):
                tt0 = blk * 512
                xi, xib, rs, rsb = self.load_stats(tt0)
                for c in range(DC):
                    k.op("dve", _stt(xi[:, c, :], xi[:, c, :], self.gfin[:, c:c + 1], rs[:, :], ALU.mult, ALU.mult), [xib, rsb, self.cb], [xib])
                for tb in range(4):
                    o, ob = p0out.next()
                    for cg in range(4):
                        ps, pb = self.ps.next()
                        fns = [_tp(ps[:, j * 128:(j + 1) * 128], xi[:, cg * 4 + j, tb * 128:(tb + 1) * 128], self.ident[:, :]) for j in range(4)]
                        k.mm(fns, [xib, self.cb], [pb])
                        if cg % 2 == 0:
                            k.op("act", _act(o[:, cg * 512:(cg + 1) * 512], ps[:, :], AF.Copy), [pb], [ob])
                        else:
                            k.op("dve", _cp(o[:, cg * 512:(cg + 1) * 512], ps[:, :]), [pb], [ob])
                    k.dma("sp", self.outs["y_all"][tt0 + tb * 128:tt0 + (tb + 1) * 128, :], o[:, :], reads=[ob])

    def dbg_dump(self):
        k = self.k
        with k.phase():
            self.xin = k.sbpool(1, [128, DC, 512], F32, "xin")
            for blk in range(NBLK):
                tt0 = blk * 512
                xi, xib = self.xin.next()
                k.dma("sp", xi[:, :, :], self.xT[:, tt0:tt0 + 512].rearrange("(c p) t -> p c t", p=128), reads=self.xbufs(tt0, 512), writes=[xib])
                k.dma("sp", self.outs["dbg_xT"][:, tt0:tt0 + 512].rearrange("(c p) t -> p c t", p=128), xi[:, :, :], reads=[xib])

    def build(self):
        cfg = self.cfg
        k = self.k
        self.load_x_T()
        for l in cfg.get("layers", range(DEPTH)):
            self.adaln(l)
            if cfg.get("mixers", True):
                kind = l % 3
                if kind == 0:
                    self.attention(l)
                elif kind == 1:
                    self.ssd(l)
                else:
                    self.lru(l)
            if cfg.get("ffn", True):
                self.ffn(l)
        if cfg.get("dbg") and not cfg.get("dbg_hT"):
            self.dbg_dump()
        self.final()
        return k.finish()

    def attention(self, l):
        k = self.k
        jl = l // 3
        Wqkv = self.ins["attn_w_qkv"][jl]
        Wo = self.ins["attn_w_o"][jl]
        nk_o, nv_o = self.outs["nk"], self.outs["nv"]
        with k.phase():
            self.alloc_gemm()
            self.xin = k.sbpool(1, [128, DC, 512], F32, "xin")
            cosT = k.sb([128, S_LEN], F32, "cosT")
            sinT = k.sb([128, S_LEN], F32, "sinT")
            qkg = k.sb([128, 2], F32, "qkg")
            tb_ = Buf()
            k.dma("sp", cosT[:, :], self.ins["c_cos"][:, :], writes=[tb_])
            k.dma("sp", sinT[:, :], self.ins["c_sin"][:, :], writes=[tb_])
            k.dma("sp", qkg[:, 0:1], self.ins["attn_q_norm"][jl].rearrange("(p o) -> p o", o=1), writes=[tb_], nc_ok=True)
            k.dma("sp", qkg[:, 1:2], self.ins["attn_k_norm"][jl].rearrange("(p o) -> p o", o=1), writes=[tb_], nc_ok=True)
            qn_p = k.sbpool(2, [128, 512], F32, "qn")
            for (t0, TT) in self.cfg.get("tiles", TILES):
                prompt = t0 >= S_LEN
                r = 1 if prompt else 0
                self.norm_mod(t0, TT, 0, r)

                def cb(gi, s, col, blk, ps, pb, t0=t0, prompt=prompt):
                    j = col // 128
                    tt0 = t0 + blk * 512
                    if j < 20:
                        sq, sqb = self.h512.next()
                        k.op("act", _act(sq[:, :], ps[:, :], AF.Square), [pb], [sqb])
                        ps2, pb2 = self.ps.next()
                        k.mm([_mmf(ps2[:, :], self.ones_h[:, :], sq[:, :])], [sqb, self.cb], [pb2])
                        rs, rsb = self.rstd_from_sumsq(ps2, pb2, 512, 1.0 / 128)
                        qn, qnb = qn_p.next()
                        gi_ = 0 if j < 16 else 1
                        k.op("dve", _stt(qn[:, :], ps[:, :], qkg[:, gi_:gi_ + 1], rs[:, :], ALU.mult, ALU.mult), [pb, rsb, tb_], [qnb])
                        o, ob = self.h512.next()
                        if not prompt:
                            qh, qhb = self.h512.next()
                            k.op("act", _act(qh[:, :], qn[:, :], AF.Copy), [qnb], [qhb])
                            ps3, pb3 = self.ps.next()
                            k.mm([_mmf(ps3[:, :], self.rot_h[:, :], qh[:, :])], [qhb, self.cb], [pb3])
                            t1, t1b = self.t512.next()
                            k.op("dve", _tt(t1[:, :], qn[:, :], cosT[:, tt0:tt0 + 512], ALU.mult), [qnb, tb_], [t1b])
                            t2, t2b = self.t512.next()
                            k.op("dve", _tt(t2[:, :], ps3[:, :], sinT[:, tt0:tt0 + 512], ALU.mult), [pb3, tb_], [t2b])
                            k.op("dve", _tt(o[:, :], t1[:, :], t2[:, :], ALU.add), [t1b, t2b], [ob])
                        else:
                            k.op("act", _act(o[:, :], qn[:, :], AF.Copy), [qnb], [ob])
                            if j >= 16:
                                hh = j - 16
                                ps4, pb4 = self.ps.next()
                                k.mm([_tp(ps4[:, i * 128:(i + 1) * 128], qn[:, i * 128:(i + 1) * 128], self.ident[:, :]) for i in range(4)], [qnb, self.cb], [pb4])
                                kf, kfb = self.t512.next()
                                k.op("act", _act(kf[:, :], ps4[:, :], AF.Copy), [pb4], [kfb])
                                for sq_ in range(2):
                                    k.dma("sp", nk_o[sq_, jl, :, hh * 128:(hh + 1) * 128].rearrange("(b p) f -> p b f", p=128),
                                          kf[:, sq_ * 256:(sq_ + 1) * 256].rearrange("p (b f) -> p b f", b=2), reads=[kfb])
                        dst = self.qT[j * 128:(j + 1) * 128, tt0:tt0 + 512] if j < 16 else self.kT[(j - 16) * 128:(j - 15) * 128, tt0:tt0 + 512]
                        k.dma("sp", dst, o[:, :], reads=[ob])
                    else:
                        hh = j - 20
                        vt, vtb = self.t512.next()
                        k.op("act", _act(vt[:, :], ps[:, :], AF.Copy), [pb], [vtb])
                        ps4, pb4 = self.ps.next()
                        k.mm([_tp(ps4[:, i * 128:(i + 1) * 128], vt[:, i * 128:(i + 1) * 128], self.ident[:, :]) for i in range(4)], [vtb, self.cb], [pb4])
                        vb, vbb = self.h512.next()
                        k.op("dve", _cp(vb[:, :], ps4[:, :]), [pb4], [vbb])
                        k.dma("sp", self.vtok[tt0:tt0 + 512, hh * 128:(hh + 1) * 128].rearrange("(b p) f -> p b f", p=128),
                              vb[:, :].rearrange("p (b f) -> p b f", b=4), reads=[vbb])
                        if prompt and self.cfg.get("vmode", 2) > 0:
                            vf, vfb = self.t512.next()
                            if self.cfg.get("vmode", 2) == 1:
                                k.op("act", _act(vf[:, :], ps4[:, :], AF.Copy), [pb4], [vfb])
                            else:
                                k.op("dve", _cp(vf[:, :], ps4[:, :]), [pb4], [vfb])
                            for sq_ in range(2):
                                k.dma("sp", nv_o[sq_, jl, :, hh * 128:(hh + 1) * 128].rearrange("(b p) f -> p b f", p=128),
                                      vf[:, sq_ * 256:(sq_ + 1) * 256].rearrange("p (b f) -> p b f", b=2), reads=[vfb])
                groups = [[(g4 * 4 + i) * 128 for i in range(4)] for g4 in self.cfg.get("qkv_groups", range(6))]
                self.gemm_fm(self.hT, self.hT_b, DC, TT, Wqkv, groups, cb)
        if self.cfg.get("attn_stop") == 1:
            return
        with k.phase():
            KTp = k.sbpool(2, [128, S_LEN + PAST], BF16, "KT")
            Vp = k.sbpool(2, [128, 36, 128], BF16, "Vt")
            ck = k.sb([128, 4, 512], F32, "ck")
            ckb = Buf()
            k.dma("sp", ck[:, :, :], self.ins["cache_k"][jl].rearrange("(b p) f -> p b f", p=128), writes=[ckb])
            QTp = k.sbpool(3, [128, 512], BF16, "QT")
            PTp = k.sbpool(4, [128, 512], BF16, "PT")
            scale = 1.0 / np.sqrt(128.0)
            jobs = [(0, 32, 0, S_LEN, True), (S_LEN, 2, S_LEN, P_LEN, False), (S_LEN + P_LEN, 2, S_LEN + P_LEN, P_LEN, False)]
            for (k0, nkb0, q0, qlen, ctx) in jobs:
                nkb = nkb0 + (4 if ctx else 0)
                QB = min(512, qlen)
                for g in range(4):
                    KT, KTb = KTp.next()
                    V, Vb = Vp.next()
                    k.dma("sp", KT[:, 0:nkb0 * 128], self.kT[g * 128:(g + 1) * 128, k0:k0 + nkb0 * 128], writes=[KTb])
                    k.dma("sp", V[:, 0:nkb0, :], self.vtok[k0:k0 + nkb0 * 128, g * 128:(g + 1) * 128].rearrange("(b p) f -> p b f", p=128), writes=[Vb])
                    if ctx:
                        ps4, pb4 = self.ps.next()
                        k.mm([_tp(ps4[:, i * 128:(i + 1) * 128], ck[:, i, g * 128:(g + 1) * 128], self.ident[:, :]) for i in range(4)], [ckb, self.cb], [pb4])
                        k.op("act", _act(KT[:, nkb0 * 128:nkb * 128], ps4[:, :], AF.Copy), [pb4], [KTb])
                        k.dma("pool", V[:, nkb0:nkb, :], self.ins["cache_v"][jl][:, g * 128:(g + 1) * 128].rearrange("(b p) f -> p b f", p=128), writes=[Vb])
                    for h in range(4):
                        hd = g * 4 + h
                        for qb in range(qlen // QB):
                            qq0 = q0 + qb * QB
                            QT, QTb = QTp.next()
                            k.dma("sp", QT[:, 0:QB], self.qT[hd * 128:(hd + 1) * 128, qq0:qq0 + QB], writes=[QTb])
                            psO, pbO = self.psacc.next()
                            psD, pbD = self.psacc.next()
                            prev = None
                            for kb in range(nkb + 1):
                                cur = None
                                if kb < nkb:
                                    psS, pbS = self.ps.next()
                                    k.mm([_mmf(psS[:, 0:QB], KT[:, kb * 128:(kb + 1) * 128], QT[:, 0:QB])], [KTb, QTb], [pbS])
                                    PT, PTb = PTp.next()
                                    k.op("act", _act(PT[:, 0:QB], psS[:, 0:QB], AF.Exp, scale=float(scale)), [pbS], [PTb])
                                    cur = (PT, PTb, kb)
                                if prev is not None:
                                    PTq, PTqb, kq = prev
                                    k.mm([_mmf(psO[:, 0:QB], V[:, kq, :], PTq[:, 0:QB], start=(kq == 0), stop=(kq == nkb - 1)),
                                          _mmf(psD[:, 0:QB], self.ones_h[:, :], PTq[:, 0:QB], start=(kq == 0), stop=(kq == nkb - 1))],
                                         [Vb, PTqb, self.cb], [pbO, pbD])
                                prev = cur
                            rc, rcb = self.t512.next()
                            k.op("dve", lambda h, rc=rc, psD=psD, QB=QB: h.reciprocal(out=rc[:, 0:QB], in_=psD[:, 0:QB]), [pbD], [rcb])
                            o, ob = self.h512.next()
                            k.op("dve", _tt(o[:, 0:QB], psO[:, 0:QB], rc[:, 0:QB], ALU.mult), [pbO, rcb], [ob])
                            k.dma("sp", self.oT[hd * 128:(hd + 1) * 128, qq0:qq0 + QB], o[:, 0:QB], reads=[ob])
        if self.cfg.get("attn_stop") == 2:
            return
        self.out_proj(Wo, DC)

    def out_proj(self, Wo, KC):
        k = self.k
        with k.phase():
            self.alloc_gemm()
            TT = 16384 // KC
            G = 8192 // (KC * 128)
            act = self.hT if KC == DC else self.hT32
            for t0 in range(0, T, TT):
                r = 1 if t0 >= S_LEN else 0
                TTl = min(TT, T - t0)
                for h2 in range(KC // DC):
                    k.dma("sp", act[:, h2 * DC:(h2 + 1) * DC, 0:TTl],
                          self.oT[h2 * D:(h2 + 1) * D, t0:t0 + TTl].rearrange("(c p) t -> p c t", p=128), writes=[self.hT_b])
                groups = [[(g4 * G + i) * 128 for i in range(G)] for g4 in range(DC // G)]
                self.gemm_fm(act, self.hT_b, KC, TTl, Wo, groups, self.resid_cb(t0, lambda c, r=r: self.gate[:, 0, r, c:c + 1]))

    def conv_chunk(self, src, pin, pinb, acc, accb, cw, cbias, c, prm):
        k = self.k
        for (t0, L, q0) in SEGQ:
            k.dma("sp", pin[:, q0:q0 + L], src[:, t0:t0 + L], writes=[pinb])
        N = PADT - 3
        k.op("dve", _ts(acc[:, 1:1 + N], pin[:, 0:N], cw[:, 0, c:c + 1], cbias[:, c:c + 1], ALU.mult, ALU.add), [pinb, prm], [accb])
        for kk in range(1, 4):
            k.op("dve", _stt(acc[:, 1:1 + N], pin[:, kk:kk + N], cw[:, kk, c:c + 1], acc[:, 1:1 + N], ALU.mult, ALU.add), [pinb, prm, accb], [accb])

    def conv_params(self, wname, bname, jl, nch, prm):
        k = self.k
        cw = k.sb([128, 4, nch], F32, "cw")
        cbias = k.sb([128, nch], F32, "cbias")
        for kk in range(4):
            k.dma("sp", cw[:, kk, :], self.ins[wname][jl, kk].rearrange("(c p) -> p c", p=128), writes=[prm], nc_ok=True)
        k.dma("sp", cbias[:, :], self.ins[bname][jl].rearrange("(c p) -> p c", p=128), writes=[prm], nc_ok=True)
        return cw, cbias

    def lru(self, l):
        k = self.k
        jl = l // 3
        Win = self.ins["lru_w_in"][jl]
        Wout = self.ins["lru_w_out"][jl]
        recT, gT = self.xbcT, self.zsT
        with k.phase():
            self.alloc_gemm()
            self.xin = k.sbpool(1, [128, DC, 512], F32, "xin")
            for (t0, TT) in TILES:
                r = 1 if t0 >= S_LEN else 0
                self.norm_mod(t0, TT, 0, r)

                def cb(gi, s, col, blk, ps, pb, t0=t0):
                    j = col // 128
                    tt0 = t0 + blk * 512
                    if j < 16:
                        o, ob = self.h512.next()
                        k.op("act", _act(o[:, :], ps[:, :], AF.Gelu_apprx_tanh), [pb], [ob])
                        k.dma("sp", gT[j * 128:(j + 1) * 128, tt0:tt0 + 512], o[:, :], reads=[ob])
                    else:
                        o, ob = self.xo.next()
                        k.op("dve", _cp(o[:, :], ps[:, :]), [pb], [ob])
                        k.dma("sp", recT[(j - 16) * 128:(j - 15) * 128, tt0:tt0 + 512], o[:, :], reads=[ob])
                groups = [[(g4 * 4 + i) * 128 for i in range(4)] for g4 in range(8)]
                self.gemm_fm(self.hT, self.hT_b, DC, TT, Win, groups, cb)
        with k.phase():
            N = PADT - 3
            prm = Buf()
            pin = k.sb([128, PADT], F32, "pin"); pinb = Buf()
            k.op("dve", lambda h: h.memset(pin[:, :], 0.0), writes=[pinb])
            acc = k.sb([128, PADT], F32, "acc"); accb = Buf()
            recb = k.sb([128, PADT], BF16, "recb"); recbb = Buf()
            R = k.sb([128, PADT], F32, "R"); Rb = Buf()
            I = k.sb([128, PADT], F32, "I"); Ib = Buf()
            Hf = k.sb([128, PADT], F32, "Hf"); Hfb = Buf()
            Hb = k.sb([128, PADT], F32, "Hb"); Hbb = Buf()
            gpad = k.sb([128, PADT], BF16, "gpad"); gpb = Buf()
            yb = k.sb([128, PADT], BF16, "yb"); ybb = Buf()
            cw, cbias = self.conv_params("lru_conv_w", "lru_conv_b", jl, 16, prm)
            ba = k.sb([128, 2, 16], F32, "ba"); bi = k.sb([128, 2, 16], F32, "bi")
            apm = k.sb([128, 32], F32, "apm"); h0 = k.sb([128, 2, 16], F32, "h0")
            wa = k.sb([128, 32, 128], BF16, "wa"); wi = k.sb([128, 32, 128], BF16, "wi")
            for d in range(2):
                k.dma("sp", ba[:, d, :], self.ins["lru_b_a"][jl, d].rearrange("(c p) -> p c", p=128), writes=[prm], nc_ok=True)
                k.dma("sp", bi[:, d, :], self.ins["lru_b_i"][jl, d].rearrange("(c p) -> p c", p=128), writes=[prm], nc_ok=True)
                k.dma("sp", apm[:, d * 16:(d + 1) * 16], self.ins["lru_a_param"][jl, d].rearrange("(c p) -> p c", p=128), writes=[prm], nc_ok=True)
                k.dma("sp", h0[:, d, :], self.ins["state_lru"][d].rearrange("(c p) -> p c", p=128), writes=[prm], nc_ok=True)
                k.dma("pool", wa[:, d * 16:(d + 1) * 16, :], self.ins["lru_w_a"][jl, d].rearrange("c w v -> w c v"), writes=[prm])
                k.dma("pool", wi[:, d * 16:(d + 1) * 16, :], self.ins["lru_w_i"][jl, d].rearrange("c w v -> w c v"), writes=[prm])
            t1 = k.sb([128, 32], F32, "lt1"); t2 = k.sb([128, 32], F32, "lt2")
            nsp8 = k.sb([128, 32], F32, "nsp8"); nsp16 = k.sb([128, 32], F32, "nsp16")
            k.op("act", _act(t1[:, :], apm[:, :], AF.Abs), [prm], [prm])
            k.op("act", _act(t1[:, :], t1[:, :], AF.Exp, scale=-1.0), [prm], [prm])
            k.op("act", _act(t1[:, :], t1[:, :], AF.Ln, bias=self.ones_f[:, 0:1], scale=1.0), [prm, self.cb], [prm])
            k.op("dve", _ts(t2[:, :], apm[:, :], -1.0, 0.0, ALU.mult, ALU.max), [prm], [prm])
            k.op("dve", _tt(t2[:, :], t2[:, :], t1[:, :], ALU.add), [prm], [prm])
            k.op("dve", _ts(nsp8[:, :], t2[:, :], -8.0, None, ALU.mult), [prm], [prm])
            k.op("dve", _ts(nsp16[:, :], t2[:, :], -16.0, None, ALU.mult), [prm], [prm])
            fin = k.sb([128, 2, 2, 16], F32, "fin"); finb = Buf()
            for c in range(16):
                self.conv_chunk(recT[c * 128:(c + 1) * 128, :], pin, pinb, acc, accb, cw, cbias, c, prm)
                k.op("act", _act(recb[:, 1:1 + N], acc[:, 1:1 + N], AF.Copy), [accb], [recbb])
                for (t0, L, q0) in SEGQ:
                    k.dma("sp", gpad[:, q0:q0 + L], gT[c * 128:(c + 1) * 128, t0:t0 + L], writes=[gpb])
                for d in range(2):
                    dc_ = d * 16 + c
                    for q in range(1, 1 + N, 512):
                        w = min(512, 1 + N - q)
                        ps, pb = self.ps.next()
                        k.mm([_mmf(ps[:, 0:w], wa[:, dc_, :], recb[:, q:q + w])], [prm, recbb], [pb])
                        k.op("act", _act(R[:, q:q + w], ps[:, 0:w], AF.Sigmoid, bias=ba[:, d, c:c + 1], scale=1.0), [pb, prm], [Rb])
                        ps2, pb2 = self.ps.next()
                        k.mm([_mmf(ps2[:, 0:w], wi[:, dc_, :], recb[:, q:q + w])], [prm, recbb], [pb2])
                        k.op("act", _act(I[:, q:q + w], ps2[:, 0:w], AF.Sigmoid, bias=bi[:, d, c:c + 1], scale=1.0), [pb2, prm], [Ib])
                    S, Sb = (Hf, Hfb) if d == 0 else (Hb, Hbb)
                    k.op("act", _act(S[:, 1:1 + N], R[:, 1:1 + N], AF.Exp, scale=nsp16[:, dc_:dc_ + 1]), [Rb, prm], [Sb])
                    k.op("act", _act(S[:, 1:1 + N], S[:, 1:1 + N], AF.Sqrt, bias=self.ones_f[:, 0:1], scale=-1.0), [Sb, self.cb], [Sb])
                    k.op("act", _act(R[:, 1:1 + N], R[:, 1:1 + N], AF.Exp, scale=nsp8[:, dc_:dc_ + 1]), [Rb, prm], [Rb])
                    k.op("dve", _tt(I[:, 1:1 + N], I[:, 1:1 + N], S[:, 1:1 + N], ALU.mult), [Ib, Sb], [Ib])
                    k.op("dve", _tt(I[:, 1:1 + N], I[:, 1:1 + N], acc[:, 1:1 + N], ALU.mult), [Ib, accb], [Ib])
                    H, Hbuf = (Hf, Hfb) if d == 0 else (Hb, Hbb)
                    for si, (t0, L, q0) in enumerate(SEGQ):
                        init = h0[:, d, c:c + 1] if si == 0 else self.zcol[:, 0:1]
                        if d == 0:
                            o_, a_, u_ = H[:, q0:q0 + L], R[:, q0:q0 + L], I[:, q0:q0 + L]
                        else:
                            o_, a_, u_ = H[:, q0 + L - 1:q0 - 1:-1], R[:, q0 + L - 1:q0 - 1:-1], I[:, q0 + L - 1:q0 - 1:-1]
                        k.op("dve", lambda h, o_=o_, a_=a_, u_=u_, init=init: h.tensor_tensor_scan(
                            out=o_, data0=a_, data1=u_, initial=init, op0=ALU.mult, op1=ALU.add), [Rb, Ib, prm, self.cb], [Hbuf])
                        if si > 0:
                            src = H[:, q0 + L - 1:q0 + L] if d == 0 else H[:, q0:q0 + 1]
                            k.op("dve", _cp(fin[:, si - 1, d, c:c + 1], src), [Hbuf], [finb])
                k.op("dve", _tt(Hf[:, 1:1 + N], Hf[:, 1:1 + N], Hb[:, 1:1 + N], ALU.add), [Hfb, Hbb], [Hfb])
                k.op("dve", _tt(yb[:, 1:1 + N], Hf[:, 1:1 + N], gpad[:, 1:1 + N], ALU.mult), [Hfb, gpb], [ybb])
                for (t0, L, q0) in SEGQ:
                    k.dma("sp", self.oT[c * 128:(c + 1) * 128, t0:t0 + L], yb[:, q0:q0 + L], reads=[ybb])
            for pr in range(2):
                for d in range(2):
                    k.dma("sp", self.outs["nlru"][pr, d].rearrange("(c p) -> p c", p=128), fin[:, pr, d, :], reads=[finb], nc_ok=True)
        self.out_proj(Wout, DC)

    def ssd(self, l):
        k = self.k
        jl = l // 3
        Win = self.ins["ssd_w_in"][jl]
        Wout = self.ins["ssd_w_out"][jl]
        N = PADT - 3

        def v3(ap, a):
            return ap.rearrange("p (a b) -> p a b", a=a)

        with k.phase():
            self.alloc_gemm()
            self.xin = k.sbpool(1, [128, DC, 512], F32, "xin")
            prm = Buf()
            dtb = k.sb([128, 1], F32, "dtb"); nea = k.sb([128, 1], F32, "nea")
            k.dma("sp", dtb[:, :], self.ins["ssd_dt_bias"][jl].rearrange("(p o) -> p o", o=1), writes=[prm], nc_ok=True)
            k.dma("sp", nea[:, :], self.ins["ssd_a_log"][jl].rearrange("(p o) -> p o", o=1), writes=[prm], nc_ok=True)
            k.op("act", _act(nea[:, :], nea[:, :], AF.Exp), [prm], [prm])
            for (t0, TT) in TILES:
                r = 1 if t0 >= S_LEN else 0
                self.norm_mod(t0, TT, 0, r)

                def cb(gi, s, col, blk, ps, pb, t0=t0):
                    j = col // 128
                    tt0 = t0 + blk * 512
                    if j < 32:
                        o, ob = self.h512.next()
                        k.op("act", _act(o[:, :], ps[:, :], AF.Silu), [pb], [ob])
                        k.dma("sp", self.zsT[j * 128:(j + 1) * 128, tt0:tt0 + 512], o[:, :], reads=[ob])
                    elif j < 80:
                        o, ob = self.xo.next()
                        k.op("dve", _cp(o[:, :], ps[:, :]), [pb], [ob])
                        k.dma("sp", self.xbcT[(j - 32) * 128:(j - 31) * 128, tt0:tt0 + 512], o[:, :], reads=[ob])
                    else:
                        xv, xvb = self.t512.next()
                        k.op("act", _act(xv[:, :], ps[:, :], AF.Identity, bias=dtb[:, 0:1], scale=1.0), [pb, prm], [xvb])
                        ax, axb = self.t512.next()
                        k.op("act", _act(ax[:, :], xv[:, :], AF.Abs), [xvb], [axb])
                        k.op("act", _act(ax[:, :], ax[:, :], AF.Exp, scale=-1.0), [axb], [axb])
                        k.op("act", _act(ax[:, :], ax[:, :], AF.Ln, bias=self.ones_f[:, 0:1], scale=1.0), [axb, self.cb], [axb])
                        dtv, dtvb = self.t512.next()
                        k.op("dve", _stt(dtv[:, :], xv[:, :], 0.0, ax[:, :], ALU.max, ALU.add), [xvb, axb], [dtvb])
                        dav, davb = self.t512.next()
                        k.op("dve", _ts(dav[:, :], dtv[:, :], nea[:, 0:1], -1.0, ALU.mult, ALU.mult), [dtvb, prm], [davb])
                        for which, (src, srcb) in enumerate(((dtv, dtvb), (dav, davb))):
                            ps4, pb4 = self.ps.next()
                            k.mm([_tp(ps4[:, i * 128:(i + 1) * 128], src[:, i * 128:(i + 1) * 128], self.ident[:, :]) for i in range(4)], [srcb, self.cb], [pb4])
                            o, ob = self.xo.next()
                            k.op("dve", _cp(o[:, :], ps4[:, :]), [pb4], [ob])
                            k.dma("sp", self.dtk[tt0:tt0 + 512, which, :].rearrange("(b p) f -> p b f", p=128),
                                  o[:, :].rearrange("p (b f) -> p b f", b=4), reads=[ob])
                groups = [[(g4 * 4 + i) * 128 for i in range(4)] for g4 in range(20)] + [[80 * 128]]
                self.gemm_fm(self.hT, self.hT_b, DC, TT, Win, groups, cb)
        if self.cfg.get("ssd_stop") == 1:
            return
        with k.phase():
            prm = Buf()
            pin = k.sb([128, PADT], F32, "pin"); pinb = Buf()
            k.op("dve", lambda h: h.memset(pin[:, :], 0.0), writes=[pinb])
            acc = k.sb([128, PADT], F32, "acc"); accb = Buf()
            sil = k.sb([128, PADT], F32, "sil"); silb_ = Buf()
            silh = k.sb([128, PADT], BF16, "silh"); silhb = Buf()
            cw, cbias = self.conv_params("ssd_conv_w", "ssd_conv_b", jl, 48, prm)
            for cc in range(48):
                self.conv_chunk(self.xbcT[cc * 128:(cc + 1) * 128, :], pin, pinb, acc, accb, cw, cbias, cc, prm)
                if cc < 40:
                    k.op("act", _act(sil[:, 1:1 + N], acc[:, 1:1 + N], AF.Silu), [accb], [silb_])
                    for grp in range(T // 512):
                        ps4, pb4 = self.ps.next()
                        fns = []
                        for i in range(4):
                            t = (grp * 4 + i) * 128
                            q = t + 1 + 2 * (0 if t < S_LEN else (1 if t < S_LEN + P_LEN else 2))
                            fns.append(_tp(ps4[:, i * 128:(i + 1) * 128], sil[:, q:q + 128], self.ident[:, :]))
                        k.mm(fns, [silb_, self.cb], [pb4])
                        o, ob = self.h512.next()
                        if grp % 2 == 0:
                            k.op("dve", _cp(o[:, :], ps4[:, :]), [pb4], [ob])
                        else:
                            k.op("act", _act(o[:, :], ps4[:, :], AF.Copy), [pb4], [ob])
                        dst = self.xtok[grp * 512:(grp + 1) * 512, cc * 128:(cc + 1) * 128] if cc < 32 else \
                            self.btok[grp * 512:(grp + 1) * 512, (cc - 32) * 128:(cc - 31) * 128]
                        k.dma("sp", dst.rearrange("(b p) f -> p b f", p=128), o[:, :].rearrange("p (b f) -> p b f", b=4), reads=[ob])
                if cc >= 32:
                    k.op("act", _act(silh[:, 1:1 + N], acc[:, 1:1 + N], AF.Silu), [accb], [silhb])
                    for (t0, L, q0) in SEGQ:
                        k.dma("sp", self.bcT[(cc - 32) * 128:(cc - 31) * 128, t0:t0 + L], silh[:, q0:q0 + L], reads=[silhb])
        if self.cfg.get("ssd_stop") == 2:
            return
        with k.phase():
            prm = Buf()
            tri = k.sb([128, 4, 128], F32, "tri")
            for i in range(4):
                k.dma("sp", tri[:, i, :], self.ins["c_tri"][i], writes=[prm])
            stT = k.sb([128, 4096], F32, "stT"); stTb = [Buf() for _ in range(8)]
            stH = k.sb([128, 4096], BF16, "stH"); stHb = [Buf() for _ in range(8)]
            hst = k.sb([128, 32, 128], F32, "hst"); hstb = Buf()
            xt_p = k.sbpool(2, [128, 4096], BF16, "xt")
            xdt_p = k.sbpool(2, [128, 4096], BF16, "xdt")
            xw_p = k.sbpool(2, [128, 4096], BF16, "xw")
            bt_p = k.sbpool(2, [128, 1024], BF16, "bt")
            bT_p = k.sbpool(2, [128, 8, 128], BF16, "bT")
            cT_p = k.sbpool(2, [128, 8, 128], BF16, "cT")
            dd_p = k.sbpool(2, [128, 2, 128], F32, "dd")
            sc_p = k.sbpool(2, [128, 192], F32, "sc")
            dU_p = k.sbpool(3, [128, 512], F32, "dU")
            E_p = k.sbpool(3, [128, 512], F32, "E")
            MT_p = k.sbpool(4, [128, 512], BF16, "MT")
            SM_p = k.sbpool(2, [128, 128], F32, "SM")
            y_p = k.sbpool(3, [128, 512], F32, "yo")
            for d in range(2):
                Lm = tri[:, 0 if d == 0 else 2, :]
                Um = tri[:, 1 if d == 0 else 3, :]
                for si, (s0, SL, _) in enumerate(SEGS):
                    nch = SL // 128
                    if si == 0:
                        for half in range(2):
                            k.dma("sp", hst[:, half * 16:(half + 1) * 16, :],
                                  self.ins["state_ssm"][d, half * 2048:(half + 1) * 2048, :].rearrange("(b p) n -> p b n", p=128), writes=[hstb])
                        for r4 in range(8):
                            ps, pb = self.ps.next()
                            k.mm([_tp(ps[:, i * 128:(i + 1) * 128], hst[:, r4 * 4 + i, :], self.ident[:, :]) for i in range(4)], [hstb, self.cb], [pb])
                            k.op("dve", _cp(stT[:, r4 * 512:(r4 + 1) * 512], ps[:, :]), [pb], [stTb[r4]])
                            k.op("act", _act(stH[:, r4 * 512:(r4 + 1) * 512], stT[:, r4 * 512:(r4 + 1) * 512], AF.Copy), [stTb[r4]], [stHb[r4]])
                    else:
                        for g in range(8):
                            k.op("dve", lambda h, g=g: h.memset(stT[:, g * 512:(g + 1) * 512], 0.0), writes=[stTb[g]])
                            k.op("dve", lambda h, g=g: h.memset(stH[:, g * 512:(g + 1) * 512], 0.0), writes=[stHb[g]])
                    order = range(nch) if d == 0 else range(nch - 1, -1, -1)
                    if self.cfg.get("ssd_nch"):
                        order = list(order)[:self.cfg["ssd_nch"]]
                    for ci in order:
                        t0 = s0 + ci * 128
                        xt, xtb = xt_p.next()
                        k.dma("sp", xt[:, :], self.xtok[t0:t0 + 128, :], writes=[xtb])
                        bt, btb = bt_p.next()
                        k.dma("sp", bt[:, :], self.btok[t0:t0 + 128, :], writes=[btb])
                        bT, bTb = bT_p.next()
                        k.dma("sp", bT[:, :, :], self.bcT[0:1024, t0:t0 + 128].rearrange("(g p) t -> p g t", p=128), writes=[bTb])
                        cT, cTb = cT_p.next()
                        k.dma("sp", cT[:, :, :], self.bcT[1024:2048, t0:t0 + 128].rearrange("(g p) t -> p g t", p=128), writes=[cTb])
                        dd, ddb = dd_p.next()
                        k.dma("sp", dd[:, :, :], self.dtk[t0:t0 + 128, :, :], writes=[ddb])
                        dt = dd[:, 0, d * 64:(d + 1) * 64]
                        dta = dd[:, 1, d * 64:(d + 1) * 64]
                        psc, pscb = self.ps.next()
                        k.mm([_mmf(psc[:, 0:64], Um, dta), _mmf(psc[:, 64:128], Lm, dta), _mmf(psc[:, 128:192], self.ones_f[:, :], dta)],
                             [prm, ddb, self.cb], [pscb])
                        sc, scb = sc_p.next()
                        k.op("act", _act(sc[:, :], psc[:, 0:192], AF.Exp), [pscb], [scb])
                        xdt, xdtb = xdt_p.next()
                        k.op("dve", _tt(v3(xdt[:, :], 64), v3(xt[:, :], 64), dt.unsqueeze(2).broadcast_to([128, 64, 64]), ALU.mult), [xtb, ddb], [xdtb])
                        xw, xwb = xw_p.next()
                        k.op("dve", _tt(v3(xw[:, :], 64), v3(xdt[:, :], 64), sc[:, 64:128].unsqueeze(2).broadcast_to([128, 64, 64]), ALU.mult), [xdtb, scb], [xwb])
                        for g in range(8):
                            gs = slice(g * 512, (g + 1) * 512)
                            psS, psSb = self.ps.next()
                            k.mm([_mmf(psS[:, 0:128], bT[:, g, :], cT[:, g, :])], [bTb, cTb], [psSb])
                            SM, SMb = SM_p.next()
                            k.op("dve", _tt(SM[:, :], psS[:, 0:128], Um, ALU.mult), [psSb, prm], [SMb])
                            MTs = []
                            for hf in range(2):
                                h0_ = g * 8 + hf * 4
                                dU, dUb = dU_p.next()
                                k.op("dve", _tt(v3(dU[:, :], 4), Um.unsqueeze(1).broadcast_to([128, 4, 128]),
                                                dta[:, h0_:h0_ + 4].unsqueeze(2).broadcast_to([128, 4, 128]), ALU.mult), [prm, ddb], [dUb])
                                psE, psEb = self.ps.next()
                                k.mm([_mmf(psE[:, :], Lm, dU[:, :])], [prm, dUb], [psEb])
                                E, Eb = E_p.next()
                                k.op("act", _act(E[:, :], psE[:, :], AF.Exp), [psEb], [Eb])
                                MT, MTb = MT_p.next()
                                k.op("dve", _tt(v3(MT[:, :], 4), v3(E[:, :], 4), SM[:, :].unsqueeze(1).broadcast_to([128, 4, 128]), ALU.mult), [Eb, SMb], [MTb])
                                MTs.append((MT, MTb))
                            psY, psYb = self.psacc.next()
                            fns = [_mmf(psY[:, h * 64:(h + 1) * 64], MTs[h // 4][0][:, (h % 4) * 128:(h % 4 + 1) * 128],
                                        xdt[:, (g * 8 + h) * 64:(g * 8 + h + 1) * 64]) for h in range(8)]
                            k.mm(fns, [MTs[0][1], MTs[1][1], xdtb], [psYb])
                            psI, psIb = self.psacc.next()
                            k.mm([_mmf(psI[:, :], cT[:, g, :], stH[:, gs])], [cTb, stHb[g]], [psIb])
                            yo, yob = y_p.next()
                            k.op("dve", _tt(v3(yo[:, :], 8), v3(psI[:, :], 8), sc[:, g * 8:(g + 1) * 8].unsqueeze(2).broadcast_to([128, 8, 64]), ALU.mult),
                                 [psIb, scb], [yob])
                            k.op("dve", _tt(yo[:, :], yo[:, :], psY[:, :], ALU.add), [yob, psYb], [yob])
                            k.dma("sp", self.ytok[d, t0:t0 + 128, gs], yo[:, :], reads=[yob])
                            psD, psDb = self.ps.next()
                            k.mm([_mmf(psD[:, :], bt[:, g * 128:(g + 1) * 128], xw[:, gs])], [btb, xwb], [psDb])
                            k.op("dve", _tt(v3(stT[:, gs], 8), v3(stT[:, gs], 8), sc[:, 128 + g * 8:128 + (g + 1) * 8].unsqueeze(2).broadcast_to([128, 8, 64]), ALU.mult),
                                 [scb, stTb[g]], [stTb[g]])
                            k.op("dve", _tt(stT[:, gs], stT[:, gs], psD[:, :], ALU.add), [stTb[g], psDb], [stTb[g]])
                            k.op("act", _act(stH[:, gs], stT[:, gs], AF.Copy), [stTb[g]], [stHb[g]])
                    if si > 0:
                        for r4 in range(8):
                            ps, pb = self.ps.next()
                            k.mm([_tp(ps[:, i * 128:(i + 1) * 128], stT[:, (r4 * 4 + i) * 128:(r4 * 4 + i + 1) * 128], self.ident[:, :]) for i in range(4)],
                                 [stTb[r4], self.cb], [pb])
                            o, ob = y_p.next()
                            k.op("dve", _cp(o[:, :], ps[:, :]), [pb], [ob])
                            k.dma("sp", self.outs["nssm"][si - 1, d, r4 * 512:(r4 + 1) * 512, :].rearrange("(b p) n -> p b n", p=128),
                                  o[:, :].rearrange("p (b n) -> p b n", b=4), reads=[ob])
        if self.cfg.get("ssd_stop") == 3:
            return
        with k.phase():
            prm = Buf()
            dsk = k.sb([128, 64], F32, "dsk")
            k.dma("sp", dsk[:, :], self.ins["ssd_d"][jl].partition_broadcast(128), writes=[prm], nc_ok=True)
            nw = k.sb([128, 32], F32, "nw")
            k.dma("sp", nw[:, :], self.ins["ssd_norm"][jl].rearrange("(c p) -> p c", p=128), writes=[prm], nc_ok=True)
            yf_p = k.sbpool(2, [128, 4096], F32, "yf")
            yb_p = k.sbpool(2, [128, 4096], F32, "yb2")
            xt_p = k.sbpool(2, [128, 4096], BF16, "xt2")
            zs_p = k.sbpool(2, [128, 32, 128], BF16, "zs")
            gt_p = k.sbpool(1, [128, 32, 128], F32, "gt")
            yo_p = k.sbpool(2, [128, 32, 128], BF16, "yo2")
            for b_ in range(T // 128):
                t0 = b_ * 128
                yf, yfb = yf_p.next()
                k.dma("sp", yf[:, :], self.ytok[0, t0:t0 + 128, :], writes=[yfb])
                y2, y2b = yb_p.next()
                k.dma("sp", y2[:, :], self.ytok[1, t0:t0 + 128, :], writes=[y2b])
                xt, xtb = xt_p.next()
                k.dma("sp", xt[:, :], self.xtok[t0:t0 + 128, :], writes=[xtb])
                zs, zsb = zs_p.next()
                for half in range(2):
                    k.dma("sp", zs[:, half * 16:(half + 1) * 16, :],
                          self.zsT[half * 2048:(half + 1) * 2048, t0:t0 + 128].rearrange("(c p) t -> p c t", p=128), writes=[zsb])
                k.op("dve", _tt(yf[:, :], yf[:, :], y2[:, :], ALU.add), [yfb, y2b], [yfb])
                k.op("dve", _tt(v3(y2[:, :], 64), v3(xt[:, :], 64), dsk[:, :].unsqueeze(2).broadcast_to([128, 64, 64]), ALU.mult), [xtb, prm], [y2b])
                k.op("dve", _tt(yf[:, :], yf[:, :], y2[:, :], ALU.add), [yfb, y2b], [yfb])
                gt, gtb = gt_p.next()
                psQ, psQb = self.psacc.next()
                for c4 in range(8):
                    ps, pb = self.ps.next()
                    k.mm([_tp(ps[:, i * 128:(i + 1) * 128], yf[:, (c4 * 4 + i) * 128:(c4 * 4 + i + 1) * 128], self.ident[:, :]) for i in range(4)],
                         [yfb, self.cb], [pb])
                    k.op("dve", _tt(gt[:, c4 * 4:(c4 + 1) * 4, :], v3(ps[:, :], 4), zs[:, c4 * 4:(c4 + 1) * 4, :], ALU.mult), [pb, zsb], [gtb])
                    sq, sqb = self.h512.next()
                    k.op("act", _act(v3(sq[:, :], 4), gt[:, c4 * 4:(c4 + 1) * 4, :], AF.Square), [gtb], [sqb])
                    k.mm([_mmf(psQ[:, 0:128], self.ones_h[:, :], sq[:, i * 128:(i + 1) * 128], start=(c4 == 0 and i == 0), stop=(c4 == 7 and i == 3))
                          for i in range(4)], [sqb, self.cb], [psQb])
                rs, rsb = self.rstd_from_sumsq(psQ, psQb, 128, 1.0 / 4096)
                k.op("dve", _tt(gt[:, :, :], gt[:, :, :], nw[:, :].unsqueeze(2).broadcast_to([128, 32, 128]), ALU.mult), [gtb, prm], [gtb])
                yo, yob = yo_p.next()
                k.op("dve", _tt(yo[:, :, :], gt[:, :, :], rs[:, 0:128].unsqueeze(1).broadcast_to([128, 32, 128]), ALU.mult), [gtb, rsb], [yob])
                for half in range(2):
                    k.dma("sp", self.oT[half * 2048:(half + 1) * 2048, t0:t0 + 128].rearrange("(c p) t -> p c t", p=128),
                          yo[:, half * 16:(half + 1) * 16, :], reads=[yob])
        self.out_proj(Wout, 32)


def host_consts():
    ident = np.eye(128, dtype=np.float32)
    rot = np.zeros((128, 128), np.float32)
    for m in range(64):
        rot[m + 64, m] = -1.0
        rot[m, m + 64] = 1.0
    rows = S_LEN // 64
    row = np.repeat(np.arange(rows, dtype=np.float32), 64)
    col = np.tile(np.arange(64, dtype=np.float32), rows)
    inv = (10000.0 ** (-np.arange(32, dtype=np.float32) / 32)).astype(np.float32)
    ang = np.concatenate([row[:, None] * inv, col[:, None] * inv], axis=-1).astype(np.float32)
    cos = np.cos(ang).T.astype(np.float32)
    sin = np.sin(ang).T.astype(np.float32)
    c_cos = np.ascontiguousarray(np.concatenate([cos, cos], 0))
    c_sin = np.ascontiguousarray(np.concatenate([sin, sin], 0))
    kk = np.arange(128)
    tri = np.stack([
        (kk[:, None] > kk[None, :]),
        (kk[:, None] <= kk[None, :]),
        (kk[:, None] < kk[None, :]),
        (kk[:, None] >= kk[None, :]),
    ]).astype(np.float32)
    return {"c_ident": ident, "c_rot": rot, "c_cos": c_cos, "c_sin": c_sin, "c_tri": tri}


def make_in_maps(inputs, cores):
    consts = host_consts()
    maps = []
    shared = {}
    for name in ("ada_w", "ada_b", "norm_mix", "norm_ffn", "attn_w_qkv", "attn_q_norm", "attn_k_norm", "attn_w_o",
                 "ssd_w_in", "ssd_conv_w", "ssd_conv_b", "ssd_d", "ssd_norm", "ssd_w_out",
                 "lru_w_in", "lru_conv_w", "lru_conv_b", "lru_w_a", "lru_b_a", "lru_w_i", "lru_b_i", "lru_a_param", "lru_w_out",
                 "ffn_w_in", "ffn_w_out", "final_norm"):
        shared[name] = np.ascontiguousarray(inputs[name], dtype=np.float32)
    shared["ssd_dt_bias"] = np.ascontiguousarray(inputs["ssd_dt_bias"], dtype=np.float32).reshape(1, 128)
    shared["ssd_a_log"] = np.ascontiguousarray(inputs["ssd_a_log"], dtype=np.float32).reshape(1, 128)
    shared.update(consts)
    for i in cores:
        m = dict(shared)
        m["x_all"] = np.ascontiguousarray(np.concatenate(
            [inputs["x_sample"][i], inputs["x_prompt"][2 * i], inputs["x_prompt"][2 * i + 1]], axis=0), dtype=np.float32)
        m["cache_k"] = np.ascontiguousarray(inputs["cache_k"][i].reshape(2, PAST, 512), dtype=np.float32)
        m["cache_v"] = np.ascontiguousarray(inputs["cache_v"][i].reshape(2, PAST, 512), dtype=np.float32)
        m["state_ssm"] = np.ascontiguousarray(inputs["state_ssm"][i, 0].reshape(2, 4096, 128), dtype=np.float32)
        m["state_lru"] = np.ascontiguousarray(inputs["state_lru"][i, 0], dtype=np.float32)
        m["cond"] = np.ascontiguousarray(np.stack([inputs["c"][i], inputs["c_ctx"]], 0), dtype=np.float32)
        maps.append(m)
    return maps


def run(inputs, cores, cfg):
    net = Net(cfg)
    nc = net.build()
    maps = make_in_maps(inputs, cores)
    res = run_bass_kernel_spmd(nc, maps, core_ids=list(range(len(cores))))
    return res


def kernel(**inputs):
    inputs = {k: np.asarray(v) for k, v in inputs.items()}
    cores = list(range(8))
    res = run(inputs, cores, {})
    B, BS = 16, 8
    y_prompt = np.zeros((B, P_LEN, D), np.float32)
    y_sample = np.zeros((BS, S_LEN, D), np.float32)
    nk = np.zeros((B, 2, P_LEN, 4, 128), np.float32)
    nv = np.zeros((B, 2, P_LEN, 4, 128), np.float32)
    nssm = np.zeros((B, 1, 2, 64, 64, 128), np.float32)
    nlru = np.zeros((B, 1, 2, D), np.float32)
    for i, r in enumerate(res.results):
        ya = r["y_all"]
        y_sample[i] = ya[:S_LEN]
        y_prompt[2 * i] = ya[S_LEN:S_LEN + P_LEN]
        y_prompt[2 * i + 1] = ya[S_LEN + P_LEN:]
        nk[2 * i:2 * i + 2] = r["nk"].reshape(2, 2, P_LEN, 4, 128)
        nv[2 * i:2 * i + 2] = r["nv"].reshape(2, 2, P_LEN, 4, 128)
        nssm[2 * i:2 * i + 2, 0] = r["nssm"].reshape(2, 2, 64, 64, 128)
        nlru[2 * i:2 * i + 2, 0] = r["nlru"].reshape(2, 2, D)
    return (y_prompt, y_sample, nk, nv, nssm, nlru)
```

```python
import numpy as np
import ml_dtypes
import concourse.bass as bass
import concourse.mybir as mybir
from concourse.bass_utils import run_bass_kernel_spmd

F32 = mybir.dt.float32
BF16 = mybir.dt.bfloat16
ALU = mybir.AluOpType
AF = mybir.ActivationFunctionType

D = 2048
DC = 16
S_LEN = 4096
P_LEN = 256
T = S_LEN + 2 * P_LEN
NBLK = T // 512
DEPTH = 4
DFF = 5632
EPS = 1e-6
PAST = 512
TILES = [(0, 1024), (1024, 1024), (2048, 1024), (3072, 1024), (4096, 512)]
SEGS = [(0, S_LEN, 0), (S_LEN, P_LEN, 1), (S_LEN + P_LEN, P_LEN, 1)]
PADT = T + 7
SEGQ = [(0, S_LEN, 1), (S_LEN, P_LEN, S_LEN + 3), (S_LEN + P_LEN, P_LEN, S_LEN + P_LEN + 5)]


class Buf:
    __slots__ = ("w", "r")

    def __init__(self):
        self.w = None
        self.r = {}


class Eng:
    def __init__(self, name):
        self.name = name
        self.sem = None
        self.count = 0
        self.waited = {}
        self.rec = []


class Pool:
    def __init__(self, items):
        self.items = items
        self.i = 0

    def next(self):
        it = self.items[self.i % len(self.items)]
        self.i += 1
        return it


class Phase:
    def __init__(self, k):
        self.k = k
        self.cms = []

    def __enter__(self):
        assert self.k.cur_phase is None
        self.k.cur_phase = self
        return self

    def __exit__(self, *a):
        self.k.barrier()
        for cm in reversed(self.cms):
            cm.__exit__(None, None, None)
        self.k.cur_phase = None
        return False


class K:
    def __init__(self, dbg=None):
        self.nc = bass.Bass("TRN2", target_bir_lowering=False)
        self.E = {n: Eng(n) for n in ("pe", "act", "dve", "pool", "sp")}
        self.semof = {}
        self.sem_ctx = []
        for n, e in self.E.items():
            e.sem = self._sem("e_" + n)
            self.semof[n] = e.sem
        self.dslots = {}
        for q, n in (("sp", 44), ("pool", 28)):
            sl = []
            for i in range(n):
                key = "d%s%d" % (q, i)
                s = self._sem(key)
                self.semof[key] = s
                sl.append([key, s, 0])
            self.dslots[q] = Pool(sl)
        self.nsb = 0
        self.bar_sem = self._sem("bar")
        self.bar_count = 0
        self.cur_phase = None

    def _sem(self, name):
        cm = self.nc.semaphore(name)
        s = cm.__enter__()
        self.sem_ctx.append(cm)
        return s

    def sb(self, shape, dt, name=None):
        self.nsb += 1
        name = (name or "sb") + "_%d" % self.nsb
        if self.cur_phase is not None:
            cm = self.nc.sbuf_tensor(name, list(shape), dt)
            t = cm.__enter__()
            self.cur_phase.cms.append(cm)
            return t
        return self.nc.alloc_sbuf_tensor(name, list(shape), dt)

    def sbpool(self, n, shape, dt, name):
        return Pool([(self.sb(shape, dt, "%s%d" % (name, i)), Buf()) for i in range(n)])

    def dram(self, name, shape, dt):
        return self.nc.dram_tensor(name, list(shape), dt).ap()

    def _waits(self, e, reads, writes):
        need = {}
        for b in reads:
            w = b.w
            if w is not None and need.get(w[0], 0) < w[1]:
                need[w[0]] = w[1]
        for b in writes:
            w = b.w
            if w is not None and need.get(w[0], 0) < w[1]:
                need[w[0]] = w[1]
            for k, v in b.r.items():
                if need.get(k, 0) < v:
                    need[k] = v
        for k, v in need.items():
            if k == "pe" and e.name == "pe":
                continue
            if e.waited.get(k, 0) >= v:
                continue
            e.waited[k] = v
            sem = self.semof[k]
            e.rec.append(lambda h, sem=sem, v=v: h.wait_ge(sem, v))

    def _mark(self, ev, reads, writes):
        k, v = ev
        for b in reads:
            if b.r.get(k, 0) < v:
                b.r[k] = v
        for b in writes:
            b.w = ev
            b.r = {}

    def op(self, en, fn, reads=(), writes=()):
        e = self.E[en]
        self._waits(e, reads, writes)
        e.count += 1
        sem = e.sem
        e.rec.append(lambda h, fn=fn, sem=sem: fn(h).then_inc(sem, 1))
        self._mark((en, e.count), reads, writes)

    def mm(self, fns, reads, writes):
        e = self.E["pe"]
        self._waits(e, reads, writes)
        for fn in fns[:-1]:
            e.rec.append(fn)
        e.count += 1
        sem = e.sem
        fn = fns[-1]
        e.rec.append(lambda h, fn=fn, sem=sem: fn(h).then_inc(sem, 1))
        self._mark(("pe", e.count), reads, writes)

    def dma(self, q, out, in_, reads=(), writes=(), nc_ok=False):
        e = self.E[q]
        self._waits(e, reads, writes)
        slot = self.dslots[q].next()
        key, sem, val = slot
        if val > 0 and e.waited.get(key, 0) < val:
            e.waited[key] = val
            e.rec.append(lambda h, sem=sem, v=val: h.wait_ge(sem, v))
        slot[2] = val + 16
        if nc_ok:
            e.rec.append(lambda h, out=out, in_=in_, sem=sem: h.dma_start(out=out, in_=in_, allow_slow_non_contiguous=True).then_inc(sem, 16))
        else:
            e.rec.append(lambda h, out=out, in_=in_, sem=sem: h.dma_start(out=out, in_=in_).then_inc(sem, 16))
        self._mark((key, val + 16), reads, writes)

    def barrier(self):
        sp = self.E["sp"]
        allk = {}
        for q in ("sp", "pool"):
            for key, sem, val in self.dslots[q].items:
                if val > 0:
                    allk[key] = val
        for n in ("pe", "act", "dve", "pool"):
            if self.E[n].count > 0:
                allk[n] = self.E[n].count
        for key, val in allk.items():
            if sp.waited.get(key, 0) < val:
                sp.waited[key] = val
                sp.rec.append(lambda h, sem=self.semof[key], v=val: h.wait_ge(sem, v))
        self.bar_count += 1
        bc = self.bar_count
        sp.rec.append(lambda h, sem=self.bar_sem: h.sem_inc(sem, 1))
        for n in ("pe", "act", "dve", "pool"):
            e = self.E[n]
            e.rec.append(lambda h, sem=self.bar_sem, v=bc: h.wait_ge(sem, v))
            for key, val in allk.items():
                if e.waited.get(key, 0) < val:
                    e.waited[key] = val

    def phase(self):
        return Phase(self)

    def finish(self):
        e = self.E["sp"]
        for q in ("sp", "pool"):
            for key, sem, val in self.dslots[q].items:
                if val > 0:
                    e.rec.append(lambda h, sem=sem, v=val: h.wait_ge(sem, v))
        for n in ("pe", "act", "dve"):
            en = self.E[n]
            if en.count > 0:
                e.rec.append(lambda h, sem=en.sem, v=en.count: h.wait_ge(sem, v))
        nc = self.nc
        E = self.E
        with nc.Block() as block:
            @block.tensor
            def _(h):
                for f in E["pe"].rec:
                    f(h)

            @block.scalar
            def _(h):
                for f in E["act"].rec:
                    f(h)

            @block.vector
            def _(h):
                for f in E["dve"].rec:
                    f(h)

            @block.gpsimd
            def _(h):
                for f in E["pool"].rec:
                    f(h)

            @block.sync
            def _(h):
                for f in E["sp"].rec:
                    f(h)
        return nc


def _act(out, in_, func, **kw):
    return lambda h: h.activation(out=out, in_=in_, func=func, **kw)


def _tt(out, a, b, op):
    return lambda h: h.tensor_tensor(out=out, in0=a, in1=b, op=op)


def _stt(out, in0, scalar, in1, op0, op1):
    return lambda h: h.scalar_tensor_tensor(out=out, in0=in0, scalar=scalar, in1=in1, op0=op0, op1=op1)


def _ts(out, in0, s1, s2, op0, op1=None):
    if op1 is None:
        return lambda h: h.tensor_scalar(out=out, in0=in0, scalar1=s1, scalar2=None, op0=op0)
    return lambda h: h.tensor_scalar(out=out, in0=in0, scalar1=s1, scalar2=s2, op0=op0, op1=op1)


def _cp(out, in_):
    return lambda h: h.tensor_copy(out=out, in_=in_)


def _mmf(out, lhsT, rhs, start=True, stop=True):
    return lambda h: h.matmul(out, lhsT=lhsT, rhs=rhs, start=start, stop=stop)


def _tp(out, in_, ident):
    return lambda h: h.transpose(out, in_, ident)


class Net:
    def __init__(self, cfg):
        self.cfg = cfg
        k = self.k = K()
        nc = k.nc
        self.ins = {}
        self.outs = {}

        def inp(name, shape, dt=F32):
            self.ins[name] = nc.dram_tensor(name, list(shape), dt, kind="ExternalInput").ap()
            return self.ins[name]

        def outp(name, shape):
            self.outs[name] = nc.dram_tensor(name, list(shape), F32, kind="ExternalOutput").ap()
            return self.outs[name]

        inp("x_all", [T, D])
        inp("cache_k", [2, PAST, 512]); inp("cache_v", [2, PAST, 512])
        inp("state_ssm", [2, 4096, 128]); inp("state_lru", [2, D])
        inp("cond", [2, D])
        inp("ada_w", [DEPTH, D, 6 * D]); inp("ada_b", [DEPTH, 6 * D])
        inp("norm_mix", [DEPTH, D]); inp("norm_ffn", [DEPTH, D])
        inp("attn_w_qkv", [2, D, 3072]); inp("attn_q_norm", [2, 128]); inp("attn_k_norm", [2, 128]); inp("attn_w_o", [2, D, D])
        inp("ssd_w_in", [1, D, 10368]); inp("ssd_conv_w", [1, 4, 6144]); inp("ssd_conv_b", [1, 6144])
        inp("ssd_dt_bias", [1, 128]); inp("ssd_a_log", [1, 128]); inp("ssd_d", [1, 64]); inp("ssd_norm", [1, 4096])
        inp("ssd_w_out", [1, 4096, D])
        inp("lru_w_in", [1, D, 4096]); inp("lru_conv_w", [1, 4, D]); inp("lru_conv_b", [1, D])
        inp("lru_w_a", [1, 2, 16, 128, 128]); inp("lru_b_a", [1, 2, D]); inp("lru_w_i", [1, 2, 16, 128, 128]); inp("lru_b_i", [1, 2, D])
        inp("lru_a_param", [1, 2, D]); inp("lru_w_out", [1, D, D])
        inp("ffn_w_in", [DEPTH, D, 2 * DFF]); inp("ffn_w_out", [DEPTH, DFF, D])
        inp("final_norm", [D])
        inp("c_ident", [128, 128]); inp("c_rot", [128, 128]); inp("c_cos", [128, S_LEN]); inp("c_sin", [128, S_LEN])
        inp("c_tri", [4, 128, 128])
        outp("y_all", [T, D])
        outp("nk", [2, 2, P_LEN, 512]); outp("nv", [2, 2, P_LEN, 512])
        outp("nssm", [2, 2, 4096, 128]); outp("nlru", [2, 2, D])
        if cfg.get("dbg"):
            outp("dbg_xT", [D, T])

        self.xT = k.dram("xT", [D, T], F32)
        self.xT_b = [[Buf() for _ in range(NBLK)] for _ in range(DC)]
        self.qT = k.dram("qT", [D, T], BF16)
        self.kT = k.dram("kT", [512, T], BF16)
        self.vtok = k.dram("vtok", [T, 512], BF16)
        self.oT = k.dram("oT", [4096, T], BF16)
        self.xbcT = k.dram("xbcT", [6144, T], F32)
        self.zsT = k.dram("zsT", [4096, T], BF16)
        self.dtk = k.dram("dtk", [T, 2, 128], F32)
        self.xtok = k.dram("xtok", [T, 4096], BF16)
        self.btok = k.dram("btok", [T, 1024], BF16)
        self.bcT = k.dram("bcT", [2048, T], BF16)
        self.ytok = k.dram("ytok", [2, T, 4096], F32)
        self.ident = k.sb([128, 128], F32, "ident"); self.ident_b = Buf()
        self.identh = k.sb([128, 128], BF16, "identh")
        self.ones_h = k.sb([128, 128], BF16, "ones_h")
        self.ones_f = k.sb([128, 128], F32, "ones_f")
        self.rot_h = k.sb([128, 128], BF16, "rot_h")
        self.cb = Buf()
        k.dma("sp", self.ident[:, :], self.ins["c_ident"][:, :], writes=[self.cb])
        k.dma("pool", self.identh[:, :], self.ins["c_ident"][:, :], writes=[self.cb])
        k.dma("pool", self.rot_h[:, :], self.ins["c_rot"][:, :], writes=[self.cb])
        k.op("dve", lambda h: h.memset(self.ones_h[:, :], 1.0), writes=[self.cb])
        k.op("dve", lambda h: h.memset(self.ones_f[:, :], 1.0), writes=[self.cb])
        self.epsb = k.sb([128, 1], F32, "epsb")
        k.op("dve", lambda h: h.memset(self.epsb[:, :], EPS), writes=[self.cb])
        self.ps = Pool([(nc.alloc_psum_tensor("ps%d" % i, [128, 512], F32), Buf()) for i in range(4)])
        self.psacc = Pool([(nc.alloc_psum_tensor("pa%d" % i, [128, 512], F32), Buf()) for i in range(4)])
        self.zcol = k.sb([128, 1], F32, "zcol")
        k.op("dve", lambda h: h.memset(self.zcol[:, :], 0.0), writes=[self.cb])
        self.t512 = k.sbpool(6, [128, 512], F32, "t512")
        self.rsp = k.sbpool(3, [128, 512], F32, "rsp")
        self.sgp = k.sbpool(3, [128, 512], F32, "sgp")
        self.h512 = k.sbpool(4, [128, 512], BF16, "h512")
        self.xpc = k.sbpool(3, [128, 512], F32, "xpc")
        self.xo = k.sbpool(3, [128, 512], F32, "xo")
        self.modT = k.sb([128, 96, 2], F32, "modT"); self.mod_b = Buf()
        self.gmix = k.sb([128, DEPTH, DC], F32, "gmix")
        self.gffn = k.sb([128, DEPTH, DC], F32, "gffn")
        self.gfin = k.sb([128, DC], F32, "gfin")
        self.AB = k.sb([128, 2, 2, 2, DC], F32, "AB"); self.AB_b = Buf()
        self.gate = k.sb([128, 2, 2, DC], F32, "gate")
        self.adabT = k.sb([128, DEPTH, 96], F32, "adabT")
        k.dma("sp", self.gmix[:, :, :], self.ins["norm_mix"].rearrange("l (c p) -> p l c", p=128), writes=[self.cb], nc_ok=True)
        k.dma("sp", self.gffn[:, :, :], self.ins["norm_ffn"].rearrange("l (c p) -> p l c", p=128), writes=[self.cb], nc_ok=True)
        k.dma("sp", self.gfin[:, :], self.ins["final_norm"].rearrange("(c p) -> p c", p=128), writes=[self.cb], nc_ok=True)
        k.dma("sp", self.adabT[:, :, :], self.ins["ada_b"].rearrange("l (j p) -> p l j", p=128), writes=[self.cb], nc_ok=True)
        self.condT = k.sb([128, DC, 2], F32, "condT")
        self.condh = k.sb([128, DC, 2], BF16, "condh")
        for r in range(2):
            k.dma("sp", self.condT[:, :, r], self.ins["cond"][r].rearrange("(c p) -> p c", p=128), writes=[self.cb], nc_ok=True)
        k.op("act", _act(self.condh[:, :, :], self.condT[:, :, :], AF.Silu), [self.cb], [self.cb])

    def alloc_gemm(self, nw=3, hT=True):
        k = self.k
        self.wpool = k.sbpool(nw, [128, 8192], BF16, "w")
        if hT:
            self.hTf = k.sb([128, 16384], BF16, "hT"); self.hT_b = Buf()
            self.hT = self.hTf[:, :].rearrange("p (k t) -> p k t", k=DC)
            self.hT32 = self.hTf[:, :].rearrange("p (k t) -> p k t", k=32)

    def xbufs(self, t0, n):
        b0 = t0 // 512
        return [self.xT_b[c][b0 + i] for c in range(DC) for i in range(n // 512)]

    def load_x_T(self):
        k = self.k
        with k.phase():
            stage = k.sbpool(2, [128, D], F32, "p0in")
            for tb in range(T // 128):
                st, sbuf = stage.next()
                k.dma("sp", st[:, :], self.ins["x_all"][tb * 128:(tb + 1) * 128, :], writes=[sbuf])
                for cg in range(4):
                    ps, pb = self.ps.next()
                    fns = [_tp(ps[:, j * 128:(j + 1) * 128], st[:, (cg * 4 + j) * 128:(cg * 4 + j + 1) * 128], self.ident[:, :]) for j in range(4)]
                    k.mm(fns, [sbuf, self.cb], [pb])
                    o, ob = self.xo.next()
                    if cg % 2 == 0:
                        k.op("act", _act(o[:, :], ps[:, :], AF.Copy), [pb], [ob])
                    else:
                        k.op("dve", _cp(o[:, :], ps[:, :]), [pb], [ob])
                    dst = self.xT[cg * 512:(cg + 1) * 512, tb * 128:(tb + 1) * 128].rearrange("(c p) t -> p c t", p=128)
                    wb = [self.xT_b[cg * 4 + j][tb // 4] for j in range(4)]
                    k.dma("sp", dst, o[:, :].rearrange("p (c t) -> p c t", c=4), reads=[ob], writes=wb)

    def rstd_from_sumsq(self, ps, pb, n, inv_n):
        k = self.k
        sd, sdb = self.t512.next()
        k.op("act", _act(sd[:, :n], ps[:, :n], AF.Sqrt, scale=inv_n, bias=self.epsb[:, 0:1]), [pb, self.cb], [sdb])
        rs, rsb = self.rsp.next()
        k.op("dve", lambda h: h.reciprocal(out=rs[:, :n], in_=sd[:, :n]), [sdb], [rsb])
        return rs, rsb

    def load_stats(self, tt0):
        k = self.k
        xi, xib = self.xin.next()
        src = self.xT[:, tt0:tt0 + 512].rearrange("(c p) t -> p c t", p=128)
        k.dma("sp", xi[:, :, :], src, reads=self.xbufs(tt0, 512), writes=[xib])
        ps, pb = self.ps.next()
        for c in range(DC):
            sq, sqb = self.h512.next()
            k.op("act", _act(sq[:, :], xi[:, c, :], AF.Square), [xib], [sqb])
            k.mm([_mmf(ps[:, :], self.ones_h[:, :], sq[:, :], start=(c == 0), stop=(c == DC - 1))], [sqb, self.cb], [pb])
        rs, rsb = self.rstd_from_sumsq(ps, pb, 512, 1.0 / D)
        return xi, xib, rs, rsb

    def norm_mod(self, t0, TT, which, r):
        k = self.k
        for blk in range(TT // 512):
            xi, xib, rs, rsb = self.load_stats(t0 + blk * 512)
            for c in range(DC):
                tm, tmb = self.t512.next()
                k.op("dve", _stt(tm[:, :], xi[:, c, :], self.AB[:, which, r, 0, c:c + 1], rs[:, :], ALU.mult, ALU.mult),
                     [xib, rsb, self.AB_b], [tmb])
                k.op("act", _act(self.hT[:, c, blk * 512:(blk + 1) * 512], tm[:, :], AF.Identity, bias=self.AB[:, which, r, 1, c:c + 1], scale=1.0),
                     [tmb, self.AB_b], [self.hT_b])

    def gemm_fm(self, act, act_b, KC, TT, W, groups, cb):
        k = self.k
        Wv = W.rearrange("(k p) n -> p k n", p=128)
        for gi, grp in enumerate(groups):
            G = len(grp)
            assert KC * G * 128 <= 8192
            wt, wb = self.wpool.next()
            wv = wt[:, 0:KC * G * 128].rearrange("p (k n) -> p k n", k=KC)
            s = 0
            while s < G:
                e = s
                while e + 1 < G and grp[e + 1] == grp[e] + 128:
                    e += 1
                n = (e - s + 1) * 128
                k.dma("pool", wv[:, :, s * 128:s * 128 + n], Wv[:, :, grp[s]:grp[s] + n], writes=[wb])
                s = e + 1
            for blk in range(TT // 512):
                for s, col in enumerate(grp):
                    ps, pb = self.ps.next()
                    fns = [_mmf(ps[:, :], wv[:, kk, s * 128:(s + 1) * 128], act[:, kk, blk * 512:(blk + 1) * 512],
                                start=(kk == 0), stop=(kk == KC - 1)) for kk in range(KC)]
                    k.mm(fns, [wb, act_b], [pb])
                    cb(gi, s, col, blk, ps, pb)

    def resid_cb(self, t0, gate_ap_fn):
        k = self.k

        def cb(gi, s, col, blk, ps, pb):
            c = col // 128
            tt0 = t0 + blk * 512
            xb = self.xT_b[c][tt0 // 512]
            xp, xpb = self.xpc.next()
            k.dma("sp", xp[:, :], self.xT[c * 128:(c + 1) * 128, tt0:tt0 + 512], reads=[xb], writes=[xpb])
            o, ob = self.xo.next()
            k.op("dve", _stt(o[:, :], ps[:, :], gate_ap_fn(c), xp[:, :], ALU.mult, ALU.add), [pb, xpb, self.AB_b], [ob])
            k.dma("sp", self.xT[c * 128:(c + 1) * 128, tt0:tt0 + 512], o[:, :], reads=[ob], writes=[xb])
        return cb

    def adaln(self, l):
        k = self.k
        Wv = self.ins["ada_w"][l].rearrange("(k p) n -> p k n", p=128)
        psT, psTb = self.psacc.next()
        ph = k.phase(); ph.__enter__()
        self.alloc_gemm(hT=False)
        for cb_ in range(24):
            wt, wb = self.wpool.next()
            wv = wt[:, :].rearrange("p (k n) -> p k n", k=DC)
            k.dma("pool", wv[:, :, :], Wv[:, :, cb_ * 512:(cb_ + 1) * 512], writes=[wb])
            ps, pb = self.ps.next()
            fns = [_mmf(ps[0:2, :], self.condh[:, kk, :], wv[:, kk, :], start=(kk == 0), stop=(kk == DC - 1)) for kk in range(DC)]
            k.mm(fns, [wb, self.cb], [pb])
            row, rowb = self.t512.next()
            k.op("dve", _cp(row[0:2, :], ps[0:2, :]), [pb], [rowb])
            fns = [_tp(psT[:, (cb_ * 4 + j) * 2:(cb_ * 4 + j + 1) * 2], row[0:2, j * 128:(j + 1) * 128], self.ident[0:2, 0:2]) for j in range(4)]
            k.mm(fns, [rowb, self.cb], [psTb])
        for r in range(2):
            k.op("dve", _tt(self.modT[:, :, r], psT[:, 0:192].rearrange("p (j r) -> p j r", r=2)[:, :, r], self.adabT[:, l, :], ALU.add),
                 [psTb, self.cb], [self.mod_b, self.AB_b])
        for which, (gt, o_sh, o_sc, o_g) in enumerate(((self.gmix, 0, 16, 32), (self.gffn, 48, 64, 80))):
            for r in range(2):
                k.op("dve", _stt(self.AB[:, which, r, 0, :], self.modT[:, o_sc:o_sc + 16, r], 1.0, gt[:, l, :], ALU.add, ALU.mult),
                     [self.mod_b, self.cb], [self.AB_b])
                k.op("dve", _cp(self.AB[:, which, r, 1, :], self.modT[:, o_sh:o_sh + 16, r]), [self.mod_b], [self.AB_b])
                k.op("dve", _cp(self.gate[:, which, r, :], self.modT[:, o_g:o_g + 16, r]), [self.mod_b], [self.AB_b])
        ph.__exit__(None, None, None)

    def ffn(self, l):
        k = self.k
        Win = self.ins["ffn_w_in"][l]
        Wout = self.ins["ffn_w_out"][l]
        with k.phase():
            self.alloc_gemm()
            self.xin = k.sbpool(1, [128, DC, 512], F32, "xin")
            big = k.sb([128, 22 * 1024], BF16, "big")
            big_b = Buf()
            for (t0, TT) in TILES:
                r = 1 if t0 >= S_LEN else 0
                self.norm_mod(t0, TT, 1, r)
                if self.cfg.get("dbg_hT"):
                    for c in range(DC):
                        tm, tmb = self.t512.next()
                        k.op("dve", _cp(tm[:, :], self.hT[:, c, 0:512]), [self.hT_b], [tmb])
                        k.dma("sp", self.outs["dbg_xT"][c * 128:(c + 1) * 128, t0:t0 + 512], tm[:, :], reads=[tmb])
                    continue
                hid = big[:, 0:22 * TT].rearrange("p (j t) -> p j t", j=22)
                for half in range(2):
                    state = {}

                    def cb_in(gi, s, col, blk, ps, pb, state=state, hid=hid):
                        if s == 0:
                            sg, sgb = self.sgp.next()
                            k.op("act", _act(sg[:, :], ps[:, :], AF.Silu), [pb], [sgb])
                            state["sg"] = (sg, sgb)
                        else:
                            sg, sgb = state["sg"]
                            k.op("dve", _tt(hid[:, gi, blk * 512:(blk + 1) * 512], ps[:, :], sg[:, :], ALU.mult), [pb, sgb], [big_b])
                    groups = [[(half * 22 + j) * 128, DFF + (half * 22 + j) * 128] for j in range(22)]
                    self.gemm_fm(self.hT, self.hT_b, DC, TT, Win, groups, cb_in)
                    Wo = Wout[half * 2816:(half + 1) * 2816, :]
                    groups = [[c * 128, (c + 1) * 128] for c in range(0, DC, 2)]
                    self.gemm_fm(hid, big_b, 22, TT, Wo, groups, self.resid_cb(t0, lambda c, r=r: self.gate[:, 1, r, c:c + 1]))

    def final(self):
        k = self.k
        with k.phase():
            self.xin = k.sbpool(1, [128, DC, 512], F32, "xin")
            p0out = k.sbpool(2, [128, D], F32, "p0out")
            for blk in range(NBLK):
                tt0 = blk * 512
                xi, xib, rs, rsb = self.load_stats(tt0)
                for c in range(DC):
                    k.op("dve", _stt(xi[:, c, :], xi[:, c, :], self.gfin[:, c:c + 1], rs[:, :], ALU.mult, ALU.mult), [xib, rsb, self.cb], [xib])
                for tb in range(4):
                    o, ob = p0out.next()
                    for cg in range(4):
                        ps, pb = self.ps.next()
                        fns = [_tp(ps[:, j * 128:(j + 1) * 128], xi[:, cg * 4 + j, tb * 128:(tb + 1) * 128], self.ident[:, :]) for j in range(4)]
                        k.mm(fns, [xib, self.cb], [pb])
                        if cg % 2 == 0:
                            k.op("act", _act(o[:, cg * 512:(cg + 1) * 512], ps[:, :], AF.Copy), [pb], [ob])
                        else:
                            k.op("dve", _cp(o[:, cg * 512:(cg + 1) * 512], ps[:, :]), [pb], [ob])
                    k.dma("sp", self.outs["y_all"][tt0 + tb * 128:tt0 + (tb + 1) * 128, :], o[:, :], reads=[ob])

    def dbg_dump(self):
        k = self.k
        with k.phase():
            self.xin = k.sbpool(1, [128, DC, 512], F32, "xin")
            for blk in range(NBLK):
                tt0 = blk * 512
                xi, xib = self.xin.next()
                k.dma("sp", xi[:, :, :], self.xT[:, tt0:tt0 + 512].rearrange("(c p) t -> p c t", p=128), reads=self.xbufs(tt0, 512), writes=[xib])
                k.dma("sp", self.outs["dbg_xT"][:, tt0:tt0 + 512].rearrange("(c p) t -> p c t", p=128), xi[:, :, :], reads=[xib])

    def build(self):
        cfg = self.cfg
        k = self.k
        self.load_x_T()
        for l in cfg.get("layers", range(DEPTH)):
            self.adaln(l)
            if cfg.get("mixers", True):
                kind = l % 3
                if kind == 0:
                    self.attention(l)
                elif kind == 1:
                    self.ssd(l)
                else:
                    self.lru(l)
            if cfg.get("ffn", True):
                self.ffn(l)
        if cfg.get("dbg") and not cfg.get("dbg_hT"):
            self.dbg_dump()
        self.final()
        return k.finish()

    def attention(self, l):
        k = self.k
        jl = l // 3
        Wqkv = self.ins["attn_w_qkv"][jl]
        Wo = self.ins["attn_w_o"][jl]
        nk_o, nv_o = self.outs["nk"], self.outs["nv"]
        with k.phase():
            self.alloc_gemm()
            self.xin = k.sbpool(1, [128, DC, 512], F32, "xin")
            cosT = k.sb([128, S_LEN], F32, "cosT")
            sinT = k.sb([128, S_LEN], F32, "sinT")
            qkg = k.sb([128, 2], F32, "qkg")
            tb_ = Buf()
            k.dma("sp", cosT[:, :], self.ins["c_cos"][:, :], writes=[tb_])
            k.dma("sp", sinT[:, :], self.ins["c_sin"][:, :], writes=[tb_])
            k.dma("sp", qkg[:, 0:1], self.ins["attn_q_norm"][jl].rearrange("(p o) -> p o", o=1), writes=[tb_], nc_ok=True)
            k.dma("sp", qkg[:, 1:2], self.ins["attn_k_norm"][jl].rearrange("(p o) -> p o", o=1), writes=[tb_], nc_ok=True)
            qn_p = k.sbpool(2, [128, 512], F32, "qn")
            for (t0, TT) in self.cfg.get("tiles", TILES):
                prompt = t0 >= S_LEN
                r = 1 if prompt else 0
                self.norm_mod(t0, TT, 0, r)

                def cb(gi, s, col, blk, ps, pb, t0=t0, prompt=prompt):
                    j = col // 128
                    tt0 = t0 + blk * 512
                    if j < 20:
                        sq, sqb = self.h512.next()
                        k.op("act", _act(sq[:, :], ps[:, :], AF.Square), [pb], [sqb])
                        ps2, pb2 = self.ps.next()
                        k.mm([_mmf(ps2[:, :], self.ones_h[:, :], sq[:, :])], [sqb, self.cb], [pb2])
                        rs, rsb = self.rstd_from_sumsq(ps2, pb2, 512, 1.0 / 128)
                        qn, qnb = qn_p.next()
                        gi_ = 0 if j < 16 else 1
                        k.op("dve", _stt(qn[:, :], ps[:, :], qkg[:, gi_:gi_ + 1], rs[:, :], ALU.mult, ALU.mult), [pb, rsb, tb_], [qnb])
                        o, ob = self.h512.next()
                        if not prompt:
                            qh, qhb = self.h512.next()
                            k.op("act", _act(qh[:, :], qn[:, :], AF.Copy), [qnb], [qhb])
                            ps3, pb3 = self.ps.next()
                            k.mm([_mmf(ps3[:, :], self.rot_h[:, :], qh[:, :])], [qhb, self.cb], [pb3])
                            t1, t1b = self.t512.next()
                            k.op("dve", _tt(t1[:, :], qn[:, :], cosT[:, tt0:tt0 + 512], ALU.mult), [qnb, tb_], [t1b])
                            t2, t2b = self.t512.next()
                            k.op("dve", _tt(t2[:, :], ps3[:, :], sinT[:, tt0:tt0 + 512], ALU.mult), [pb3, tb_], [t2b])
                            k.op("dve", _tt(o[:, :], t1[:, :], t2[:, :], ALU.add), [t1b, t2b], [ob])
                        else:
                            k.op("act", _act(o[:, :], qn[:, :], AF.Copy), [qnb], [ob])
                            if j >= 16:
                                hh = j - 16
                                ps4, pb4 = self.ps.next()
                                k.mm([_tp(ps4[:, i * 128:(i + 1) * 128], qn[:, i * 128:(i + 1) * 128], self.ident[:, :]) for i in range(4)], [qnb, self.cb], [pb4])
                                kf, kfb = self.t512.next()
                                k.op("act", _act(kf[:, :], ps4[:, :], AF.Copy), [pb4], [kfb])
                                for sq_ in range(2):
                                    k.dma("sp", nk_o[sq_, jl, :, hh * 128:(hh + 1) * 128].rearrange("(b p) f -> p b f", p=128),
                                          kf[:, sq_ * 256:(sq_ + 1) * 256].rearrange("p (b f) -> p b f", b=2), reads=[kfb])
                        dst = self.qT[j * 128:(j + 1) * 128, tt0:tt0 + 512] if j < 16 else self.kT[(j - 16) * 128:(j - 15) * 128, tt0:tt0 + 512]
                        k.dma("sp", dst, o[:, :], reads=[ob])
                    else:
                        hh = j - 20
                        vt, vtb = self.t512.next()
                        k.op("act", _act(vt[:, :], ps[:, :], AF.Copy), [pb], [vtb])
                        ps4, pb4 = self.ps.next()
                        k.mm([_tp(ps4[:, i * 128:(i + 1) * 128], vt[:, i * 128:(i + 1) * 128], self.ident[:, :]) for i in range(4)], [vtb, self.cb], [pb4])
                        vb, vbb = self.h512.next()
                        k.op("dve", _cp(vb[:, :], ps4[:, :]), [pb4], [vbb])
                        k.dma("sp", self.vtok[tt0:tt0 + 512, hh * 128:(hh + 1) * 128].rearrange("(b p) f -> p b f", p=128),
                              vb[:, :].rearrange("p (b f) -> p b f", b=4), reads=[vbb])
                        if prompt and self.cfg.get("vmode", 2) > 0:
                            vf, vfb = self.t512.next()
                            if self.cfg.get("vmode", 2) == 1:
                                k.op("act", _act(vf[:, :], ps4[:, :], AF.Copy), [pb4], [vfb])
                            else:
                                k.op("dve", _cp(vf[:, :], ps4[:, :]), [pb4], [vfb])
                            for sq_ in range(2):
                                k.dma("sp", nv_o[sq_, jl, :, hh * 128:(hh + 1) * 128].rearrange("(b p) f -> p b f", p=128),
                                      vf[:, sq_ * 256:(sq_ + 1) * 256].rearrange("p (b f) -> p b f", b=2), reads=[vfb])
                groups = [[(g4 * 4 + i) * 128 for i in range(4)] for g4 in self.cfg.get("qkv_groups", range(6))]
                self.gemm_fm(self.hT, self.hT_b, DC, TT, Wqkv, groups, cb)
        if self.cfg.get("attn_stop") == 1:
            return
        with k.phase():
            KTp = k.sbpool(2, [128, S_LEN + PAST], BF16, "KT")
            Vp = k.sbpool(2, [128, 36, 128], BF16, "Vt")
            ck = k.sb([128, 4, 512], F32, "ck")
            ckb = Buf()
            k.dma("sp", ck[:, :, :], self.ins["cache_k"][jl].rearrange("(b p) f -> p b f", p=128), writes=[ckb])
            QTp = k.sbpool(3, [128, 512], BF16, "QT")
            PTp = k.sbpool(4, [128, 512], BF16, "PT")
            scale = 1.0 / np.sqrt(128.0)
            jobs = [(0, 32, 0, S_LEN, True), (S_LEN, 2, S_LEN, P_LEN, False), (S_LEN + P_LEN, 2, S_LEN + P_LEN, P_LEN, False)]
            for (k0, nkb0, q0, qlen, ctx) in jobs:
                nkb = nkb0 + (4 if ctx else 0)
                QB = min(512, qlen)
                for g in range(4):
                    KT, KTb = KTp.next()
                    V, Vb = Vp.next()
                    k.dma("sp", KT[:, 0:nkb0 * 128], self.kT[g * 128:(g + 1) * 128, k0:k0 + nkb0 * 128], writes=[KTb])
                    k.dma("sp", V[:, 0:nkb0, :], self.vtok[k0:k0 + nkb0 * 128, g * 128:(g + 1) * 128].rearrange("(b p) f -> p b f", p=128), writes=[Vb])
                    if ctx:
                        ps4, pb4 = self.ps.next()
                        k.mm([_tp(ps4[:, i * 128:(i + 1) * 128], ck[:, i, g * 128:(g + 1) * 128], self.ident[:, :]) for i in range(4)], [ckb, self.cb], [pb4])
                        k.op("act", _act(KT[:, nkb0 * 128:nkb * 128], ps4[:, :], AF.Copy), [pb4], [KTb])
                        k.dma("pool", V[:, nkb0:nkb, :], self.ins["cache_v"][jl][:, g * 128:(g + 1) * 128].rearrange("(b p) f -> p b f", p=128), writes=[Vb])
                    for h in range(4):
                        hd = g * 4 + h
                        for qb in range(qlen // QB):
                            qq0 = q0 + qb * QB
                            QT, QTb = QTp.next()
                            k.dma("sp", QT[:, 0:QB], self.qT[hd * 128:(hd + 1) * 128, qq0:qq0 + QB], writes=[QTb])
                            psO, pbO = self.psacc.next()
                            psD, pbD = self.psacc.next()
                            prev = None
                            for kb in range(nkb + 1):
                                cur = None
                                if kb < nkb:
                                    psS, pbS = self.ps.next()
                                    k.mm([_mmf(psS[:, 0:QB], KT[:, kb * 128:(kb + 1) * 128], QT[:, 0:QB])], [KTb, QTb], [pbS])
                                    PT, PTb = PTp.next()
                                    k.op("act", _act(PT[:, 0:QB], psS[:, 0:QB], AF.Exp, scale=float(scale)), [pbS], [PTb])
                                    cur = (PT, PTb, kb)
                                if prev is not None:
                                    PTq, PTqb, kq = prev
                                    k.mm([_mmf(psO[:, 0:QB], V[:, kq, :], PTq[:, 0:QB], start=(kq == 0), stop=(kq == nkb - 1)),
                                          _mmf(psD[:, 0:QB], self.ones_h[:, :], PTq[:, 0:QB], start=(kq == 0), stop=(kq == nkb - 1))],
                                         [Vb, PTqb, self.cb], [pbO, pbD])
                                prev = cur
                            rc, rcb = self.t512.next()
                            k.op("dve", lambda h, rc=rc, psD=psD, QB=QB: h.reciprocal(out=rc[:, 0:QB], in_=psD[:, 0:QB]), [pbD], [rcb])
                            o, ob = self.h512.next()
                            k.op("dve", _tt(o[:, 0:QB], psO[:, 0:QB], rc[:, 0:QB], ALU.mult), [pbO, rcb], [ob])
                            k.dma("sp", self.oT[hd * 128:(hd + 1) * 128, qq0:qq0 + QB], o[:, 0:QB], reads=[ob])
        if self.cfg.get("attn_stop") == 2:
            return
        self.out_proj(Wo, DC)

    def out_proj(self, Wo, KC):
        k = self.k
        with k.phase():
            self.alloc_gemm()
            TT = 16384 // KC
            G = 8192 // (KC * 128)
            act = self.hT if KC == DC else self.hT32
            for t0 in range(0, T, TT):
                r = 1 if t0 >= S_LEN else 0
                TTl = min(TT, T - t0)
                for h2 in range(KC // DC):
                    k.dma("sp", act[:, h2 * DC:(h2 + 1) * DC, 0:TTl],
                          self.oT[h2 * D:(h2 + 1) * D, t0:t0 + TTl].rearrange("(c p) t -> p c t", p=128), writes=[self.hT_b])
                groups = [[(g4 * G + i) * 128 for i in range(G)] for g4 in range(DC // G)]
                self.gemm_fm(act, self.hT_b, KC, TTl, Wo, groups, self.resid_cb(t0, lambda c, r=r: self.gate[:, 0, r, c:c + 1]))

    def conv_chunk(self, src, pin, pinb, acc, accb, cw, cbias, c, prm):
        k = self.k
        for (t0, L, q0) in SEGQ:
            k.dma("sp", pin[:, q0:q0 + L], src[:, t0:t0 + L], writes=[pinb])
        N = PADT - 3
        k.op("dve", _ts(acc[:, 1:1 + N], pin[:, 0:N], cw[:, 0, c:c + 1], cbias[:, c:c + 1], ALU.mult, ALU.add), [pinb, prm], [accb])
        for kk in range(1, 4):
            k.op("dve", _stt(acc[:, 1:1 + N], pin[:, kk:kk + N], cw[:, kk, c:c + 1], acc[:, 1:1 + N], ALU.mult, ALU.add), [pinb, prm, accb], [accb])

    def conv_params(self, wname, bname, jl, nch, prm):
        k = self.k
        cw = k.sb([128, 4, nch], F32, "cw")
        cbias = k.sb([128, nch], F32, "cbias")
        for kk in range(4):
            k.dma("sp", cw[:, kk, :], self.ins[wname][jl, kk].rearrange("(c p) -> p c", p=128), writes=[prm], nc_ok=True)
        k.dma("sp", cbias[:, :], self.ins[bname][jl].rearrange("(c p) -> p c", p=128), writes=[prm], nc_ok=True)
        return cw, cbias

    def lru(self, l):
        k = self.k
        jl = l // 3
        Win = self.ins["lru_w_in"][jl]
        Wout = self.ins["lru_w_out"][jl]
        recT, gT = self.xbcT, self.zsT
        with k.phase():
            self.alloc_gemm()
            self.xin = k.sbpool(1, [128, DC, 512], F32, "xin")
            for (t0, TT) in TILES:
                r = 1 if t0 >= S_LEN else 0
                self.norm_mod(t0, TT, 0, r)

                def cb(gi, s, col, blk, ps, pb, t0=t0):
                    j = col // 128
                    tt0 = t0 + blk * 512
                    if j < 16:
                        o, ob = self.h512.next()
                        k.op("act", _act(o[:, :], ps[:, :], AF.Gelu_apprx_tanh), [pb], [ob])
                        k.dma("sp", gT[j * 128:(j + 1) * 128, tt0:tt0 + 512], o[:, :], reads=[ob])
                    else:
                        o, ob = self.xo.next()
                        k.op("dve", _cp(o[:, :], ps[:, :]), [pb], [ob])
                        k.dma("sp", recT[(j - 16) * 128:(j - 15) * 128, tt0:tt0 + 512], o[:, :], reads=[ob])
                groups = [[(g4 * 4 + i) * 128 for i in range(4)] for g4 in range(8)]
                self.gemm_fm(self.hT, self.hT_b, DC, TT, Win, groups, cb)
        with k.phase():
            N = PADT - 3
            prm = Buf()
            pin = k.sb([128, PADT], F32, "pin"); pinb = Buf()
            k.op("dve", lambda h: h.memset(pin[:, :], 0.0), writes=[pinb])
            acc = k.sb([128, PADT], F32, "acc"); accb = Buf()
            recb = k.sb([128, PADT], BF16, "recb"); recbb = Buf()
            R = k.sb([128, PADT], F32, "R"); Rb = Buf()
            I = k.sb([128, PADT], F32, "I"); Ib = Buf()
            Hf = k.sb([128, PADT], F32, "Hf"); Hfb = Buf()
            Hb = k.sb([128, PADT], F32, "Hb"); Hbb = Buf()
            gpad = k.sb([128, PADT], BF16, "gpad"); gpb = Buf()
            yb = k.sb([128, PADT], BF16, "yb"); ybb = Buf()
            cw, cbias = self.conv_params("lru_conv_w", "lru_conv_b", jl, 16, prm)
            ba = k.sb([128, 2, 16], F32, "ba"); bi = k.sb([128, 2, 16], F32, "bi")
            apm = k.sb([128, 32], F32, "apm"); h0 = k.sb([128, 2, 16], F32, "h0")
            wa = k.sb([128, 32, 128], BF16, "wa"); wi = k.sb([128, 32, 128], BF16, "wi")
            for d in range(2):
                k.dma("sp", ba[:, d, :], self.ins["lru_b_a"][jl, d].rearrange("(c p) -> p c", p=128), writes=[prm], nc_ok=True)
                k.dma("sp", bi[:, d, :], self.ins["lru_b_i"][jl, d].rearrange("(c p) -> p c", p=128), writes=[prm], nc_ok=True)
                k.dma("sp", apm[:, d * 16:(d + 1) * 16], self.ins["lru_a_param"][jl, d].rearrange("(c p) -> p c", p=128), writes=[prm], nc_ok=True)
                k.dma("sp", h0[:, d, :], self.ins["state_lru"][d].rearrange("(c p) -> p c", p=128), writes=[prm], nc_ok=True)
                k.dma("pool", wa[:, d * 16:(d + 1) * 16, :], self.ins["lru_w_a"][jl, d].rearrange("c w v -> w c v"), writes=[prm])
                k.dma("pool", wi[:, d * 16:(d + 1) * 16, :], self.ins["lru_w_i"][jl, d].rearrange("c w v -> w c v"), writes=[prm])
            t1 = k.sb([128, 32], F32, "lt1"); t2 = k.sb([128, 32], F32, "lt2")
            nsp8 = k.sb([128, 32], F32, "nsp8"); nsp16 = k.sb([128, 32], F32, "nsp16")
            k.op("act", _act(t1[:, :], apm[:, :], AF.Abs), [prm], [prm])
            k.op("act", _act(t1[:, :], t1[:, :], AF.Exp, scale=-1.0), [prm], [prm])
            k.op("act", _act(t1[:, :], t1[:, :], AF.Ln, bias=self.ones_f[:, 0:1], scale=1.0), [prm, self.cb], [prm])
            k.op("dve", _ts(t2[:, :], apm[:, :], -1.0, 0.0, ALU.mult, ALU.max), [prm], [prm])
            k.op("dve", _tt(t2[:, :], t2[:, :], t1[:, :], ALU.add), [prm], [prm])
            k.op("dve", _ts(nsp8[:, :], t2[:, :], -8.0, None, ALU.mult), [prm], [prm])
            k.op("dve", _ts(nsp16[:, :], t2[:, :], -16.0, None, ALU.mult), [prm], [prm])
            fin = k.sb([128, 2, 2, 16], F32, "fin"); finb = Buf()
            for c in range(16):
                self.conv_chunk(recT[c * 128:(c + 1) * 128, :], pin, pinb, acc, accb, cw, cbias, c, prm)
                k.op("act", _act(recb[:, 1:1 + N], acc[:, 1:1 + N], AF.Copy), [accb], [recbb])
                for (t0, L, q0) in SEGQ:
                    k.dma("sp", gpad[:, q0:q0 + L], gT[c * 128:(c + 1) * 128, t0:t0 + L], writes=[gpb])
                for d in range(2):
                    dc_ = d * 16 + c
                    for q in range(1, 1 + N, 512):
                        w = min(512, 1 + N - q)
                        ps, pb = self.ps.next()
                        k.mm([_mmf(ps[:, 0:w], wa[:, dc_, :], recb[:, q:q + w])], [prm, recbb], [pb])
                        k.op("act", _act(R[:, q:q + w], ps[:, 0:w], AF.Sigmoid, bias=ba[:, d, c:c + 1], scale=1.0), [pb, prm], [Rb])
                        ps2, pb2 = self.ps.next()
                        k.mm([_mmf(ps2[:, 0:w], wi[:, dc_, :], recb[:, q:q + w])], [prm, recbb], [pb2])
                        k.op("act", _act(I[:, q:q + w], ps2[:, 0:w], AF.Sigmoid, bias=bi[:, d, c:c + 1], scale=1.0), [pb2, prm], [Ib])
                    S, Sb = (Hf, Hfb) if d == 0 else (Hb, Hbb)
                    k.op("act", _act(S[:, 1:1 + N], R[:, 1:1 + N], AF.Exp, scale=nsp16[:, dc_:dc_ + 1]), [Rb, prm], [Sb])
                    k.op("act", _act(S[:, 1:1 + N], S[:, 1:1 + N], AF.Sqrt, bias=self.ones_f[:, 0:1], scale=-1.0), [Sb, self.cb], [Sb])
                    k.op("act", _act(R[:, 1:1 + N], R[:, 1:1 + N], AF.Exp, scale=nsp8[:, dc_:dc_ + 1]), [Rb, prm], [Rb])
                    k.op("dve", _tt(I[:, 1:1 + N], I[:, 1:1 + N], S[:, 1:1 + N], ALU.mult), [Ib, Sb], [Ib])
                    k.op("dve", _tt(I[:, 1:1 + N], I[:, 1:1 + N], acc[:, 1:1 + N], ALU.mult), [Ib, accb], [Ib])
                    H, Hbuf = (Hf, Hfb) if d == 0 else (Hb, Hbb)
                    for si, (t0, L, q0) in enumerate(SEGQ):
                        init = h0[:, d, c:c + 1] if si == 0 else self.zcol[:, 0:1]
                        if d == 0:
                            o_, a_, u_ = H[:, q0:q0 + L], R[:, q0:q0 + L], I[:, q0:q0 + L]
                        else:
                            o_, a_, u_ = H[:, q0 + L - 1:q0 - 1:-1], R[:, q0 + L - 1:q0 - 1:-1], I[:, q0 + L - 1:q0 - 1:-1]
                        k.op("dve", lambda h, o_=o_, a_=a_, u_=u_, init=init: h.tensor_tensor_scan(
                            out=o_, data0=a_, data1=u_, initial=init, op0=ALU.mult, op1=ALU.add), [Rb, Ib, prm, self.cb], [Hbuf])
                        if si > 0:
                            src = H[:, q0 + L - 1:q0 + L] if d == 0 else H[:, q0:q0 + 1]
                            k.op("dve", _cp(fin[:, si - 1, d, c:c + 1], src), [Hbuf], [finb])
                k.op("dve", _tt(Hf[:, 1:1 + N], Hf[:, 1:1 + N], Hb[:, 1:1 + N], ALU.add), [Hfb, Hbb], [Hfb])
                k.op("dve", _tt(yb[:, 1:1 + N], Hf[:, 1:1 + N], gpad[:, 1:1 + N], ALU.mult), [Hfb, gpb], [ybb])
                for (t0, L, q0) in SEGQ:
                    k.dma("sp", self.oT[c * 128:(c + 1) * 128, t0:t0 + L], yb[:, q0:q0 + L], reads=[ybb])
            for pr in range(2):
                for d in range(2):
                    k.dma("sp", self.outs["nlru"][pr, d].rearrange("(c p) -> p c", p=128), fin[:, pr, d, :], reads=[finb], nc_ok=True)
        self.out_proj(Wout, DC)

    def ssd(self, l):
        k = self.k
        jl = l // 3
        Win = self.ins["ssd_w_in"][jl]
        Wout = self.ins["ssd_w_out"][jl]
        N = PADT - 3

        def v3(ap, a):
            return ap.rearrange("p (a b) -> p a b", a=a)

        with k.phase():
            self.alloc_gemm()
            self.xin = k.sbpool(1, [128, DC, 512], F32, "xin")
            prm = Buf()
            dtb = k.sb([128, 1], F32, "dtb"); nea = k.sb([128, 1], F32, "nea")
            k.dma("sp", dtb[:, :], self.ins["ssd_dt_bias"][jl].rearrange("(p o) -> p o", o=1), writes=[prm], nc_ok=True)
            k.dma("sp", nea[:, :], self.ins["ssd_a_log"][jl].rearrange("(p o) -> p o", o=1), writes=[prm], nc_ok=True)
            k.op("act", _act(nea[:, :], nea[:, :], AF.Exp), [prm], [prm])
            for (t0, TT) in TILES:
                r = 1 if t0 >= S_LEN else 0
                self.norm_mod(t0, TT, 0, r)

                def cb(gi, s, col, blk, ps, pb, t0=t0):
                    j = col // 128
                    tt0 = t0 + blk * 512
                    if j < 32:
                        o, ob = self.h512.next()
                        k.op("act", _act(o[:, :], ps[:, :], AF.Silu), [pb], [ob])
                        k.dma("sp", self.zsT[j * 128:(j + 1) * 128, tt0:tt0 + 512], o[:, :], reads=[ob])
                    elif j < 80:
                        o, ob = self.xo.next()
                        k.op("dve", _cp(o[:, :], ps[:, :]), [pb], [ob])
                        k.dma("sp", self.xbcT[(j - 32) * 128:(j - 31) * 128, tt0:tt0 + 512], o[:, :], reads=[ob])
                    else:
                        xv, xvb = self.t512.next()
                        k.op("act", _act(xv[:, :], ps[:, :], AF.Identity, bias=dtb[:, 0:1], scale=1.0), [pb, prm], [xvb])
                        ax, axb = self.t512.next()
                        k.op("act", _act(ax[:, :], xv[:, :], AF.Abs), [xvb], [axb])
                        k.op("act", _act(ax[:, :], ax[:, :], AF.Exp, scale=-1.0), [axb], [axb])
                        k.op("act", _act(ax[:, :], ax[:, :], AF.Ln, bias=self.ones_f[:, 0:1], scale=1.0), [axb, self.cb], [axb])
                        dtv, dtvb = self.t512.next()
                        k.op("dve", _stt(dtv[:, :], xv[:, :], 0.0, ax[:, :], ALU.max, ALU.add), [xvb, axb], [dtvb])
                        dav, davb = self.t512.next()
                        k.op("dve", _ts(dav[:, :], dtv[:, :], nea[:, 0:1], -1.0, ALU.mult, ALU.mult), [dtvb, prm], [davb])
                        for which, (src, srcb) in enumerate(((dtv, dtvb), (dav, davb))):
                            ps4, pb4 = self.ps.next()
                            k.mm([_tp(ps4[:, i * 128:(i + 1) * 128], src[:, i * 128:(i + 1) * 128], self.ident[:, :]) for i in range(4)], [srcb, self.cb], [pb4])
                            o, ob = self.xo.next()
                            k.op("dve", _cp(o[:, :], ps4[:, :]), [pb4], [ob])
                            k.dma("sp", self.dtk[tt0:tt0 + 512, which, :].rearrange("(b p) f -> p b f", p=128),
                                  o[:, :].rearrange("p (b f) -> p b f", b=4), reads=[ob])
                groups = [[(g4 * 4 + i) * 128 for i in range(4)] for g4 in range(20)] + [[80 * 128]]
                self.gemm_fm(self.hT, self.hT_b, DC, TT, Win, groups, cb)
        if self.cfg.get("ssd_stop") == 1:
            return
        with k.phase():
            prm = Buf()
            pin = k.sb([128, PADT], F32, "pin"); pinb = Buf()
            k.op("dve", lambda h: h.memset(pin[:, :], 0.0), writes=[pinb])
            acc = k.sb([128, PADT], F32, "acc"); accb = Buf()
            sil = k.sb([128, PADT], F32, "sil"); silb_ = Buf()
            silh = k.sb([128, PADT], BF16, "silh"); silhb = Buf()
            cw, cbias = self.conv_params("ssd_conv_w", "ssd_conv_b", jl, 48, prm)
            for cc in range(48):
                self.conv_chunk(self.xbcT[cc * 128:(cc + 1) * 128, :], pin, pinb, acc, accb, cw, cbias, cc, prm)
                if cc < 40:
                    k.op("act", _act(sil[:, 1:1 + N], acc[:, 1:1 + N], AF.Silu), [accb], [silb_])
                    for grp in range(T // 512):
                        ps4, pb4 = self.ps.next()
                        fns = []
                        for i in range(4):
                            t = (grp * 4 + i) * 128
                            q = t + 1 + 2 * (0 if t < S_LEN else (1 if t < S_LEN + P_LEN else 2))
                            fns.append(_tp(ps4[:, i * 128:(i + 1) * 128], sil[:, q:q + 128], self.ident[:, :]))
                        k.mm(fns, [silb_, self.cb], [pb4])
                        o, ob = self.h512.next()
                        if grp % 2 == 0:
                            k.op("dve", _cp(o[:, :], ps4[:, :]), [pb4], [ob])
                        else:
                            k.op("act", _act(o[:, :], ps4[:, :], AF.Copy), [pb4], [ob])
                        dst = self.xtok[grp * 512:(grp + 1) * 512, cc * 128:(cc + 1) * 128] if cc < 32 else \
                            self.btok[grp * 512:(grp + 1) * 512, (cc - 32) * 128:(cc - 31) * 128]
                        k.dma("sp", dst.rearrange("(b p) f -> p b f", p=128), o[:, :].rearrange("p (b f) -> p b f", b=4), reads=[ob])
                if cc >= 32:
                    k.op("act", _act(silh[:, 1:1 + N], acc[:, 1:1 + N], AF.Silu), [accb], [silhb])
                    for (t0, L, q0) in SEGQ:
                        k.dma("sp", self.bcT[(cc - 32) * 128:(cc - 31) * 128, t0:t0 + L], silh[:, q0:q0 + L], reads=[silhb])
        if self.cfg.get("ssd_stop") == 2:
            return
        with k.phase():
            prm = Buf()
            tri = k.sb([128, 4, 128], F32, "tri")
            for i in range(4):
                k.dma("sp", tri[:, i, :], self.ins["c_tri"][i], writes=[prm])
            stT = k.sb([128, 4096], F32, "stT"); stTb = [Buf() for _ in range(8)]
            stH = k.sb([128, 4096], BF16, "stH"); stHb = [Buf() for _ in range(8)]
            hst = k.sb([128, 32, 128], F32, "hst"); hstb = Buf()
            xt_p = k.sbpool(3, [128, 4096], BF16, "xt")
            xdt_p = k.sbpool(2, [128, 4096], BF16, "xdt")
            xw_p = k.sbpool(2, [128, 4096], BF16, "xw")
            bt_p = k.sbpool(3, [128, 1024], BF16, "bt")
            bT_p = k.sbpool(3, [128, 8, 128], BF16, "bT")
            cT_p = k.sbpool(3, [128, 8, 128], BF16, "cT")
            dd_p = k.sbpool(3, [128, 2, 128], F32, "dd")
            sc_p = k.sbpool(2, [128, 192], F32, "sc")
            dU_p = k.sbpool(4, [128, 512], F32, "dU")
            E_p = k.sbpool(6, [128, 512], F32, "E")
            MT_p = k.sbpool(4, [128, 512], BF16, "MT")
            SM_p = k.sbpool(4, [128, 128], F32, "SM")
            y_p = k.sbpool(3, [128, 512], F32, "yo")

            def issue_loads(t0):
                xt, xtb = xt_p.next()
                k.dma("sp", xt[:, :], self.xtok[t0:t0 + 128, :], writes=[xtb])
                bt, btb = bt_p.next()
                k.dma("sp", bt[:, :], self.btok[t0:t0 + 128, :], writes=[btb])
                bT, bTb = bT_p.next()
                k.dma("sp", bT[:, :, :], self.bcT[0:1024, t0:t0 + 128].rearrange("(g p) t -> p g t", p=128), writes=[bTb])
                cT, cTb = cT_p.next()
                k.dma("sp", cT[:, :, :], self.bcT[1024:2048, t0:t0 + 128].rearrange("(g p) t -> p g t", p=128), writes=[cTb])
                dd, ddb = dd_p.next()
                k.dma("sp", dd[:, :, :], self.dtk[t0:t0 + 128, :, :], writes=[ddb])
                return dict(xt=xt, xtb=xtb, bt=bt, btb=btb, bT=bT, bTb=bTb, cT=cT, cTb=cTb, dd=dd, ddb=ddb, t0=t0)

            for d in range(2):
                Lm = tri[:, 0 if d == 0 else 2, :]
                Um = tri[:, 1 if d == 0 else 3, :]

                def prologue(c, d=d, Lm=Lm, Um=Um):
                    c["dt"] = c["dd"][:, 0, d * 64:(d + 1) * 64]
                    c["dta"] = c["dd"][:, 1, d * 64:(d + 1) * 64]
                    psc, pscb = self.ps.next()
                    k.mm([_mmf(psc[:, 0:64], Um, c["dta"]), _mmf(psc[:, 64:128], Lm, c["dta"]), _mmf(psc[:, 128:192], self.ones_f[:, :], c["dta"])],
                         [prm, c["ddb"], self.cb], [pscb])
                    sc, scb = sc_p.next()
                    k.op("act", _act(sc[:, :], psc[:, 0:192], AF.Exp), [pscb], [scb])
                    xdt, xdtb = xdt_p.next()
                    k.op("dve", _tt(v3(xdt[:, :], 64), v3(c["xt"][:, :], 64), c["dt"].unsqueeze(2).broadcast_to([128, 64, 64]), ALU.mult), [c["xtb"], c["ddb"]], [xdtb])
                    xw, xwb = xw_p.next()
                    k.op("dve", _tt(v3(xw[:, :], 64), v3(xdt[:, :], 64), sc[:, 64:128].unsqueeze(2).broadcast_to([128, 64, 64]), ALU.mult), [xdtb, scb], [xwb])
                    c.update(sc=sc, scb=scb, xdt=xdt, xdtb=xdtb, xw=xw, xwb=xwb)

                def stage_a(c, g, Lm=Lm, Um=Um):
                    psS, psSb = self.ps.next()
                    k.mm([_mmf(psS[:, 0:128], c["bT"][:, g, :], c["cT"][:, g, :])], [c["bTb"], c["cTb"]], [psSb])
                    SM, SMb = SM_p.next()
                    k.op("dve", _tt(SM[:, :], psS[:, 0:128], Um, ALU.mult), [psSb, prm], [SMb])
                    Es = []
                    for hf in range(2):
                        h0_ = g * 8 + hf * 4
                        dU, dUb = dU_p.next()
                        k.op("dve", _tt(v3(dU[:, :], 4), Um.unsqueeze(1).broadcast_to([128, 4, 128]),
                                        c["dta"][:, h0_:h0_ + 4].unsqueeze(2).broadcast_to([128, 4, 128]), ALU.mult), [prm, c["ddb"]], [dUb])
                        psE, psEb = self.ps.next()
                        k.mm([_mmf(psE[:, :], Lm, dU[:, :])], [prm, dUb], [psEb])
                        E, Eb = E_p.next()
                        k.op("act", _act(E[:, :], psE[:, :], AF.Exp), [psEb], [Eb])
                        Es.append((E, Eb))
                    return dict(SM=SM, SMb=SMb, Es=Es)

                def stage_b(c, g, A):
                    gs = slice(g * 512, (g + 1) * 512)
                    MTs = []
                    for hf in range(2):
                        E, Eb = A["Es"][hf]
                        MT, MTb = MT_p.next()
                        k.op("dve", _tt(v3(MT[:, :], 4), v3(E[:, :], 4), A["SM"][:, :].unsqueeze(1).broadcast_to([128, 4, 128]), ALU.mult), [Eb, A["SMb"]], [MTb])
                        MTs.append((MT, MTb))
                    psY, psYb = self.psacc.next()
                    fns = [_mmf(psY[:, h * 64:(h + 1) * 64], MTs[h // 4][0][:, (h % 4) * 128:(h % 4 + 1) * 128],
                                c["xdt"][:, (g * 8 + h) * 64:(g * 8 + h + 1) * 64]) for h in range(8)]
                    k.mm(fns, [MTs[0][1], MTs[1][1], c["xdtb"]], [psYb])
                    psI, psIb = self.psacc.next()
                    k.mm([_mmf(psI[:, :], c["cT"][:, g, :], stH[:, gs])], [c["cTb"], stHb[g]], [psIb])
                    A.update(psY=psY, psYb=psYb, psI=psI, psIb=psIb)

                def stage_c(c, g, A, d=d):
                    gs = slice(g * 512, (g + 1) * 512)
                    sc, scb = c["sc"], c["scb"]
                    psD, psDb = self.ps.next()
                    k.mm([_mmf(psD[:, :], c["bt"][:, g * 128:(g + 1) * 128], c["xw"][:, gs])], [c["btb"], c["xwb"]], [psDb])
                    yo, yob = y_p.next()
                    k.op("dve", _tt(v3(yo[:, :], 8), v3(A["psI"][:, :], 8), sc[:, g * 8:(g + 1) * 8].unsqueeze(2).broadcast_to([128, 8, 64]), ALU.mult),
                         [A["psIb"], scb], [yob])
                    k.op("dve", _tt(yo[:, :], yo[:, :], A["psY"][:, :], ALU.add), [yob, A["psYb"]], [yob])
                    k.dma("sp", self.ytok[d, c["t0"]:c["t0"] + 128, gs], yo[:, :], reads=[yob])
                    k.op("dve", _tt(v3(stT[:, gs], 8), v3(stT[:, gs], 8), sc[:, 128 + g * 8:128 + (g + 1) * 8].unsqueeze(2).broadcast_to([128, 8, 64]), ALU.mult),
                         [scb, stTb[g]], [stTb[g]])
                    k.op("dve", _tt(stT[:, gs], stT[:, gs], psD[:, :], ALU.add), [stTb[g], psDb], [stTb[g]])
                    k.op("act", _act(stH[:, gs], stT[:, gs], AF.Copy), [stTb[g]], [stHb[g]])

                for si, (s0, SL, _) in enumerate(SEGS):
                    nch = SL // 128
                    order = list(range(nch)) if d == 0 else list(range(nch - 1, -1, -1))
                    if self.cfg.get("ssd_nch"):
                        order = order[:self.cfg["ssd_nch"]]
                    t0s = [s0 + ci * 128 for ci in order]
                    ctxs = {0: issue_loads(t0s[0])}
                    if len(t0s) > 1:
                        ctxs[1] = issue_loads(t0s[1])
                    if si == 0:
                        for half in range(2):
                            k.dma("sp", hst[:, half * 16:(half + 1) * 16, :],
                                  self.ins["state_ssm"][d, half * 2048:(half + 1) * 2048, :].rearrange("(b p) n -> p b n", p=128), writes=[hstb])
                        for r4 in range(8):
                            ps, pb = self.ps.next()
                            k.mm([_tp(ps[:, i * 128:(i + 1) * 128], hst[:, r4 * 4 + i, :], self.ident[:, :]) for i in range(4)], [hstb, self.cb], [pb])
                            k.op("dve", _cp(stT[:, r4 * 512:(r4 + 1) * 512], ps[:, :]), [pb], [stTb[r4]])
                            k.op("act", _act(stH[:, r4 * 512:(r4 + 1) * 512], stT[:, r4 * 512:(r4 + 1) * 512], AF.Copy), [stTb[r4]], [stHb[r4]])
                    else:
                        for g in range(8):
                            k.op("dve", lambda h, g=g: h.memset(stT[:, g * 512:(g + 1) * 512], 0.0), writes=[stTb[g]])
                            k.op("dve", lambda h, g=g: h.memset(stH[:, g * 512:(g + 1) * 512], 0.0), writes=[stHb[g]])
                    prologue(ctxs[0])
                    items = [(vi, g) for vi in range(len(t0s)) for g in range(8)]
                    As = {}
                    for step in range(len(items) + 2):
                        if step < len(items):
                            vi, g = items[step]
                            if g == 4 and vi + 1 < len(t0s):
                                if vi + 2 < len(t0s):
                                    ctxs[vi + 2] = issue_loads(t0s[vi + 2])
                                prologue(ctxs[vi + 1])
                            As[step] = stage_a(ctxs[vi], g)
                        if 0 <= step - 1 < len(items):
                            vi, g = items[step - 1]
                            stage_b(ctxs[vi], g, As[step - 1])
                        if 0 <= step - 2 < len(items):
                            vi, g = items[step - 2]
                            stage_c(ctxs[vi], g, As[step - 2])
                            del As[step - 2]
                    if si > 0:
                        for r4 in range(8):
                            ps, pb = self.ps.next()
                            k.mm([_tp(ps[:, i * 128:(i + 1) * 128], stT[:, (r4 * 4 + i) * 128:(r4 * 4 + i + 1) * 128], self.ident[:, :]) for i in range(4)],
                                 [stTb[r4], self.cb], [pb])
                            o, ob = y_p.next()
                            k.op("dve", _cp(o[:, :], ps[:, :]), [pb], [ob])
                            k.dma("sp", self.outs["nssm"][si - 1, d, r4 * 512:(r4 + 1) * 512, :].rearrange("(b p) n -> p b n", p=128),
                                  o[:, :].rearrange("p (b n) -> p b n", b=4), reads=[ob])
        if self.cfg.get("ssd_stop") == 3:
            return
        with k.phase():
            prm = Buf()
            dsk = k.sb([128, 64], F32, "dsk")
            k.dma("sp", dsk[:, :], self.ins["ssd_d"][jl].partition_broadcast(128), writes=[prm], nc_ok=True)
            nw = k.sb([128, 32], F32, "nw")
            k.dma("sp", nw[:, :], self.ins["ssd_norm"][jl].rearrange("(c p) -> p c", p=128), writes=[prm], nc_ok=True)
            yf_p = k.sbpool(2, [128, 4096], F32, "yf")
            yb_p = k.sbpool(2, [128, 4096], F32, "yb2")
            xt_p = k.sbpool(2, [128, 4096], BF16, "xt2")
            zs_p = k.sbpool(2, [128, 32, 128], BF16, "zs")
            gt_p = k.sbpool(1, [128, 32, 128], F32, "gt")
            yo_p = k.sbpool(2, [128, 32, 128], BF16, "yo2")
            for b_ in range(T // 128):
                t0 = b_ * 128
                yf, yfb = yf_p.next()
                k.dma("sp", yf[:, :], self.ytok[0, t0:t0 + 128, :], writes=[yfb])
                y2, y2b = yb_p.next()
                k.dma("sp", y2[:, :], self.ytok[1, t0:t0 + 128, :], writes=[y2b])
                xt, xtb = xt_p.next()
                k.dma("sp", xt[:, :], self.xtok[t0:t0 + 128, :], writes=[xtb])
                zs, zsb = zs_p.next()
                for half in range(2):
                    k.dma("sp", zs[:, half * 16:(half + 1) * 16, :],
                          self.zsT[half * 2048:(half + 1) * 2048, t0:t0 + 128].rearrange("(c p) t -> p c t", p=128), writes=[zsb])
                k.op("dve", _tt(yf[:, :], yf[:, :], y2[:, :], ALU.add), [yfb, y2b], [yfb])
                k.op("dve", _tt(v3(y2[:, :], 64), v3(xt[:, :], 64), dsk[:, :].unsqueeze(2).broadcast_to([128, 64, 64]), ALU.mult), [xtb, prm], [y2b])
                k.op("dve", _tt(yf[:, :], yf[:, :], y2[:, :], ALU.add), [yfb, y2b], [yfb])
                gt, gtb = gt_p.next()
                psQ, psQb = self.psacc.next()
                for c4 in range(8):
                    ps, pb = self.ps.next()
                    k.mm([_tp(ps[:, i * 128:(i + 1) * 128], yf[:, (c4 * 4 + i) * 128:(c4 * 4 + i + 1) * 128], self.ident[:, :]) for i in range(4)],
                         [yfb, self.cb], [pb])
                    k.op("dve", _tt(gt[:, c4 * 4:(c4 + 1) * 4, :], v3(ps[:, :], 4), zs[:, c4 * 4:(c4 + 1) * 4, :], ALU.mult), [pb, zsb], [gtb])
                    sq, sqb = self.h512.next()
                    k.op("act", _act(v3(sq[:, :], 4), gt[:, c4 * 4:(c4 + 1) * 4, :], AF.Square), [gtb], [sqb])
                    k.mm([_mmf(psQ[:, 0:128], self.ones_h[:, :], sq[:, i * 128:(i + 1) * 128], start=(c4 == 0 and i == 0), stop=(c4 == 7 and i == 3))
                          for i in range(4)], [sqb, self.cb], [psQb])
                rs, rsb = self.rstd_from_sumsq(psQ, psQb, 128, 1.0 / 4096)
                k.op("dve", _tt(gt[:, :, :], gt[:, :, :], nw[:, :].unsqueeze(2).broadcast_to([128, 32, 128]), ALU.mult), [gtb, prm], [gtb])
                yo, yob = yo_p.next()
                k.op("dve", _tt(yo[:, :, :], gt[:, :, :], rs[:, 0:128].unsqueeze(1).broadcast_to([128, 32, 128]), ALU.mult), [gtb, rsb], [yob])
                for half in range(2):
                    k.dma("sp", self.oT[half * 2048:(half + 1) * 2048, t0:t0 + 128].rearrange("(c p) t -> p c t", p=128),
                          yo[:, half * 16:(half + 1) * 16, :], reads=[yob])
        self.out_proj(Wout, 32)


def host_consts():
    ident = np.eye(128, dtype=np.float32)
    rot = np.zeros((128, 128), np.float32)
    for m in range(64):
        rot[m + 64, m] = -1.0
        rot[m, m + 64] = 1.0
    rows = S_LEN // 64
    row = np.repeat(np.arange(rows, dtype=np.float32), 64)
    col = np.tile(np.arange(64, dtype=np.float32), rows)
    inv = (10000.0 ** (-np.arange(32, dtype=np.float32) / 32)).astype(np.float32)
    ang = np.concatenate([row[:, None] * inv, col[:, None] * inv], axis=-1).astype(np.float32)
    cos = np.cos(ang).T.astype(np.float32)
    sin = np.sin(ang).T.astype(np.float32)
    c_cos = np.ascontiguousarray(np.concatenate([cos, cos], 0))
    c_sin = np.ascontiguousarray(np.concatenate([sin, sin], 0))
    kk = np.arange(128)
    tri = np.stack([
        (kk[:, None] > kk[None, :]),
        (kk[:, None] <= kk[None, :]),
        (kk[:, None] < kk[None, :]),
        (kk[:, None] >= kk[None, :]),
    ]).astype(np.float32)
    return {"c_ident": ident, "c_rot": rot, "c_cos": c_cos, "c_sin": c_sin, "c_tri": tri}


def make_in_maps(inputs, cores):
    consts = host_consts()
    maps = []
    shared = {}
    for name in ("ada_w", "ada_b", "norm_mix", "norm_ffn", "attn_w_qkv", "attn_q_norm", "attn_k_norm", "attn_w_o",
                 "ssd_w_in", "ssd_conv_w", "ssd_conv_b", "ssd_d", "ssd_norm", "ssd_w_out",
                 "lru_w_in", "lru_conv_w", "lru_conv_b", "lru_w_a", "lru_b_a", "lru_w_i", "lru_b_i", "lru_a_param", "lru_w_out",
                 "ffn_w_in", "ffn_w_out", "final_norm"):
        shared[name] = np.ascontiguousarray(inputs[name], dtype=np.float32)
    shared["ssd_dt_bias"] = np.ascontiguousarray(inputs["ssd_dt_bias"], dtype=np.float32).reshape(1, 128)
    shared["ssd_a_log"] = np.ascontiguousarray(inputs["ssd_a_log"], dtype=np.float32).reshape(1, 128)
    shared.update(consts)
    for i in cores:
        m = dict(shared)
        m["x_all"] = np.ascontiguousarray(np.concatenate(
            [inputs["x_sample"][i], inputs["x_prompt"][2 * i], inputs["x_prompt"][2 * i + 1]], axis=0), dtype=np.float32)
        m["cache_k"] = np.ascontiguousarray(inputs["cache_k"][i].reshape(2, PAST, 512), dtype=np.float32)
        m["cache_v"] = np.ascontiguousarray(inputs["cache_v"][i].reshape(2, PAST, 512), dtype=np.float32)
        m["state_ssm"] = np.ascontiguousarray(inputs["state_ssm"][i, 0].reshape(2, 4096, 128), dtype=np.float32)
        m["state_lru"] = np.ascontiguousarray(inputs["state_lru"][i, 0], dtype=np.float32)
        m["cond"] = np.ascontiguousarray(np.stack([inputs["c"][i], inputs["c_ctx"]], 0), dtype=np.float32)
        maps.append(m)
    return maps


def run(inputs, cores, cfg):
    net = Net(cfg)
    nc = net.build()
    maps = make_in_maps(inputs, cores)
    res = run_bass_kernel_spmd(nc, maps, core_ids=list(range(len(cores))))
    return res


def kernel(**inputs):
    inputs = {k: np.asarray(v) for k, v in inputs.items()}
    cores = list(range(8))
    res = run(inputs, cores, {})
    B, BS = 16, 8
    y_prompt = np.zeros((B, P_LEN, D), np.float32)
    y_sample = np.zeros((BS, S_LEN, D), np.float32)
    nk = np.zeros((B, 2, P_LEN, 4, 128), np.float32)
    nv = np.zeros((B, 2, P_LEN, 4, 128), np.float32)
    nssm = np.zeros((B, 1, 2, 64, 64, 128), np.float32)
    nlru = np.zeros((B, 1, 2, D), np.float32)
    for i, r in enumerate(res.results):
        ya = r["y_all"]
        y_sample[i] = ya[:S_LEN]
        y_prompt[2 * i] = ya[S_LEN:S_LEN + P_LEN]
        y_prompt[2 * i + 1] = ya[S_LEN + P_LEN:]
        nk[2 * i:2 * i + 2] = r["nk"].reshape(2, 2, P_LEN, 4, 128)
        nv[2 * i:2 * i + 2] = r["nv"].reshape(2, 2, P_LEN, 4, 128)
        nssm[2 * i:2 * i + 2, 0] = r["nssm"].reshape(2, 2, 64, 64, 128)
        nlru[2 * i:2 * i + 2, 0] = r["nlru"].reshape(2, 2, D)
    return (y_prompt, y_sample, nk, nv, nssm, nlru)
```

```python
import numpy as np
import ml_dtypes
import concourse.bass as bass
import concourse.mybir as mybir
from concourse.bass_utils import run_bass_kernel_spmd

F32 = mybir.dt.float32
BF16 = mybir.dt.bfloat16
ALU = mybir.AluOpType
AF = mybir.ActivationFunctionType

D = 2048
DC = 16
S_LEN = 4096
P_LEN = 256
T = S_LEN + 2 * P_LEN
NBLK = T // 512
DEPTH = 4
DFF = 5632
EPS = 1e-6
PAST = 512
TILES = [(0, 1024), (1024, 1024), (2048, 1024), (3072, 1024), (4096, 512)]
SEGS = [(0, S_LEN, 0), (S_LEN, P_LEN, 1), (S_LEN + P_LEN, P_LEN, 1)]
PADT = T + 7
SEGQ = [(0, S_LEN, 1), (S_LEN, P_LEN, S_LEN + 3), (S_LEN + P_LEN, P_LEN, S_LEN + P_LEN + 5)]


class Buf:
    __slots__ = ("w", "r")

    def __init__(self):
        self.w = None
        self.r = {}


class Eng:
    def __init__(self, name):
        self.name = name
        self.sem = None
        self.count = 0
        self.waited = {}
        self.rec = []


class Pool:
    def __init__(self, items):
        self.items = items
        self.i = 0

    def next(self):
        it = self.items[self.i % len(self.items)]
        self.i += 1
        return it


class Phase:
    def __init__(self, k):
        self.k = k
        self.cms = []

    def __enter__(self):
        assert self.k.cur_phase is None
        self.k.cur_phase = self
        return self

    def __exit__(self, *a):
        self.k.barrier()
        for cm in reversed(self.cms):
            cm.__exit__(None, None, None)
        self.k.cur_phase = None
        return False


class K:
    def __init__(self, dbg=None):
        self.nc = bass.Bass("TRN2", target_bir_lowering=False)
        self.E = {n: Eng(n) for n in ("pe", "act", "dve", "pool", "sp")}
        self.semof = {}
        self.sem_ctx = []
        for n, e in self.E.items():
            e.sem = self._sem("e_" + n)
            self.semof[n] = e.sem
        self.dslots = {}
        for q, n in (("sp", 44), ("pool", 28)):
            sl = []
            for i in range(n):
                key = "d%s%d" % (q, i)
                s = self._sem(key)
                self.semof[key] = s
                sl.append([key, s, 0])
            self.dslots[q] = Pool(sl)
        self.nsb = 0
        self.bar_sem = self._sem("bar")
        self.bar_count = 0
        self.cur_phase = None

    def _sem(self, name):
        cm = self.nc.semaphore(name)
        s = cm.__enter__()
        self.sem_ctx.append(cm)
        return s

    def sb(self, shape, dt, name=None):
        self.nsb += 1
        name = (name or "sb") + "_%d" % self.nsb
        if self.cur_phase is not None:
            cm = self.nc.sbuf_tensor(name, list(shape), dt)
            t = cm.__enter__()
            self.cur_phase.cms.append(cm)
            return t
        return self.nc.alloc_sbuf_tensor(name, list(shape), dt)

    def sbpool(self, n, shape, dt, name):
        return Pool([(self.sb(shape, dt, "%s%d" % (name, i)), Buf()) for i in range(n)])

    def dram(self, name, shape, dt):
        return self.nc.dram_tensor(name, list(shape), dt).ap()

    def _waits(self, e, reads, writes):
        need = {}
        for b in reads:
            w = b.w
            if w is not None and need.get(w[0], 0) < w[1]:
                need[w[0]] = w[1]
        for b in writes:
            w = b.w
            if w is not None and need.get(w[0], 0) < w[1]:
                need[w[0]] = w[1]
            for k, v in b.r.items():
                if need.get(k, 0) < v:
                    need[k] = v
        for k, v in need.items():
            if k == "pe" and e.name == "pe":
                continue
            if e.waited.get(k, 0) >= v:
                continue
            e.waited[k] = v
            sem = self.semof[k]
            e.rec.append(lambda h, sem=sem, v=v: h.wait_ge(sem, v))

    def _mark(self, ev, reads, writes):
        k, v = ev
        for b in reads:
            if b.r.get(k, 0) < v:
                b.r[k] = v
        for b in writes:
            b.w = ev
            b.r = {}

    def op(self, en, fn, reads=(), writes=()):
        e = self.E[en]
        self._waits(e, reads, writes)
        e.count += 1
        sem = e.sem
        e.rec.append(lambda h, fn=fn, sem=sem: fn(h).then_inc(sem, 1))
        self._mark((en, e.count), reads, writes)

    def mm(self, fns, reads, writes):
        e = self.E["pe"]
        self._waits(e, reads, writes)
        for fn in fns[:-1]:
            e.rec.append(fn)
        e.count += 1
        sem = e.sem
        fn = fns[-1]
        e.rec.append(lambda h, fn=fn, sem=sem: fn(h).then_inc(sem, 1))
        self._mark(("pe", e.count), reads, writes)

    def dma(self, q, out, in_, reads=(), writes=(), nc_ok=False):
        e = self.E[q]
        self._waits(e, reads, writes)
        slot = self.dslots[q].next()
        key, sem, val = slot
        if val > 0 and e.waited.get(key, 0) < val:
            e.waited[key] = val
            e.rec.append(lambda h, sem=sem, v=val: h.wait_ge(sem, v))
        slot[2] = val + 16
        if nc_ok:
            e.rec.append(lambda h, out=out, in_=in_, sem=sem: h.dma_start(out=out, in_=in_, allow_slow_non_contiguous=True).then_inc(sem, 16))
        else:
            e.rec.append(lambda h, out=out, in_=in_, sem=sem: h.dma_start(out=out, in_=in_).then_inc(sem, 16))
        self._mark((key, val + 16), reads, writes)

    def barrier(self):
        sp = self.E["sp"]
        allk = {}
        for q in ("sp", "pool"):
            for key, sem, val in self.dslots[q].items:
                if val > 0:
                    allk[key] = val
        for n in ("pe", "act", "dve", "pool"):
            if self.E[n].count > 0:
                allk[n] = self.E[n].count
        for key, val in allk.items():
            if sp.waited.get(key, 0) < val:
                sp.waited[key] = val
                sp.rec.append(lambda h, sem=self.semof[key], v=val: h.wait_ge(sem, v))
        self.bar_count += 1
        bc = self.bar_count
        sp.rec.append(lambda h, sem=self.bar_sem: h.sem_inc(sem, 1))
        for n in ("pe", "act", "dve", "pool"):
            e = self.E[n]
            e.rec.append(lambda h, sem=self.bar_sem, v=bc: h.wait_ge(sem, v))
            for key, val in allk.items():
                if e.waited.get(key, 0) < val:
                    e.waited[key] = val

    def phase(self):
        return Phase(self)

    def finish(self):
        e = self.E["sp"]
        for q in ("sp", "pool"):
            for key, sem, val in self.dslots[q].items:
                if val > 0:
                    e.rec.append(lambda h, sem=sem, v=val: h.wait_ge(sem, v))
        for n in ("pe", "act", "dve"):
            en = self.E[n]
            if en.count > 0:
                e.rec.append(lambda h, sem=en.sem, v=en.count: h.wait_ge(sem, v))
        nc = self.nc
        E = self.E
        with nc.Block() as block:
            @block.tensor
            def _(h):
                for f in E["pe"].rec:
                    f(h)

            @block.scalar
            def _(h):
                for f in E["act"].rec:
                    f(h)

            @block.vector
            def _(h):
                for f in E["dve"].rec:
                    f(h)

            @block.gpsimd
            def _(h):
                for f in E["pool"].rec:
                    f(h)

            @block.sync
            def _(h):
                for f in E["sp"].rec:
                    f(h)
        return nc


def _act(out, in_, func, **kw):
    return lambda h: h.activation(out=out, in_=in_, func=func, **kw)


def _tt(out, a, b, op):
    return lambda h: h.tensor_tensor(out=out, in0=a, in1=b, op=op)


def _stt(out, in0, scalar, in1, op0, op1):
    return lambda h: h.scalar_tensor_tensor(out=out, in0=in0, scalar=scalar, in1=in1, op0=op0, op1=op1)


def _ts(out, in0, s1, s2, op0, op1=None):
    if op1 is None:
        return lambda h: h.tensor_scalar(out=out, in0=in0, scalar1=s1, scalar2=None, op0=op0)
    return lambda h: h.tensor_scalar(out=out, in0=in0, scalar1=s1, scalar2=s2, op0=op0, op1=op1)


def _cp(out, in_):
    return lambda h: h.tensor_copy(out=out, in_=in_)


def _mmf(out, lhsT, rhs, start=True, stop=True):
    return lambda h: h.matmul(out, lhsT=lhsT, rhs=rhs, start=start, stop=stop)


def _tp(out, in_, ident):
    return lambda h: h.transpose(out, in_, ident)


class Net:
    def __init__(self, cfg):
        self.cfg = cfg
        k = self.k = K()
        nc = k.nc
        self.ins = {}
        self.outs = {}

        def inp(name, shape, dt=F32):
            self.ins[name] = nc.dram_tensor(name, list(shape), dt, kind="ExternalInput").ap()
            return self.ins[name]

        def outp(name, shape):
            self.outs[name] = nc.dram_tensor(name, list(shape), F32, kind="ExternalOutput").ap()
            return self.outs[name]

        inp("x_all", [T, D])
        inp("cache_k", [2, PAST, 512]); inp("cache_v", [2, PAST, 512])
        inp("state_ssm", [2, 4096, 128]); inp("state_lru", [2, D])
        inp("cond", [2, D])
        inp("ada_w", [DEPTH, D, 6 * D]); inp("ada_b", [DEPTH, 6 * D])
        inp("norm_mix", [DEPTH, D]); inp("norm_ffn", [DEPTH, D])
        inp("attn_w_qkv", [2, D, 3072]); inp("attn_q_norm", [2, 128]); inp("attn_k_norm", [2, 128]); inp("attn_w_o", [2, D, D])
        inp("ssd_w_in", [1, D, 10368]); inp("ssd_conv_w", [1, 4, 6144]); inp("ssd_conv_b", [1, 6144])
        inp("ssd_dt_bias", [1, 128]); inp("ssd_a_log", [1, 128]); inp("ssd_d", [1, 64]); inp("ssd_norm", [1, 4096])
        inp("ssd_w_out", [1, 4096, D])
        inp("lru_w_in", [1, D, 4096]); inp("lru_conv_w", [1, 4, D]); inp("lru_conv_b", [1, D])
        inp("lru_w_a", [1, 2, 16, 128, 128]); inp("lru_b_a", [1, 2, D]); inp("lru_w_i", [1, 2, 16, 128, 128]); inp("lru_b_i", [1, 2, D])
        inp("lru_a_param", [1, 2, D]); inp("lru_w_out", [1, D, D])
        inp("ffn_w_in", [DEPTH, D, 2 * DFF]); inp("ffn_w_out", [DEPTH, DFF, D])
        inp("final_norm", [D])
        inp("c_ident", [128, 128]); inp("c_rot", [128, 128]); inp("c_cos", [128, S_LEN]); inp("c_sin", [128, S_LEN])
        inp("c_tri", [4, 128, 128])
        outp("y_all", [T, D])
        outp("nk", [2, 2, P_LEN, 512]); outp("nv", [2, 2, P_LEN, 512])
        outp("nssm", [2, 2, 4096, 128]); outp("nlru", [2, 2, D])
        if cfg.get("dbg"):
            outp("dbg_xT", [D, T])

        self.xT = k.dram("xT", [D, T], F32)
        self.xT_b = [[Buf() for _ in range(NBLK)] for _ in range(DC)]
        self.qT = k.dram("qT", [D, T], BF16)
        self.kT = k.dram("kT", [512, T], BF16)
        self.vtok = k.dram("vtok", [T, 512], BF16)
        self.oT = k.dram("oT", [4096, T], BF16)
        self.xbcT = k.dram("xbcT", [6144, T], F32)
        self.zsT = k.dram("zsT", [4096, T], BF16)
        self.dtk = k.dram("dtk", [T, 2, 128], F32)
        self.xtok = k.dram("xtok", [T, 4096], BF16)
        self.btok = k.dram("btok", [T, 1024], BF16)
        self.bcT = k.dram("bcT", [2048, T], BF16)
        self.ytok = k.dram("ytok", [2, T, 4096], F32)
        self.ident = k.sb([128, 128], F32, "ident"); self.ident_b = Buf()
        self.identh = k.sb([128, 128], BF16, "identh")
        self.ones_h = k.sb([128, 128], BF16, "ones_h")
        self.ones_f = k.sb([128, 128], F32, "ones_f")
        self.rot_h = k.sb([128, 128], BF16, "rot_h")
        self.cb = Buf()
        k.dma("sp", self.ident[:, :], self.ins["c_ident"][:, :], writes=[self.cb])
        k.dma("pool", self.identh[:, :], self.ins["c_ident"][:, :], writes=[self.cb])
        k.dma("pool", self.rot_h[:, :], self.ins["c_rot"][:, :], writes=[self.cb])
        k.op("dve", lambda h: h.memset(self.ones_h[:, :], 1.0), writes=[self.cb])
        k.op("dve", lambda h: h.memset(self.ones_f[:, :], 1.0), writes=[self.cb])
        self.epsb = k.sb([128, 1], F32, "epsb")
        k.op("dve", lambda h: h.memset(self.epsb[:, :], EPS), writes=[self.cb])
        self.ps = Pool([(nc.alloc_psum_tensor("ps%d" % i, [128, 512], F32), Buf()) for i in range(4)])
        self.psacc = Pool([(nc.alloc_psum_tensor("pa%d" % i, [128, 512], F32), Buf()) for i in range(4)])
        self.zcol = k.sb([128, 1], F32, "zcol")
        k.op("dve", lambda h: h.memset(self.zcol[:, :], 0.0), writes=[self.cb])
        self.t512 = k.sbpool(6, [128, 512], F32, "t512")
        self.rsp = k.sbpool(3, [128, 512], F32, "rsp")
        self.sgp = k.sbpool(3, [128, 512], F32, "sgp")
        self.h512 = k.sbpool(8, [128, 512], BF16, "h512")
        self.xpc = k.sbpool(3, [128, 512], F32, "xpc")
        self.xo = k.sbpool(3, [128, 512], F32, "xo")
        self.modT = k.sb([128, 96, 2], F32, "modT"); self.mod_b = Buf()
        self.gmix = k.sb([128, DEPTH, DC], F32, "gmix")
        self.gffn = k.sb([128, DEPTH, DC], F32, "gffn")
        self.gfin = k.sb([128, DC], F32, "gfin")
        self.AB = k.sb([128, 2, 2, 2, DC], F32, "AB"); self.AB_b = Buf()
        self.gate = k.sb([128, 2, 2, DC], F32, "gate")
        self.adabT = k.sb([128, DEPTH, 96], F32, "adabT")
        k.dma("sp", self.gmix[:, :, :], self.ins["norm_mix"].rearrange("l (c p) -> p l c", p=128), writes=[self.cb], nc_ok=True)
        k.dma("sp", self.gffn[:, :, :], self.ins["norm_ffn"].rearrange("l (c p) -> p l c", p=128), writes=[self.cb], nc_ok=True)
        k.dma("sp", self.gfin[:, :], self.ins["final_norm"].rearrange("(c p) -> p c", p=128), writes=[self.cb], nc_ok=True)
        k.dma("sp", self.adabT[:, :, :], self.ins["ada_b"].rearrange("l (j p) -> p l j", p=128), writes=[self.cb], nc_ok=True)
        self.condT = k.sb([128, DC, 2], F32, "condT")
        self.condh = k.sb([128, DC, 2], BF16, "condh")
        for r in range(2):
            k.dma("sp", self.condT[:, :, r], self.ins["cond"][r].rearrange("(c p) -> p c", p=128), writes=[self.cb], nc_ok=True)
        k.op("act", _act(self.condh[:, :, :], self.condT[:, :, :], AF.Silu), [self.cb], [self.cb])

    def alloc_gemm(self, nw=3, hT=True):
        k = self.k
        self.wpool = k.sbpool(nw, [128, 8192], BF16, "w")
        if hT:
            self.hTf = k.sb([128, 16384], BF16, "hT"); self.hT_b = Buf()
            self.hT = self.hTf[:, :].rearrange("p (k t) -> p k t", k=DC)
            self.hT32 = self.hTf[:, :].rearrange("p (k t) -> p k t", k=32)

    def xbufs(self, t0, n):
        b0 = t0 // 512
        return [self.xT_b[c][b0 + i] for c in range(DC) for i in range(n // 512)]

    def load_x_T(self):
        k = self.k
        with k.phase():
            stage = k.sbpool(2, [128, D], F32, "p0in")
            for tb in range(T // 128):
                st, sbuf = stage.next()
                k.dma("sp", st[:, :], self.ins["x_all"][tb * 128:(tb + 1) * 128, :], writes=[sbuf])
                for cg in range(4):
                    ps, pb = self.ps.next()
                    fns = [_tp(ps[:, j * 128:(j + 1) * 128], st[:, (cg * 4 + j) * 128:(cg * 4 + j + 1) * 128], self.ident[:, :]) for j in range(4)]
                    k.mm(fns, [sbuf, self.cb], [pb])
                    o, ob = self.xo.next()
                    if cg % 2 == 0:
                        k.op("act", _act(o[:, :], ps[:, :], AF.Copy), [pb], [ob])
                    else:
                        k.op("dve", _cp(o[:, :], ps[:, :]), [pb], [ob])
                    dst = self.xT[cg * 512:(cg + 1) * 512, tb * 128:(tb + 1) * 128].rearrange("(c p) t -> p c t", p=128)
                    wb = [self.xT_b[cg * 4 + j][tb // 4] for j in range(4)]
                    k.dma("sp", dst, o[:, :].rearrange("p (c t) -> p c t", c=4), reads=[ob], writes=wb)

    def rstd_from_sumsq(self, ps, pb, n, inv_n):
        k = self.k
        sd, sdb = self.t512.next()
        k.op("act", _act(sd[:, :n], ps[:, :n], AF.Sqrt, scale=inv_n, bias=self.epsb[:, 0:1]), [pb, self.cb], [sdb])
        rs, rsb = self.rsp.next()
        k.op("dve", lambda h: h.reciprocal(out=rs[:, :n], in_=sd[:, :n]), [sdb], [rsb])
        return rs, rsb

    def load_stats(self, tt0):
        k = self.k
        xi, xib = self.xin.next()
        src = self.xT[:, tt0:tt0 + 512].rearrange("(c p) t -> p c t", p=128)
        k.dma("sp", xi[:, :, :], src, reads=self.xbufs(tt0, 512), writes=[xib])
        ps, pb = self.ps.next()
        for c in range(DC):
            sq, sqb = self.h512.next()
            k.op("act", _act(sq[:, :], xi[:, c, :], AF.Square), [xib], [sqb])
            k.mm([_mmf(ps[:, :], self.ones_h[:, :], sq[:, :], start=(c == 0), stop=(c == DC - 1))], [sqb, self.cb], [pb])
        rs, rsb = self.rstd_from_sumsq(ps, pb, 512, 1.0 / D)
        return xi, xib, rs, rsb

    def norm_mod(self, t0, TT, which, r):
        k = self.k
        for blk in range(TT // 512):
            xi, xib, rs, rsb = self.load_stats(t0 + blk * 512)
            for c in range(DC):
                tm, tmb = self.t512.next()
                k.op("dve", _stt(tm[:, :], xi[:, c, :], self.AB[:, which, r, 0, c:c + 1], rs[:, :], ALU.mult, ALU.mult),
                     [xib, rsb, self.AB_b], [tmb])
                k.op("act", _act(self.hT[:, c, blk * 512:(blk + 1) * 512], tm[:, :], AF.Identity, bias=self.AB[:, which, r, 1, c:c + 1], scale=1.0),
                     [tmb, self.AB_b], [self.hT_b])

    def gemm_fm(self, act, act_b, KC, TT, W, groups, cb):
        k = self.k
        Wv = W.rearrange("(k p) n -> p k n", p=128)
        pending = []
        for gi, grp in enumerate(groups):
            G = len(grp)
            assert KC * G * 128 <= 8192
            wt, wb = self.wpool.next()
            wv = wt[:, 0:KC * G * 128].rearrange("p (k n) -> p k n", k=KC)
            s = 0
            while s < G:
                e = s
                while e + 1 < G and grp[e + 1] == grp[e] + 128:
                    e += 1
                n = (e - s + 1) * 128
                k.dma("pool", wv[:, :, s * 128:s * 128 + n], Wv[:, :, grp[s]:grp[s] + n], writes=[wb])
                s = e + 1
            for blk in range(TT // 512):
                for s, col in enumerate(grp):
                    ps, pb = self.ps.next()
                    fns = [_mmf(ps[:, :], wv[:, kk, s * 128:(s + 1) * 128], act[:, kk, blk * 512:(blk + 1) * 512],
                                start=(kk == 0), stop=(kk == KC - 1)) for kk in range(KC)]
                    k.mm(fns, [wb, act_b], [pb])
                    for gen in list(pending):
                        try:
                            next(gen)
                        except StopIteration:
                            pending.remove(gen)
                    r = cb(gi, s, col, blk, ps, pb)
                    if r is not None:
                        try:
                            next(r)
                            pending.append(r)
                        except StopIteration:
                            pass
        while pending:
            for gen in list(pending):
                try:
                    next(gen)
                except StopIteration:
                    pending.remove(gen)

    def resid_cb(self, t0, gate_ap_fn):
        k = self.k

        def cb(gi, s, col, blk, ps, pb):
            c = col // 128
            tt0 = t0 + blk * 512
            xb = self.xT_b[c][tt0 // 512]
            xp, xpb = self.xpc.next()
            k.dma("sp", xp[:, :], self.xT[c * 128:(c + 1) * 128, tt0:tt0 + 512], reads=[xb], writes=[xpb])
            o, ob = self.xo.next()
            k.op("dve", _stt(o[:, :], ps[:, :], gate_ap_fn(c), xp[:, :], ALU.mult, ALU.add), [pb, xpb, self.AB_b], [ob])
            k.dma("sp", self.xT[c * 128:(c + 1) * 128, tt0:tt0 + 512], o[:, :], reads=[ob], writes=[xb])
        return cb

    def adaln(self, l):
        k = self.k
        Wv = self.ins["ada_w"][l].rearrange("(k p) n -> p k n", p=128)
        psT, psTb = self.psacc.next()
        ph = k.phase(); ph.__enter__()
        self.alloc_gemm(hT=False)
        for cb_ in range(24):
            wt, wb = self.wpool.next()
            wv = wt[:, :].rearrange("p (k n) -> p k n", k=DC)
            k.dma("pool", wv[:, :, :], Wv[:, :, cb_ * 512:(cb_ + 1) * 512], writes=[wb])
            ps, pb = self.ps.next()
            fns = [_mmf(ps[0:2, :], self.condh[:, kk, :], wv[:, kk, :], start=(kk == 0), stop=(kk == DC - 1)) for kk in range(DC)]
            k.mm(fns, [wb, self.cb], [pb])
            row, rowb = self.t512.next()
            k.op("dve", _cp(row[0:2, :], ps[0:2, :]), [pb], [rowb])
            fns = [_tp(psT[:, (cb_ * 4 + j) * 2:(cb_ * 4 + j + 1) * 2], row[0:2, j * 128:(j + 1) * 128], self.ident[0:2, 0:2]) for j in range(4)]
            k.mm(fns, [rowb, self.cb], [psTb])
        for r in range(2):
            k.op("dve", _tt(self.modT[:, :, r], psT[:, 0:192].rearrange("p (j r) -> p j r", r=2)[:, :, r], self.adabT[:, l, :], ALU.add),
                 [psTb, self.cb], [self.mod_b, self.AB_b])
        for which, (gt, o_sh, o_sc, o_g) in enumerate(((self.gmix, 0, 16, 32), (self.gffn, 48, 64, 80))):
            for r in range(2):
                k.op("dve", _stt(self.AB[:, which, r, 0, :], self.modT[:, o_sc:o_sc + 16, r], 1.0, gt[:, l, :], ALU.add, ALU.mult),
                     [self.mod_b, self.cb], [self.AB_b])
                k.op("dve", _cp(self.AB[:, which, r, 1, :], self.modT[:, o_sh:o_sh + 16, r]), [self.mod_b], [self.AB_b])
                k.op("dve", _cp(self.gate[:, which, r, :], self.modT[:, o_g:o_g + 16, r]), [self.mod_b], [self.AB_b])
        ph.__exit__(None, None, None)

    def ffn(self, l):
        k = self.k
        Win = self.ins["ffn_w_in"][l]
        Wout = self.ins["ffn_w_out"][l]
        with k.phase():
            self.alloc_gemm()
            self.xin = k.sbpool(1, [128, DC, 512], F32, "xin")
            big = k.sb([128, 22 * 1024], BF16, "big")
            big_b = Buf()
            for (t0, TT) in TILES:
                r = 1 if t0 >= S_LEN else 0
                self.norm_mod(t0, TT, 1, r)
                if self.cfg.get("dbg_hT"):
                    for c in range(DC):
                        tm, tmb = self.t512.next()
                        k.op("dve", _cp(tm[:, :], self.hT[:, c, 0:512]), [self.hT_b], [tmb])
                        k.dma("sp", self.outs["dbg_xT"][c * 128:(c + 1) * 128, t0:t0 + 512], tm[:, :], reads=[tmb])
                    continue
                hid = big[:, 0:22 * TT].rearrange("p (j t) -> p j t", j=22)
                for half in range(2):
                    state = {}

                    def cb_in(gi, s, col, blk, ps, pb, state=state, hid=hid):
                        if s == 0:
                            sg, sgb = self.sgp.next()
                            k.op("act", _act(sg[:, :], ps[:, :], AF.Silu), [pb], [sgb])
                            state["sg"] = (sg, sgb)
                        else:
                            sg, sgb = state["sg"]
                            k.op("dve", _tt(hid[:, gi, blk * 512:(blk + 1) * 512], ps[:, :], sg[:, :], ALU.mult), [pb, sgb], [big_b])
                    groups = [[(half * 22 + j) * 128, DFF + (half * 22 + j) * 128] for j in range(22)]
                    self.gemm_fm(self.hT, self.hT_b, DC, TT, Win, groups, cb_in)
                    Wo = Wout[half * 2816:(half + 1) * 2816, :]
                    groups = [[c * 128, (c + 1) * 128] for c in range(0, DC, 2)]
                    self.gemm_fm(hid, big_b, 22, TT, Wo, groups, self.resid_cb(t0, lambda c, r=r: self.gate[:, 1, r, c:c + 1]))

    def final(self):
        k = self.k
        with k.phase():
            self.xin = k.sbpool(1, [128, DC, 512], F32, "xin")
            p0out = k.sbpool(2, [128, D], F32, "p0out")
            for blk in range(NBLK):
                tt0 = blk * 512
                xi, xib, rs, rsb = self.load_stats(tt0)
                for c in range(DC):
                    k.op("dve", _stt(xi[:, c, :], xi[:, c, :], self.gfin[:, c:c + 1], rs[:, :], ALU.mult, ALU.mult), [xib, rsb, self.cb], [xib])
                for tb in range(4):
                    o, ob = p0out.next()
                    for cg in range(4):
                        ps, pb = self.ps.next()
                        fns = [_tp(ps[:, j * 128:(j + 1) * 128], xi[:, cg * 4 + j, tb * 128:(tb + 1) * 128], self.ident[:, :]) for j in range(4)]
                        k.mm(fns, [xib, self.cb], [pb])
                        if cg % 2 == 0:
                            k.op("act", _act(o[:, cg * 512:(cg + 1) * 512], ps[:, :], AF.Copy), [pb], [ob])
                        else:
                            k.op("dve", _cp(o[:, cg * 512:(cg + 1) * 512], ps[:, :]), [pb], [ob])
                    k.dma("sp", self.outs["y_all"][tt0 + tb * 128:tt0 + (tb + 1) * 128, :], o[:, :], reads=[ob])

    def dbg_dump(self):
        k = self.k
        with k.phase():
            self.xin = k.sbpool(1, [128, DC, 512], F32, "xin")
            for blk in range(NBLK):
                tt0 = blk * 512
                xi, xib = self.xin.next()
                k.dma("sp", xi[:, :, :], self.xT[:, tt0:tt0 + 512].rearrange("(c p) t -> p c t", p=128), reads=self.xbufs(tt0, 512), writes=[xib])
                k.dma("sp", self.outs["dbg_xT"][:, tt0:tt0 + 512].rearrange("(c p) t -> p c t", p=128), xi[:, :, :], reads=[xib])

    def build(self):
        cfg = self.cfg
        k = self.k
        self.load_x_T()
        for l in cfg.get("layers", range(DEPTH)):
            self.adaln(l)
            if cfg.get("mixers", True):
                kind = l % 3
                if kind == 0:
                    self.attention(l)
                elif kind == 1:
                    self.ssd(l)
                else:
                    self.lru(l)
            if cfg.get("ffn", True):
                self.ffn(l)
        if cfg.get("dbg") and not cfg.get("dbg_hT"):
            self.dbg_dump()
        self.final()
        return k.finish()

    def attention(self, l):
        k = self.k
        jl = l // 3
        Wqkv = self.ins["attn_w_qkv"][jl]
        Wo = self.ins["attn_w_o"][jl]
        nk_o, nv_o = self.outs["nk"], self.outs["nv"]
        with k.phase():
            self.alloc_gemm()
            self.xin = k.sbpool(1, [128, DC, 512], F32, "xin")
            cosT = k.sb([128, S_LEN], F32, "cosT")
            sinT = k.sb([128, S_LEN], F32, "sinT")
            qkg = k.sb([128, 2], F32, "qkg")
            tb_ = Buf()
            k.dma("sp", cosT[:, :], self.ins["c_cos"][:, :], writes=[tb_])
            k.dma("sp", sinT[:, :], self.ins["c_sin"][:, :], writes=[tb_])
            k.dma("sp", qkg[:, 0:1], self.ins["attn_q_norm"][jl].rearrange("(p o) -> p o", o=1), writes=[tb_], nc_ok=True)
            k.dma("sp", qkg[:, 1:2], self.ins["attn_k_norm"][jl].rearrange("(p o) -> p o", o=1), writes=[tb_], nc_ok=True)
            qn_p = k.sbpool(2, [128, 512], F32, "qn")
            for (t0, TT) in self.cfg.get("tiles", TILES):
                prompt = t0 >= S_LEN
                r = 1 if prompt else 0
                self.norm_mod(t0, TT, 0, r)

                def cb(gi, s, col, blk, ps, pb, t0=t0, prompt=prompt):
                    j = col // 128
                    tt0 = t0 + blk * 512
                    if j < 20:
                        sq, sqb = self.h512.next()
                        k.op("act", _act(sq[:, :], ps[:, :], AF.Square), [pb], [sqb])
                        yield
                        ps2, pb2 = self.psacc.next()
                        k.mm([_mmf(ps2[:, :], self.ones_h[:, :], sq[:, :])], [sqb, self.cb], [pb2])
                        rs, rsb = self.rstd_from_sumsq(ps2, pb2, 512, 1.0 / 128)
                        qn, qnb = qn_p.next()
                        gi_ = 0 if j < 16 else 1
                        k.op("dve", _stt(qn[:, :], ps[:, :], qkg[:, gi_:gi_ + 1], rs[:, :], ALU.mult, ALU.mult), [pb, rsb, tb_], [qnb])
                        o, ob = self.h512.next()
                        if not prompt:
                            qh, qhb = self.h512.next()
                            k.op("act", _act(qh[:, :], qn[:, :], AF.Copy), [qnb], [qhb])
                            yield
                            ps3, pb3 = self.psacc.next()
                            k.mm([_mmf(ps3[:, :], self.rot_h[:, :], qh[:, :])], [qhb, self.cb], [pb3])
                            t1, t1b = self.t512.next()
                            k.op("dve", _tt(t1[:, :], qn[:, :], cosT[:, tt0:tt0 + 512], ALU.mult), [qnb, tb_], [t1b])
                            t2, t2b = self.t512.next()
                            k.op("dve", _tt(t2[:, :], ps3[:, :], sinT[:, tt0:tt0 + 512], ALU.mult), [pb3, tb_], [t2b])
                            k.op("dve", _tt(o[:, :], t1[:, :], t2[:, :], ALU.add), [t1b, t2b], [ob])
                        else:
                            k.op("act", _act(o[:, :], qn[:, :], AF.Copy), [qnb], [ob])
                            if j >= 16:
                                hh = j - 16
                                ps4, pb4 = self.psacc.next()
                                k.mm([_tp(ps4[:, i * 128:(i + 1) * 128], qn[:, i * 128:(i + 1) * 128], self.ident[:, :]) for i in range(4)], [qnb, self.cb], [pb4])
                                kf, kfb = self.t512.next()
                                k.op("act", _act(kf[:, :], ps4[:, :], AF.Copy), [pb4], [kfb])
                                for sq_ in range(2):
                                    k.dma("sp", nk_o[sq_, jl, :, hh * 128:(hh + 1) * 128].rearrange("(b p) f -> p b f", p=128),
                                          kf[:, sq_ * 256:(sq_ + 1) * 256].rearrange("p (b f) -> p b f", b=2), reads=[kfb])
                        dst = self.qT[j * 128:(j + 1) * 128, tt0:tt0 + 512] if j < 16 else self.kT[(j - 16) * 128:(j - 15) * 128, tt0:tt0 + 512]
                        k.dma("sp", dst, o[:, :], reads=[ob])
                    else:
                        hh = j - 20
                        vt, vtb = self.t512.next()
                        k.op("act", _act(vt[:, :], ps[:, :], AF.Copy), [pb], [vtb])
                        ps4, pb4 = self.ps.next()
                        k.mm([_tp(ps4[:, i * 128:(i + 1) * 128], vt[:, i * 128:(i + 1) * 128], self.ident[:, :]) for i in range(4)], [vtb, self.cb], [pb4])
                        vb, vbb = self.h512.next()
                        k.op("dve", _cp(vb[:, :], ps4[:, :]), [pb4], [vbb])
                        k.dma("sp", self.vtok[tt0:tt0 + 512, hh * 128:(hh + 1) * 128].rearrange("(b p) f -> p b f", p=128),
                              vb[:, :].rearrange("p (b f) -> p b f", b=4), reads=[vbb])
                        if prompt and self.cfg.get("vmode", 2) > 0:
                            vf, vfb = self.t512.next()
                            if self.cfg.get("vmode", 2) == 1:
                                k.op("act", _act(vf[:, :], ps4[:, :], AF.Copy), [pb4], [vfb])
                            else:
                                k.op("dve", _cp(vf[:, :], ps4[:, :]), [pb4], [vfb])
                            for sq_ in range(2):
                                k.dma("sp", nv_o[sq_, jl, :, hh * 128:(hh + 1) * 128].rearrange("(b p) f -> p b f", p=128),
                                      vf[:, sq_ * 256:(sq_ + 1) * 256].rearrange("p (b f) -> p b f", b=2), reads=[vfb])
                groups = [[(g4 * 4 + i) * 128 for i in range(4)] for g4 in self.cfg.get("qkv_groups", range(6))]
                self.gemm_fm(self.hT, self.hT_b, DC, TT, Wqkv, groups, cb)
        if self.cfg.get("attn_stop") == 1:
            return
        with k.phase():
            KTp = k.sbpool(2, [128, S_LEN + PAST], BF16, "KT")
            Vp = k.sbpool(2, [128, 36, 128], BF16, "Vt")
            ck = k.sb([128, 4, 512], F32, "ck")
            ckb = Buf()
            k.dma("sp", ck[:, :, :], self.ins["cache_k"][jl].rearrange("(b p) f -> p b f", p=128), writes=[ckb])
            QTp = k.sbpool(3, [128, 512], BF16, "QT")
            PTp = k.sbpool(4, [128, 512], BF16, "PT")
            scale = 1.0 / np.sqrt(128.0)
            jobs = [(0, 32, 0, S_LEN, True), (S_LEN, 2, S_LEN, P_LEN, False), (S_LEN + P_LEN, 2, S_LEN + P_LEN, P_LEN, False)]
            for (k0, nkb0, q0, qlen, ctx) in jobs:
                nkb = nkb0 + (4 if ctx else 0)
                QB = min(512, qlen)
                for g in range(4):
                    KT, KTb = KTp.next()
                    V, Vb = Vp.next()
                    k.dma("sp", KT[:, 0:nkb0 * 128], self.kT[g * 128:(g + 1) * 128, k0:k0 + nkb0 * 128], writes=[KTb])
                    k.dma("sp", V[:, 0:nkb0, :], self.vtok[k0:k0 + nkb0 * 128, g * 128:(g + 1) * 128].rearrange("(b p) f -> p b f", p=128), writes=[Vb])
                    if ctx:
                        ps4, pb4 = self.ps.next()
                        k.mm([_tp(ps4[:, i * 128:(i + 1) * 128], ck[:, i, g * 128:(g + 1) * 128], self.ident[:, :]) for i in range(4)], [ckb, self.cb], [pb4])
                        k.op("act", _act(KT[:, nkb0 * 128:nkb * 128], ps4[:, :], AF.Copy), [pb4], [KTb])
                        k.dma("pool", V[:, nkb0:nkb, :], self.ins["cache_v"][jl][:, g * 128:(g + 1) * 128].rearrange("(b p) f -> p b f", p=128), writes=[Vb])
                    its = [(h, qb) for h in range(4) for qb in range(qlen // QB)]

                    def load_q(h, qb, g=g, q0=q0, QB=QB):
                        QT, QTb = QTp.next()
                        k.dma("sp", QT[:, 0:QB], self.qT[(g * 4 + h) * 128:(g * 4 + h + 1) * 128, q0 + qb * QB:q0 + (qb + 1) * QB], writes=[QTb])
                        return QT, QTb
                    nq = load_q(*its[0])
                    for ii, (h, qb) in enumerate(its):
                        hd = g * 4 + h
                        if True:
                            qq0 = q0 + qb * QB
                            QT, QTb = nq
                            if ii + 1 < len(its):
                                nq = load_q(*its[ii + 1])
                            psO, pbO = self.psacc.next()
                            psD, pbD = self.psacc.next()
                            prev = None
                            for kb in range(nkb + 1):
                                cur = None
                                if kb < nkb:
                                    psS, pbS = self.ps.next()
                                    k.mm([_mmf(psS[:, 0:QB], KT[:, kb * 128:(kb + 1) * 128], QT[:, 0:QB])], [KTb, QTb], [pbS])
                                    PT, PTb = PTp.next()
                                    k.op("act", _act(PT[:, 0:QB], psS[:, 0:QB], AF.Exp, scale=float(scale)), [pbS], [PTb])
                                    cur = (PT, PTb, kb)
                                if prev is not None:
                                    PTq, PTqb, kq = prev
                                    k.mm([_mmf(psO[:, 0:QB], V[:, kq, :], PTq[:, 0:QB], start=(kq == 0), stop=(kq == nkb - 1)),
                                          _mmf(psD[:, 0:QB], self.ones_h[:, :], PTq[:, 0:QB], start=(kq == 0), stop=(kq == nkb - 1))],
                                         [Vb, PTqb, self.cb], [pbO, pbD])
                                prev = cur
                            rc, rcb = self.t512.next()
                            k.op("dve", lambda h, rc=rc, psD=psD, QB=QB: h.reciprocal(out=rc[:, 0:QB], in_=psD[:, 0:QB]), [pbD], [rcb])
                            o, ob = self.h512.next()
                            k.op("dve", _tt(o[:, 0:QB], psO[:, 0:QB], rc[:, 0:QB], ALU.mult), [pbO, rcb], [ob])
                            k.dma("sp", self.oT[hd * 128:(hd + 1) * 128, qq0:qq0 + QB], o[:, 0:QB], reads=[ob])
        if self.cfg.get("attn_stop") == 2:
            return
        self.out_proj(Wo, DC)

    def out_proj(self, Wo, KC):
        k = self.k
        with k.phase():
            self.alloc_gemm()
            TT = 16384 // KC
            G = 8192 // (KC * 128)
            act = self.hT if KC == DC else self.hT32
            for t0 in range(0, T, TT):
                r = 1 if t0 >= S_LEN else 0
                TTl = min(TT, T - t0)
                for h2 in range(KC // DC):
                    k.dma("sp", act[:, h2 * DC:(h2 + 1) * DC, 0:TTl],
                          self.oT[h2 * D:(h2 + 1) * D, t0:t0 + TTl].rearrange("(c p) t -> p c t", p=128), writes=[self.hT_b])
                groups = [[(g4 * G + i) * 128 for i in range(G)] for g4 in range(DC // G)]
                self.gemm_fm(act, self.hT_b, KC, TTl, Wo, groups, self.resid_cb(t0, lambda c, r=r: self.gate[:, 0, r, c:c + 1]))

    def conv_chunk(self, src, pin, pinb, acc, accb, cw, cbias, c, prm):
        k = self.k
        for (t0, L, q0) in SEGQ:
            k.dma("sp", pin[:, q0:q0 + L], src[:, t0:t0 + L], writes=[pinb])
        N = PADT - 3
        k.op("dve", _ts(acc[:, 1:1 + N], pin[:, 0:N], cw[:, 0, c:c + 1], cbias[:, c:c + 1], ALU.mult, ALU.add), [pinb, prm], [accb])
        for kk in range(1, 4):
            k.op("dve", _stt(acc[:, 1:1 + N], pin[:, kk:kk + N], cw[:, kk, c:c + 1], acc[:, 1:1 + N], ALU.mult, ALU.add), [pinb, prm, accb], [accb])

    def conv_params(self, wname, bname, jl, nch, prm):
        k = self.k
        cw = k.sb([128, 4, nch], F32, "cw")
        cbias = k.sb([128, nch], F32, "cbias")
        for kk in range(4):
            k.dma("sp", cw[:, kk, :], self.ins[wname][jl, kk].rearrange("(c p) -> p c", p=128), writes=[prm], nc_ok=True)
        k.dma("sp", cbias[:, :], self.ins[bname][jl].rearrange("(c p) -> p c", p=128), writes=[prm], nc_ok=True)
        return cw, cbias

    def lru(self, l):
        k = self.k
        jl = l // 3
        Win = self.ins["lru_w_in"][jl]
        Wout = self.ins["lru_w_out"][jl]
        recT, gT = self.xbcT, self.zsT
        with k.phase():
            self.alloc_gemm()
            self.xin = k.sbpool(1, [128, DC, 512], F32, "xin")
            for (t0, TT) in TILES:
                r = 1 if t0 >= S_LEN else 0
                self.norm_mod(t0, TT, 0, r)

                def cb(gi, s, col, blk, ps, pb, t0=t0):
                    j = col // 128
                    tt0 = t0 + blk * 512
                    if j < 16:
                        o, ob = self.h512.next()
                        k.op("act", _act(o[:, :], ps[:, :], AF.Gelu_apprx_tanh), [pb], [ob])
                        k.dma("sp", gT[j * 128:(j + 1) * 128, tt0:tt0 + 512], o[:, :], reads=[ob])
                    else:
                        o, ob = self.xo.next()
                        k.op("dve", _cp(o[:, :], ps[:, :]), [pb], [ob])
                        k.dma("sp", recT[(j - 16) * 128:(j - 15) * 128, tt0:tt0 + 512], o[:, :], reads=[ob])
                groups = [[(g4 * 4 + i) * 128 for i in range(4)] for g4 in range(8)]
                self.gemm_fm(self.hT, self.hT_b, DC, TT, Win, groups, cb)
        with k.phase():
            N = PADT - 3
            prm = Buf()
            pin = k.sb([128, PADT], F32, "pin"); pinb = Buf()
            k.op("dve", lambda h: h.memset(pin[:, :], 0.0), writes=[pinb])
            acc = k.sb([128, PADT], F32, "acc"); accb = Buf()
            recb = k.sb([128, PADT], BF16, "recb"); recbb = Buf()
            R = k.sb([128, PADT], F32, "R"); Rb = Buf()
            I = k.sb([128, PADT], F32, "I"); Ib = Buf()
            Hf = k.sb([128, PADT], F32, "Hf"); Hfb = Buf()
            Hb = k.sb([128, PADT], F32, "Hb"); Hbb = Buf()
            gpad = k.sb([128, PADT], BF16, "gpad"); gpb = Buf()
            yb = k.sb([128, PADT], BF16, "yb"); ybb = Buf()
            cw, cbias = self.conv_params("lru_conv_w", "lru_conv_b", jl, 16, prm)
            ba = k.sb([128, 2, 16], F32, "ba"); bi = k.sb([128, 2, 16], F32, "bi")
            apm = k.sb([128, 32], F32, "apm"); h0 = k.sb([128, 2, 16], F32, "h0")
            wa = k.sb([128, 32, 128], BF16, "wa"); wi = k.sb([128, 32, 128], BF16, "wi")
            for d in range(2):
                k.dma("sp", ba[:, d, :], self.ins["lru_b_a"][jl, d].rearrange("(c p) -> p c", p=128), writes=[prm], nc_ok=True)
                k.dma("sp", bi[:, d, :], self.ins["lru_b_i"][jl, d].rearrange("(c p) -> p c", p=128), writes=[prm], nc_ok=True)
                k.dma("sp", apm[:, d * 16:(d + 1) * 16], self.ins["lru_a_param"][jl, d].rearrange("(c p) -> p c", p=128), writes=[prm], nc_ok=True)
                k.dma("sp", h0[:, d, :], self.ins["state_lru"][d].rearrange("(c p) -> p c", p=128), writes=[prm], nc_ok=True)
                k.dma("pool", wa[:, d * 16:(d + 1) * 16, :], self.ins["lru_w_a"][jl, d].rearrange("c w v -> w c v"), writes=[prm])
                k.dma("pool", wi[:, d * 16:(d + 1) * 16, :], self.ins["lru_w_i"][jl, d].rearrange("c w v -> w c v"), writes=[prm])
            t1 = k.sb([128, 32], F32, "lt1"); t2 = k.sb([128, 32], F32, "lt2")
            nsp8 = k.sb([128, 32], F32, "nsp8"); nsp16 = k.sb([128, 32], F32, "nsp16")
            k.op("act", _act(t1[:, :], apm[:, :], AF.Abs), [prm], [prm])
            k.op("act", _act(t1[:, :], t1[:, :], AF.Exp, scale=-1.0), [prm], [prm])
            k.op("act", _act(t1[:, :], t1[:, :], AF.Ln, bias=self.ones_f[:, 0:1], scale=1.0), [prm, self.cb], [prm])
            k.op("dve", _ts(t2[:, :], apm[:, :], -1.0, 0.0, ALU.mult, ALU.max), [prm], [prm])
            k.op("dve", _tt(t2[:, :], t2[:, :], t1[:, :], ALU.add), [prm], [prm])
            k.op("dve", _ts(nsp8[:, :], t2[:, :], -8.0, None, ALU.mult), [prm], [prm])
            k.op("dve", _ts(nsp16[:, :], t2[:, :], -16.0, None, ALU.mult), [prm], [prm])
            fin = k.sb([128, 2, 2, 16], F32, "fin"); finb = Buf()
            for c in range(16):
                self.conv_chunk(recT[c * 128:(c + 1) * 128, :], pin, pinb, acc, accb, cw, cbias, c, prm)
                k.op("act", _act(recb[:, 1:1 + N], acc[:, 1:1 + N], AF.Copy), [accb], [recbb])
                for (t0, L, q0) in SEGQ:
                    k.dma("sp", gpad[:, q0:q0 + L], gT[c * 128:(c + 1) * 128, t0:t0 + L], writes=[gpb])
                for d in range(2):
                    dc_ = d * 16 + c
                    for q in range(1, 1 + N, 512):
                        w = min(512, 1 + N - q)
                        ps, pb = self.ps.next()
                        k.mm([_mmf(ps[:, 0:w], wa[:, dc_, :], recb[:, q:q + w])], [prm, recbb], [pb])
                        k.op("act", _act(R[:, q:q + w], ps[:, 0:w], AF.Sigmoid, bias=ba[:, d, c:c + 1], scale=1.0), [pb, prm], [Rb])
                        ps2, pb2 = self.ps.next()
                        k.mm([_mmf(ps2[:, 0:w], wi[:, dc_, :], recb[:, q:q + w])], [prm, recbb], [pb2])
                        k.op("act", _act(I[:, q:q + w], ps2[:, 0:w], AF.Sigmoid, bias=bi[:, d, c:c + 1], scale=1.0), [pb2, prm], [Ib])
                    S, Sb = (Hf, Hfb) if d == 0 else (Hb, Hbb)
                    k.op("act", _act(S[:, 1:1 + N], R[:, 1:1 + N], AF.Exp, scale=nsp16[:, dc_:dc_ + 1]), [Rb, prm], [Sb])
                    k.op("act", _act(S[:, 1:1 + N], S[:, 1:1 + N], AF.Sqrt, bias=self.ones_f[:, 0:1], scale=-1.0), [Sb, self.cb], [Sb])
                    k.op("act", _act(R[:, 1:1 + N], R[:, 1:1 + N], AF.Exp, scale=nsp8[:, dc_:dc_ + 1]), [Rb, prm], [Rb])
                    k.op("dve", _tt(I[:, 1:1 + N], I[:, 1:1 + N], S[:, 1:1 + N], ALU.mult), [Ib, Sb], [Ib])
                    k.op("dve", _tt(I[:, 1:1 + N], I[:, 1:1 + N], acc[:, 1:1 + N], ALU.mult), [Ib, accb], [Ib])
                    H, Hbuf = (Hf, Hfb) if d == 0 else (Hb, Hbb)
                    for si, (t0, L, q0) in enumerate(SEGQ):
                        init = h0[:, d, c:c + 1] if si == 0 else self.zcol[:, 0:1]
                        if d == 0:
                            o_, a_, u_ = H[:, q0:q0 + L], R[:, q0:q0 + L], I[:, q0:q0 + L]
                        else:
                            o_, a_, u_ = H[:, q0 + L - 1:q0 - 1:-1], R[:, q0 + L - 1:q0 - 1:-1], I[:, q0 + L - 1:q0 - 1:-1]
                        k.op("dve", lambda h, o_=o_, a_=a_, u_=u_, init=init: h.tensor_tensor_scan(
                            out=o_, data0=a_, data1=u_, initial=init, op0=ALU.mult, op1=ALU.add), [Rb, Ib, prm, self.cb], [Hbuf])
                        if si > 0:
                            src = H[:, q0 + L - 1:q0 + L] if d == 0 else H[:, q0:q0 + 1]
                            k.op("dve", _cp(fin[:, si - 1, d, c:c + 1], src), [Hbuf], [finb])
                k.op("dve", _tt(Hf[:, 1:1 + N], Hf[:, 1:1 + N], Hb[:, 1:1 + N], ALU.add), [Hfb, Hbb], [Hfb])
                k.op("dve", _tt(yb[:, 1:1 + N], Hf[:, 1:1 + N], gpad[:, 1:1 + N], ALU.mult), [Hfb, gpb], [ybb])
                for (t0, L, q0) in SEGQ:
                    k.dma("sp", self.oT[c * 128:(c + 1) * 128, t0:t0 + L], yb[:, q0:q0 + L], reads=[ybb])
            for pr in range(2):
                for d in range(2):
                    k.dma("sp", self.outs["nlru"][pr, d].rearrange("(c p) -> p c", p=128), fin[:, pr, d, :], reads=[finb], nc_ok=True)
        self.out_proj(Wout, DC)

    def ssd(self, l):
        k = self.k
        jl = l // 3
        Win = self.ins["ssd_w_in"][jl]
        Wout = self.ins["ssd_w_out"][jl]
        N = PADT - 3

        def v3(ap, a):
            return ap.rearrange("p (a b) -> p a b", a=a)

        with k.phase():
            self.alloc_gemm()
            self.xin = k.sbpool(1, [128, DC, 512], F32, "xin")
            prm = Buf()
            dtb = k.sb([128, 1], F32, "dtb"); nea = k.sb([128, 1], F32, "nea")
            k.dma("sp", dtb[:, :], self.ins["ssd_dt_bias"][jl].rearrange("(p o) -> p o", o=1), writes=[prm], nc_ok=True)
            k.dma("sp", nea[:, :], self.ins["ssd_a_log"][jl].rearrange("(p o) -> p o", o=1), writes=[prm], nc_ok=True)
            k.op("act", _act(nea[:, :], nea[:, :], AF.Exp), [prm], [prm])
            for (t0, TT) in TILES:
                r = 1 if t0 >= S_LEN else 0
                self.norm_mod(t0, TT, 0, r)

                def cb(gi, s, col, blk, ps, pb, t0=t0):
                    j = col // 128
                    tt0 = t0 + blk * 512
                    if j < 32:
                        o, ob = self.h512.next()
                        k.op("act", _act(o[:, :], ps[:, :], AF.Silu), [pb], [ob])
                        k.dma("sp", self.zsT[j * 128:(j + 1) * 128, tt0:tt0 + 512], o[:, :], reads=[ob])
                    elif j < 80:
                        o, ob = self.xo.next()
                        k.op("dve", _cp(o[:, :], ps[:, :]), [pb], [ob])
                        k.dma("sp", self.xbcT[(j - 32) * 128:(j - 31) * 128, tt0:tt0 + 512], o[:, :], reads=[ob])
                    else:
                        xv, xvb = self.t512.next()
                        k.op("act", _act(xv[:, :], ps[:, :], AF.Identity, bias=dtb[:, 0:1], scale=1.0), [pb, prm], [xvb])
                        ax, axb = self.t512.next()
                        k.op("act", _act(ax[:, :], xv[:, :], AF.Abs), [xvb], [axb])
                        k.op("act", _act(ax[:, :], ax[:, :], AF.Exp, scale=-1.0), [axb], [axb])
                        k.op("act", _act(ax[:, :], ax[:, :], AF.Ln, bias=self.ones_f[:, 0:1], scale=1.0), [axb, self.cb], [axb])
                        dtv, dtvb = self.t512.next()
                        k.op("dve", _stt(dtv[:, :], xv[:, :], 0.0, ax[:, :], ALU.max, ALU.add), [xvb, axb], [dtvb])
                        dav, davb = self.t512.next()
                        k.op("dve", _ts(dav[:, :], dtv[:, :], nea[:, 0:1], -1.0, ALU.mult, ALU.mult), [dtvb, prm], [davb])
                        for which, (src, srcb) in enumerate(((dtv, dtvb), (dav, davb))):
                            ps4, pb4 = self.ps.next()
                            k.mm([_tp(ps4[:, i * 128:(i + 1) * 128], src[:, i * 128:(i + 1) * 128], self.ident[:, :]) for i in range(4)], [srcb, self.cb], [pb4])
                            o, ob = self.xo.next()
                            k.op("dve", _cp(o[:, :], ps4[:, :]), [pb4], [ob])
                            k.dma("sp", self.dtk[tt0:tt0 + 512, which, :].rearrange("(b p) f -> p b f", p=128),
                                  o[:, :].rearrange("p (b f) -> p b f", b=4), reads=[ob])
                groups = [[(g4 * 4 + i) * 128 for i in range(4)] for g4 in range(20)] + [[80 * 128]]
                self.gemm_fm(self.hT, self.hT_b, DC, TT, Win, groups, cb)
        if self.cfg.get("ssd_stop") == 1:
            return
        with k.phase():
            prm = Buf()
            pin = k.sb([128, PADT], F32, "pin"); pinb = Buf()
            k.op("dve", lambda h: h.memset(pin[:, :], 0.0), writes=[pinb])
            acc = k.sb([128, PADT], F32, "acc"); accb = Buf()
            sil = k.sb([128, PADT], F32, "sil"); silb_ = Buf()
            silh = k.sb([128, PADT], BF16, "silh"); silhb = Buf()
            cw, cbias = self.conv_params("ssd_conv_w", "ssd_conv_b", jl, 48, prm)
            for cc in range(48):
                self.conv_chunk(self.xbcT[cc * 128:(cc + 1) * 128, :], pin, pinb, acc, accb, cw, cbias, cc, prm)
                if cc < 40:
                    k.op("act", _act(sil[:, 1:1 + N], acc[:, 1:1 + N], AF.Silu), [accb], [silb_])
                    for grp in range(T // 512):
                        ps4, pb4 = self.ps.next()
                        fns = []
                        for i in range(4):
                            t = (grp * 4 + i) * 128
                            q = t + 1 + 2 * (0 if t < S_LEN else (1 if t < S_LEN + P_LEN else 2))
                            fns.append(_tp(ps4[:, i * 128:(i + 1) * 128], sil[:, q:q + 128], self.ident[:, :]))
                        k.mm(fns, [silb_, self.cb], [pb4])
                        o, ob = self.h512.next()
                        if grp % 2 == 0:
                            k.op("dve", _cp(o[:, :], ps4[:, :]), [pb4], [ob])
                        else:
                            k.op("act", _act(o[:, :], ps4[:, :], AF.Copy), [pb4], [ob])
                        dst = self.xtok[grp * 512:(grp + 1) * 512, cc * 128:(cc + 1) * 128] if cc < 32 else \
                            self.btok[grp * 512:(grp + 1) * 512, (cc - 32) * 128:(cc - 31) * 128]
                        k.dma("sp", dst.rearrange("(b p) f -> p b f", p=128), o[:, :].rearrange("p (b f) -> p b f", b=4), reads=[ob])
                if cc >= 32:
                    k.op("act", _act(silh[:, 1:1 + N], acc[:, 1:1 + N], AF.Silu), [accb], [silhb])
                    for (t0, L, q0) in SEGQ:
                        k.dma("sp", self.bcT[(cc - 32) * 128:(cc - 31) * 128, t0:t0 + L], silh[:, q0:q0 + L], reads=[silhb])
        if self.cfg.get("ssd_stop") == 2:
            return
        with k.phase():
            prm = Buf()
            tri = k.sb([128, 4, 128], F32, "tri")
            for i in range(4):
                k.dma("sp", tri[:, i, :], self.ins["c_tri"][i], writes=[prm])
            stT = k.sb([128, 4096], F32, "stT"); stTb = [Buf() for _ in range(8)]
            stH = k.sb([128, 4096], BF16, "stH"); stHb = [Buf() for _ in range(8)]
            hst = k.sb([128, 32, 128], F32, "hst"); hstb = Buf()
            xt_p = k.sbpool(3, [128, 4096], BF16, "xt")
            xdt_p = k.sbpool(2, [128, 4096], BF16, "xdt")
            xw_p = k.sbpool(2, [128, 4096], BF16, "xw")
            bt_p = k.sbpool(3, [128, 1024], BF16, "bt")
            bT_p = k.sbpool(3, [128, 8, 128], BF16, "bT")
            cT_p = k.sbpool(3, [128, 8, 128], BF16, "cT")
            dd_p = k.sbpool(3, [128, 2, 128], F32, "dd")
            sc_p = k.sbpool(2, [128, 192], F32, "sc")
            dU_p = k.sbpool(4, [128, 512], F32, "dU")
            E_p = k.sbpool(6, [128, 512], F32, "E")
            MT_p = k.sbpool(4, [128, 512], BF16, "MT")
            SM_p = k.sbpool(4, [128, 128], F32, "SM")
            y_p = k.sbpool(3, [128, 512], F32, "yo")

            def issue_loads(t0):
                xt, xtb = xt_p.next()
                k.dma("sp", xt[:, :], self.xtok[t0:t0 + 128, :], writes=[xtb])
                bt, btb = bt_p.next()
                k.dma("sp", bt[:, :], self.btok[t0:t0 + 128, :], writes=[btb])
                bT, bTb = bT_p.next()
                k.dma("sp", bT[:, :, :], self.bcT[0:1024, t0:t0 + 128].rearrange("(g p) t -> p g t", p=128), writes=[bTb])
                cT, cTb = cT_p.next()
                k.dma("sp", cT[:, :, :], self.bcT[1024:2048, t0:t0 + 128].rearrange("(g p) t -> p g t", p=128), writes=[cTb])
                dd, ddb = dd_p.next()
                k.dma("sp", dd[:, :, :], self.dtk[t0:t0 + 128, :, :], writes=[ddb])
                return dict(xt=xt, xtb=xtb, bt=bt, btb=btb, bT=bT, bTb=bTb, cT=cT, cTb=cTb, dd=dd, ddb=ddb, t0=t0)

            for d in range(2):
                Lm = tri[:, 0 if d == 0 else 2, :]
                Um = tri[:, 1 if d == 0 else 3, :]

                def prologue(c, d=d, Lm=Lm, Um=Um):
                    c["dt"] = c["dd"][:, 0, d * 64:(d + 1) * 64]
                    c["dta"] = c["dd"][:, 1, d * 64:(d + 1) * 64]
                    psc, pscb = self.ps.next()
                    k.mm([_mmf(psc[:, 0:64], Um, c["dta"]), _mmf(psc[:, 64:128], Lm, c["dta"]), _mmf(psc[:, 128:192], self.ones_f[:, :], c["dta"])],
                         [prm, c["ddb"], self.cb], [pscb])
                    sc, scb = sc_p.next()
                    k.op("act", _act(sc[:, :], psc[:, 0:192], AF.Exp), [pscb], [scb])
                    xdt, xdtb = xdt_p.next()
                    k.op("dve", _tt(v3(xdt[:, :], 64), v3(c["xt"][:, :], 64), c["dt"].unsqueeze(2).broadcast_to([128, 64, 64]), ALU.mult), [c["xtb"], c["ddb"]], [xdtb])
                    xw, xwb = xw_p.next()
                    k.op("dve", _tt(v3(xw[:, :], 64), v3(xdt[:, :], 64), sc[:, 64:128].unsqueeze(2).broadcast_to([128, 64, 64]), ALU.mult), [xdtb, scb], [xwb])
                    c.update(sc=sc, scb=scb, xdt=xdt, xdtb=xdtb, xw=xw, xwb=xwb)

                def stage_a(c, g, Lm=Lm, Um=Um):
                    psS, psSb = self.ps.next()
                    k.mm([_mmf(psS[:, 0:128], c["bT"][:, g, :], c["cT"][:, g, :])], [c["bTb"], c["cTb"]], [psSb])
                    SM, SMb = SM_p.next()
                    k.op("dve", _tt(SM[:, :], psS[:, 0:128], Um, ALU.mult), [psSb, prm], [SMb])
                    Es = []
                    for hf in range(2):
                        h0_ = g * 8 + hf * 4
                        dU, dUb = dU_p.next()
                        k.op("dve", _tt(v3(dU[:, :], 4), Um.unsqueeze(1).broadcast_to([128, 4, 128]),
                                        c["dta"][:, h0_:h0_ + 4].unsqueeze(2).broadcast_to([128, 4, 128]), ALU.mult), [prm, c["ddb"]], [dUb])
                        psE, psEb = self.ps.next()
                        k.mm([_mmf(psE[:, :], Lm, dU[:, :])], [prm, dUb], [psEb])
                        E, Eb = E_p.next()
                        k.op("act", _act(E[:, :], psE[:, :], AF.Exp), [psEb], [Eb])
                        Es.append((E, Eb))
                    return dict(SM=SM, SMb=SMb, Es=Es)

                def stage_b(c, g, A):
                    gs = slice(g * 512, (g + 1) * 512)
                    MTs = []
                    for hf in range(2):
                        E, Eb = A["Es"][hf]
                        MT, MTb = MT_p.next()
                        k.op("dve", _tt(v3(MT[:, :], 4), v3(E[:, :], 4), A["SM"][:, :].unsqueeze(1).broadcast_to([128, 4, 128]), ALU.mult), [Eb, A["SMb"]], [MTb])
                        MTs.append((MT, MTb))
                    psY, psYb = self.psacc.next()
                    fns = [_mmf(psY[:, h * 64:(h + 1) * 64], MTs[h // 4][0][:, (h % 4) * 128:(h % 4 + 1) * 128],
                                c["xdt"][:, (g * 8 + h) * 64:(g * 8 + h + 1) * 64]) for h in range(8)]
                    k.mm(fns, [MTs[0][1], MTs[1][1], c["xdtb"]], [psYb])
                    psI, psIb = self.psacc.next()
                    k.mm([_mmf(psI[:, :], c["cT"][:, g, :], stH[:, gs])], [c["cTb"], stHb[g]], [psIb])
                    A.update(psY=psY, psYb=psYb, psI=psI, psIb=psIb)

                def stage_c(c, g, A, d=d):
                    gs = slice(g * 512, (g + 1) * 512)
                    sc, scb = c["sc"], c["scb"]
                    psD, psDb = self.ps.next()
                    k.mm([_mmf(psD[:, :], c["bt"][:, g * 128:(g + 1) * 128], c["xw"][:, gs])], [c["btb"], c["xwb"]], [psDb])
                    yo, yob = y_p.next()
                    k.op("dve", _tt(v3(yo[:, :], 8), v3(A["psI"][:, :], 8), sc[:, g * 8:(g + 1) * 8].unsqueeze(2).broadcast_to([128, 8, 64]), ALU.mult),
                         [A["psIb"], scb], [yob])
                    k.op("dve", _tt(yo[:, :], yo[:, :], A["psY"][:, :], ALU.add), [yob, A["psYb"]], [yob])
                    k.dma("sp", self.ytok[d, c["t0"]:c["t0"] + 128, gs], yo[:, :], reads=[yob])
                    k.op("dve", _tt(v3(stT[:, gs], 8), v3(stT[:, gs], 8), sc[:, 128 + g * 8:128 + (g + 1) * 8].unsqueeze(2).broadcast_to([128, 8, 64]), ALU.mult),
                         [scb, stTb[g]], [stTb[g]])
                    k.op("dve", _tt(stT[:, gs], stT[:, gs], psD[:, :], ALU.add), [stTb[g], psDb], [stTb[g]])
                    k.op("act", _act(stH[:, gs], stT[:, gs], AF.Copy), [stTb[g]], [stHb[g]])

                for si, (s0, SL, _) in enumerate(SEGS):
                    nch = SL // 128
                    order = list(range(nch)) if d == 0 else list(range(nch - 1, -1, -1))
                    if self.cfg.get("ssd_nch"):
                        order = order[:self.cfg["ssd_nch"]]
                    t0s = [s0 + ci * 128 for ci in order]
                    ctxs = {0: issue_loads(t0s[0])}
                    if len(t0s) > 1:
                        ctxs[1] = issue_loads(t0s[1])
                    if si == 0:
                        for half in range(2):
                            k.dma("sp", hst[:, half * 16:(half + 1) * 16, :],
                                  self.ins["state_ssm"][d, half * 2048:(half + 1) * 2048, :].rearrange("(b p) n -> p b n", p=128), writes=[hstb])
                        for r4 in range(8):
                            ps, pb = self.ps.next()
                            k.mm([_tp(ps[:, i * 128:(i + 1) * 128], hst[:, r4 * 4 + i, :], self.ident[:, :]) for i in range(4)], [hstb, self.cb], [pb])
                            k.op("dve", _cp(stT[:, r4 * 512:(r4 + 1) * 512], ps[:, :]), [pb], [stTb[r4]])
                            k.op("act", _act(stH[:, r4 * 512:(r4 + 1) * 512], stT[:, r4 * 512:(r4 + 1) * 512], AF.Copy), [stTb[r4]], [stHb[r4]])
                    else:
                        for g in range(8):
                            k.op("dve", lambda h, g=g: h.memset(stT[:, g * 512:(g + 1) * 512], 0.0), writes=[stTb[g]])
                            k.op("dve", lambda h, g=g: h.memset(stH[:, g * 512:(g + 1) * 512], 0.0), writes=[stHb[g]])
                    prologue(ctxs[0])
                    items = [(vi, g) for vi in range(len(t0s)) for g in range(8)]
                    As = {}
                    for step in range(len(items) + 2):
                        if step < len(items):
                            vi, g = items[step]
                            if g == 4 and vi + 1 < len(t0s):
                                if vi + 2 < len(t0s):
                                    ctxs[vi + 2] = issue_loads(t0s[vi + 2])
                                prologue(ctxs[vi + 1])
                            As[step] = stage_a(ctxs[vi], g)
                        if 0 <= step - 1 < len(items):
                            vi, g = items[step - 1]
                            stage_b(ctxs[vi], g, As[step - 1])
                        if 0 <= step - 2 < len(items):
                            vi, g = items[step - 2]
                            stage_c(ctxs[vi], g, As[step - 2])
                            del As[step - 2]
                    if si > 0:
                        for r4 in range(8):
                            ps, pb = self.ps.next()
                            k.mm([_tp(ps[:, i * 128:(i + 1) * 128], stT[:, (r4 * 4 + i) * 128:(r4 * 4 + i + 1) * 128], self.ident[:, :]) for i in range(4)],
                                 [stTb[r4], self.cb], [pb])
                            o, ob = y_p.next()
                            k.op("dve", _cp(o[:, :], ps[:, :]), [pb], [ob])
                            k.dma("sp", self.outs["nssm"][si - 1, d, r4 * 512:(r4 + 1) * 512, :].rearrange("(b p) n -> p b n", p=128),
                                  o[:, :].rearrange("p (b n) -> p b n", b=4), reads=[ob])
        if self.cfg.get("ssd_stop") == 3:
            return
        with k.phase():
            prm = Buf()
            dsk = k.sb([128, 64], F32, "dsk")
            k.dma("sp", dsk[:, :], self.ins["ssd_d"][jl].partition_broadcast(128), writes=[prm], nc_ok=True)
            nw = k.sb([128, 32], F32, "nw")
            k.dma("sp", nw[:, :], self.ins["ssd_norm"][jl].rearrange("(c p) -> p c", p=128), writes=[prm], nc_ok=True)
            yf_p = k.sbpool(2, [128, 4096], F32, "yf")
            yb_p = k.sbpool(2, [128, 4096], F32, "yb2")
            xt_p = k.sbpool(2, [128, 4096], BF16, "xt2")
            zs_p = k.sbpool(2, [128, 32, 128], BF16, "zs")
            gt_p = k.sbpool(1, [128, 32, 128], F32, "gt")
            yo_p = k.sbpool(2, [128, 32, 128], BF16, "yo2")
            def load_blk(t0):
                yf, yfb = yf_p.next()
                k.dma("sp", yf[:, :], self.ytok[0, t0:t0 + 128, :], writes=[yfb])
                y2, y2b = yb_p.next()
                k.dma("sp", y2[:, :], self.ytok[1, t0:t0 + 128, :], writes=[y2b])
                xt, xtb = xt_p.next()
                k.dma("sp", xt[:, :], self.xtok[t0:t0 + 128, :], writes=[xtb])
                zs, zsb = zs_p.next()
                for half in range(2):
                    k.dma("sp", zs[:, half * 16:(half + 1) * 16, :],
                          self.zsT[half * 2048:(half + 1) * 2048, t0:t0 + 128].rearrange("(c p) t -> p c t", p=128), writes=[zsb])
                return (yf, yfb, y2, y2b, xt, xtb, zs, zsb)
            nblk = load_blk(0)
            for b_ in range(T // 128):
                t0 = b_ * 128
                (yf, yfb, y2, y2b, xt, xtb, zs, zsb) = nblk
                if b_ + 1 < T // 128:
                    nblk = load_blk(t0 + 128)
                k.op("dve", _tt(yf[:, :], yf[:, :], y2[:, :], ALU.add), [yfb, y2b], [yfb])
                k.op("dve", _tt(v3(y2[:, :], 64), v3(xt[:, :], 64), dsk[:, :].unsqueeze(2).broadcast_to([128, 64, 64]), ALU.mult), [xtb, prm], [y2b])
                k.op("dve", _tt(yf[:, :], yf[:, :], y2[:, :], ALU.add), [yfb, y2b], [yfb])
                gt, gtb = gt_p.next()
                psQ, psQb = self.psacc.next()
                for c4 in range(8):
                    ps, pb = self.ps.next()
                    k.mm([_tp(ps[:, i * 128:(i + 1) * 128], yf[:, (c4 * 4 + i) * 128:(c4 * 4 + i + 1) * 128], self.ident[:, :]) for i in range(4)],
                         [yfb, self.cb], [pb])
                    k.op("dve", _tt(gt[:, c4 * 4:(c4 + 1) * 4, :], v3(ps[:, :], 4), zs[:, c4 * 4:(c4 + 1) * 4, :], ALU.mult), [pb, zsb], [gtb])
                    sq, sqb = self.h512.next()
                    k.op("act", _act(v3(sq[:, :], 4), gt[:, c4 * 4:(c4 + 1) * 4, :], AF.Square), [gtb], [sqb])
                    k.mm([_mmf(psQ[:, 0:128], self.ones_h[:, :], sq[:, i * 128:(i + 1) * 128], start=(c4 == 0 and i == 0), stop=(c4 == 7 and i == 3))
                          for i in range(4)], [sqb, self.cb], [psQb])
                rs, rsb = self.rstd_from_sumsq(psQ, psQb, 128, 1.0 / 4096)
                k.op("dve", _tt(gt[:, :, :], gt[:, :, :], nw[:, :].unsqueeze(2).broadcast_to([128, 32, 128]), ALU.mult), [gtb, prm], [gtb])
                yo, yob = yo_p.next()
                k.op("dve", _tt(yo[:, :, :], gt[:, :, :], rs[:, 0:128].unsqueeze(1).broadcast_to([128, 32, 128]), ALU.mult), [gtb, rsb], [yob])
                for half in range(2):
                    k.dma("sp", self.oT[half * 2048:(half + 1) * 2048, t0:t0 + 128].rearrange("(c p) t -> p c t", p=128),
                          yo[:, half * 16:(half + 1) * 16, :], reads=[yob])
        self.out_proj(Wout, 32)


def host_consts():
    ident = np.eye(128, dtype=np.float32)
    rot = np.zeros((128, 128), np.float32)
    for m in range(64):
        rot[m + 64, m] = -1.0
        rot[m, m + 64] = 1.0
    rows = S_LEN // 64
    row = np.repeat(np.arange(rows, dtype=np.float32), 64)
    col = np.tile(np.arange(64, dtype=np.float32), rows)
    inv = (10000.0 ** (-np.arange(32, dtype=np.float32) / 32)).astype(np.float32)
    ang = np.concatenate([row[:, None] * inv, col[:, None] * inv], axis=-1).astype(np.float32)
    cos = np.cos(ang).T.astype(np.float32)
    sin = np.sin(ang).T.astype(np.float32)
    c_cos = np.ascontiguousarray(np.concatenate([cos, cos], 0))
    c_sin = np.ascontiguousarray(np.concatenate([sin, sin], 0))
    kk = np.arange(128)
    tri = np.stack([
        (kk[:, None] > kk[None, :]),
        (kk[:, None] <= kk[None, :]),
        (kk[:, None] < kk[None, :]),
        (kk[:, None] >= kk[None, :]),
    ]).astype(np.float32)
    return {"c_ident": ident, "c_rot": rot, "c_cos": c_cos, "c_sin": c_sin, "c_tri": tri}


def make_in_maps(inputs, cores):
    consts = host_consts()
    maps = []
    shared = {}
    for name in ("ada_w", "ada_b", "norm_mix", "norm_ffn", "attn_w_qkv", "attn_q_norm", "attn_k_norm", "attn_w_o",
                 "ssd_w_in", "ssd_conv_w", "ssd_conv_b", "ssd_d", "ssd_norm", "ssd_w_out",
                 "lru_w_in", "lru_conv_w", "lru_conv_b", "lru_w_a", "lru_b_a", "lru_w_i", "lru_b_i", "lru_a_param", "lru_w_out",
                 "ffn_w_in", "ffn_w_out", "final_norm"):
        shared[name] = np.ascontiguousarray(inputs[name], dtype=np.float32)
    shared["ssd_dt_bias"] = np.ascontiguousarray(inputs["ssd_dt_bias"], dtype=np.float32).reshape(1, 128)
    shared["ssd_a_log"] = np.ascontiguousarray(inputs["ssd_a_log"], dtype=np.float32).reshape(1, 128)
    shared.update(consts)
    for i in cores:
        m = dict(shared)
        m["x_all"] = np.ascontiguousarray(np.concatenate(
            [inputs["x_sample"][i], inputs["x_prompt"][2 * i], inputs["x_prompt"][2 * i + 1]], axis=0), dtype=np.float32)
        m["cache_k"] = np.ascontiguousarray(inputs["cache_k"][i].reshape(2, PAST, 512), dtype=np.float32)
        m["cache_v"] = np.ascontiguousarray(inputs["cache_v"][i].reshape(2, PAST, 512), dtype=np.float32)
        m["state_ssm"] = np.ascontiguousarray(inputs["state_ssm"][i, 0].reshape(2, 4096, 128), dtype=np.float32)
        m["state_lru"] = np.ascontiguousarray(inputs["state_lru"][i, 0], dtype=np.float32)
        m["cond"] = np.ascontiguousarray(np.stack([inputs["c"][i], inputs["c_ctx"]], 0), dtype=np.float32)
        maps.append(m)
    return maps


def run(inputs, cores, cfg):
    net = Net(cfg)
    nc = net.build()
    maps = make_in_maps(inputs, cores)
    res = run_bass_kernel_spmd(nc, maps, core_ids=list(range(len(cores))))
    return res


def kernel(**inputs):
    inputs = {k: np.asarray(v) for k, v in inputs.items()}
    cores = list(range(8))
    res = run(inputs, cores, {})
    B, BS = 16, 8
    y_prompt = np.zeros((B, P_LEN, D), np.float32)
    y_sample = np.zeros((BS, S_LEN, D), np.float32)
    nk = np.zeros((B, 2, P_LEN, 4, 128), np.float32)
    nv = np.zeros((B, 2, P_LEN, 4, 128), np.float32)
    nssm = np.zeros((B, 1, 2, 64, 64, 128), np.float32)
    nlru = np.zeros((B, 1, 2, D), np.float32)
    for i, r in enumerate(res.results):
        ya = r["y_all"]
        y_sample[i] = ya[:S_LEN]
        y_prompt[2 * i] = ya[S_LEN:S_LEN + P_LEN]
        y_prompt[2 * i + 1] = ya[S_LEN + P_LEN:]
        nk[2 * i:2 * i + 2] = r["nk"].reshape(2, 2, P_LEN, 4, 128)
        nv[2 * i:2 * i + 2] = r["nv"].reshape(2, 2, P_LEN, 4, 128)
        nssm[2 * i:2 * i + 2, 0] = r["nssm"].reshape(2, 2, 64, 64, 128)
        nlru[2 * i:2 * i + 2, 0] = r["nlru"].reshape(2, 2, D)
    return (y_prompt, y_sample, nk, nv, nssm, nlru)
```

```python
import numpy as np
import ml_dtypes
import concourse.bass as bass
import concourse.mybir as mybir
from concourse.bass_utils import run_bass_kernel_spmd

F32 = mybir.dt.float32
BF16 = mybir.dt.bfloat16
ALU = mybir.AluOpType
AF = mybir.ActivationFunctionType

D = 2048
DC = 16
S_LEN = 4096
P_LEN = 256
T = S_LEN + 2 * P_LEN
NBLK = T // 512
DEPTH = 4
DFF = 5632
EPS = 1e-6
PAST = 512
TILES = [(0, 1024), (1024, 1024), (2048, 1024), (3072, 1024), (4096, 512)]
SEGS = [(0, S_LEN, 0), (S_LEN, P_LEN, 1), (S_LEN + P_LEN, P_LEN, 1)]
PADT = T + 7
SEGQ = [(0, S_LEN, 1), (S_LEN, P_LEN, S_LEN + 3), (S_LEN + P_LEN, P_LEN, S_LEN + P_LEN + 5)]


class Buf:
    __slots__ = ("w", "r")

    def __init__(self):
        self.w = None
        self.r = {}


class Eng:
    def __init__(self, name):
        self.name = name
        self.sem = None
        self.count = 0
        self.waited = {}
        self.rec = []


class Pool:
    def __init__(self, items):
        self.items = items
        self.i = 0

    def next(self):
        it = self.items[self.i % len(self.items)]
        self.i += 1
        return it


class Phase:
    def __init__(self, k):
        self.k = k
        self.cms = []

    def __enter__(self):
        assert self.k.cur_phase is None
        self.k.cur_phase = self
        return self

    def __exit__(self, *a):
        self.k.barrier()
        for cm in reversed(self.cms):
            cm.__exit__(None, None, None)
        self.k.cur_phase = None
        return False


class K:
    def __init__(self, dbg=None):
        self.nc = bass.Bass("TRN2", target_bir_lowering=False)
        self.E = {n: Eng(n) for n in ("pe", "act", "dve", "pool", "sp")}
        self.semof = {}
        self.sem_ctx = []
        for n, e in self.E.items():
            e.sem = self._sem("e_" + n)
            self.semof[n] = e.sem
        self.dslots = {}
        for q, n in (("sp", 44), ("pool", 28)):
            sl = []
            for i in range(n):
                key = "d%s%d" % (q, i)
                s = self._sem(key)
                self.semof[key] = s
                sl.append([key, s, 0])
            self.dslots[q] = Pool(sl)
        self.nsb = 0
        self.bar_sem = self._sem("bar")
        self.bar_count = 0
        self.cur_phase = None

    def _sem(self, name):
        cm = self.nc.semaphore(name)
        s = cm.__enter__()
        self.sem_ctx.append(cm)
        return s

    def sb(self, shape, dt, name=None):
        self.nsb += 1
        name = (name or "sb") + "_%d" % self.nsb
        if self.cur_phase is not None:
            cm = self.nc.sbuf_tensor(name, list(shape), dt)
            t = cm.__enter__()
            self.cur_phase.cms.append(cm)
            return t
        return self.nc.alloc_sbuf_tensor(name, list(shape), dt)

    def sbpool(self, n, shape, dt, name):
        return Pool([(self.sb(shape, dt, "%s%d" % (name, i)), Buf()) for i in range(n)])

    def dram(self, name, shape, dt):
        return self.nc.dram_tensor(name, list(shape), dt).ap()

    def _waits(self, e, reads, writes):
        need = {}
        for b in reads:
            w = b.w
            if w is not None and need.get(w[0], 0) < w[1]:
                need[w[0]] = w[1]
        for b in writes:
            w = b.w
            if w is not None and need.get(w[0], 0) < w[1]:
                need[w[0]] = w[1]
            for k, v in b.r.items():
                if need.get(k, 0) < v:
                    need[k] = v
        for k, v in need.items():
            if k == "pe" and e.name == "pe":
                continue
            if e.waited.get(k, 0) >= v:
                continue
            e.waited[k] = v
            sem = self.semof[k]
            e.rec.append(lambda h, sem=sem, v=v: h.wait_ge(sem, v))

    def _mark(self, ev, reads, writes):
        k, v = ev
        for b in reads:
            if b.r.get(k, 0) < v:
                b.r[k] = v
        for b in writes:
            b.w = ev
            b.r = {}

    def op(self, en, fn, reads=(), writes=()):
        e = self.E[en]
        self._waits(e, reads, writes)
        e.count += 1
        sem = e.sem
        e.rec.append(lambda h, fn=fn, sem=sem: fn(h).then_inc(sem, 1))
        self._mark((en, e.count), reads, writes)

    def mm(self, fns, reads, writes):
        e = self.E["pe"]
        self._waits(e, reads, writes)
        for fn in fns[:-1]:
            e.rec.append(fn)
        e.count += 1
        sem = e.sem
        fn = fns[-1]
        e.rec.append(lambda h, fn=fn, sem=sem: fn(h).then_inc(sem, 1))
        self._mark(("pe", e.count), reads, writes)

    def dma(self, q, out, in_, reads=(), writes=(), nc_ok=False):
        e = self.E[q]
        self._waits(e, reads, writes)
        slot = self.dslots[q].next()
        key, sem, val = slot
        if val > 0 and e.waited.get(key, 0) < val:
            e.waited[key] = val
            e.rec.append(lambda h, sem=sem, v=val: h.wait_ge(sem, v))
        slot[2] = val + 16
        if nc_ok:
            e.rec.append(lambda h, out=out, in_=in_, sem=sem: h.dma_start(out=out, in_=in_, allow_slow_non_contiguous=True).then_inc(sem, 16))
        else:
            e.rec.append(lambda h, out=out, in_=in_, sem=sem: h.dma_start(out=out, in_=in_).then_inc(sem, 16))
        self._mark((key, val + 16), reads, writes)

    def barrier(self):
        sp = self.E["sp"]
        allk = {}
        for q in ("sp", "pool"):
            for key, sem, val in self.dslots[q].items:
                if val > 0:
                    allk[key] = val
        for n in ("pe", "act", "dve", "pool"):
            if self.E[n].count > 0:
                allk[n] = self.E[n].count
        for key, val in allk.items():
            if sp.waited.get(key, 0) < val:
                sp.waited[key] = val
                sp.rec.append(lambda h, sem=self.semof[key], v=val: h.wait_ge(sem, v))
        self.bar_count += 1
        bc = self.bar_count
        sp.rec.append(lambda h, sem=self.bar_sem: h.sem_inc(sem, 1))
        for n in ("pe", "act", "dve", "pool"):
            e = self.E[n]
            e.rec.append(lambda h, sem=self.bar_sem, v=bc: h.wait_ge(sem, v))
            for key, val in allk.items():
                if e.waited.get(key, 0) < val:
                    e.waited[key] = val

    def phase(self):
        return Phase(self)

    def finish(self):
        e = self.E["sp"]
        for q in ("sp", "pool"):
            for key, sem, val in self.dslots[q].items:
                if val > 0:
                    e.rec.append(lambda h, sem=sem, v=val: h.wait_ge(sem, v))
        for n in ("pe", "act", "dve"):
            en = self.E[n]
            if en.count > 0:
                e.rec.append(lambda h, sem=en.sem, v=en.count: h.wait_ge(sem, v))
        nc = self.nc
        E = self.E
        with nc.Block() as block:
            @block.tensor
            def _(h):
                for f in E["pe"].rec:
                    f(h)

            @block.scalar
            def _(h):
                for f in E["act"].rec:
                    f(h)

            @block.vector
            def _(h):
                for f in E["dve"].rec:
                    f(h)

            @block.gpsimd
            def _(h):
                for f in E["pool"].rec:
                    f(h)

            @block.sync
            def _(h):
                for f in E["sp"].rec:
                    f(h)
        return nc


def _act(out, in_, func, **kw):
    return lambda h: h.activation(out=out, in_=in_, func=func, **kw)


def _tt(out, a, b, op):
    return lambda h: h.tensor_tensor(out=out, in0=a, in1=b, op=op)


def _stt(out, in0, scalar, in1, op0, op1):
    return lambda h: h.scalar_tensor_tensor(out=out, in0=in0, scalar=scalar, in1=in1, op0=op0, op1=op1)


def _ts(out, in0, s1, s2, op0, op1=None):
    if op1 is None:
        return lambda h: h.tensor_scalar(out=out, in0=in0, scalar1=s1, scalar2=None, op0=op0)
    return lambda h: h.tensor_scalar(out=out, in0=in0, scalar1=s1, scalar2=s2, op0=op0, op1=op1)


def _cp(out, in_):
    return lambda h: h.tensor_copy(out=out, in_=in_)


def _mmf(out, lhsT, rhs, start=True, stop=True):
    return lambda h: h.matmul(out, lhsT=lhsT, rhs=rhs, start=start, stop=stop)


def _tp(out, in_, ident):
    return lambda h: h.transpose(out, in_, ident)


class Net:
    def __init__(self, cfg):
        self.cfg = cfg
        k = self.k = K()
        nc = k.nc
        self.ins = {}
        self.outs = {}

        def inp(name, shape, dt=F32):
            self.ins[name] = nc.dram_tensor(name, list(shape), dt, kind="ExternalInput").ap()
            return self.ins[name]

        def outp(name, shape):
            self.outs[name] = nc.dram_tensor(name, list(shape), F32, kind="ExternalOutput").ap()
            return self.outs[name]

        inp("x_all", [T, D])
        inp("cache_k", [2, PAST, 512]); inp("cache_v", [2, PAST, 512])
        inp("state_ssm", [2, 4096, 128]); inp("state_lru", [2, D])
        inp("cond", [2, D])
        inp("ada_w", [DEPTH, D, 6 * D]); inp("ada_b", [DEPTH, 6 * D])
        inp("norm_mix", [DEPTH, D]); inp("norm_ffn", [DEPTH, D])
        inp("attn_w_qkv", [2, D, 3072]); inp("attn_q_norm", [2, 128]); inp("attn_k_norm", [2, 128]); inp("attn_w_o", [2, D, D])
        inp("ssd_w_in", [1, D, 10368]); inp("ssd_conv_w", [1, 4, 6144]); inp("ssd_conv_b", [1, 6144])
        inp("ssd_dt_bias", [1, 128]); inp("ssd_a_log", [1, 128]); inp("ssd_d", [1, 64]); inp("ssd_norm", [1, 4096])
        inp("ssd_w_out", [1, 4096, D])
        inp("lru_w_in", [1, D, 4096]); inp("lru_conv_w", [1, 4, D]); inp("lru_conv_b", [1, D])
        inp("lru_w_a", [1, 2, 16, 128, 128]); inp("lru_b_a", [1, 2, D]); inp("lru_w_i", [1, 2, 16, 128, 128]); inp("lru_b_i", [1, 2, D])
        inp("lru_a_param", [1, 2, D]); inp("lru_w_out", [1, D, D])
        inp("ffn_w_in", [DEPTH, D, 2 * DFF]); inp("ffn_w_out", [DEPTH, DFF, D])
        inp("final_norm", [D])
        inp("c_ident", [128, 128]); inp("c_rot", [128, 128]); inp("c_cos", [128, S_LEN]); inp("c_sin", [128, S_LEN])
        inp("c_tri", [4, 128, 128])
        outp("y_all", [T, D])
        outp("nk", [2, 2, P_LEN, 512]); outp("nv", [2, 2, P_LEN, 512])
        outp("nssm", [2, 2, 4096, 128]); outp("nlru", [2, 2, D])
        if cfg.get("dbg"):
            outp("dbg_xT", [D, T])

        self.xT = k.dram("xT", [D, T], F32)
        self.xT_b = [[Buf() for _ in range(NBLK)] for _ in range(DC)]
        self.qT = k.dram("qT", [D, T], BF16)
        self.kT = k.dram("kT", [512, T], BF16)
        self.vtok = k.dram("vtok", [T, 512], BF16)
        self.oT = k.dram("oT", [4096, T], BF16)
        self.xbcT = k.dram("xbcT", [6144, T], F32)
        self.zsT = k.dram("zsT", [4096, T], BF16)
        self.dtk = k.dram("dtk", [T, 2, 128], F32)
        self.xtok = k.dram("xtok", [T, 4096], BF16)
        self.btok = k.dram("btok", [T, 1024], BF16)
        self.bcT = k.dram("bcT", [2048, T], BF16)
        self.ytok = k.dram("ytok", [2, T, 4096], F32)
        self.ident = k.sb([128, 128], F32, "ident"); self.ident_b = Buf()
        self.identh = k.sb([128, 128], BF16, "identh")
        self.ones_h = k.sb([128, 128], BF16, "ones_h")
        self.ones_f = k.sb([128, 128], F32, "ones_f")
        self.rot_h = k.sb([128, 128], BF16, "rot_h")
        self.cb = Buf()
        k.dma("sp", self.ident[:, :], self.ins["c_ident"][:, :], writes=[self.cb])
        k.dma("pool", self.identh[:, :], self.ins["c_ident"][:, :], writes=[self.cb])
        k.dma("pool", self.rot_h[:, :], self.ins["c_rot"][:, :], writes=[self.cb])
        k.op("dve", lambda h: h.memset(self.ones_h[:, :], 1.0), writes=[self.cb])
        k.op("dve", lambda h: h.memset(self.ones_f[:, :], 1.0), writes=[self.cb])
        self.epsb = k.sb([128, 1], F32, "epsb")
        k.op("dve", lambda h: h.memset(self.epsb[:, :], EPS), writes=[self.cb])
        self.ps = Pool([(nc.alloc_psum_tensor("ps%d" % i, [128, 512], F32), Buf()) for i in range(4)])
        self.psacc = Pool([(nc.alloc_psum_tensor("pa%d" % i, [128, 512], F32), Buf()) for i in range(4)])
        self.zcol = k.sb([128, 1], F32, "zcol")
        k.op("dve", lambda h: h.memset(self.zcol[:, :], 0.0), writes=[self.cb])
        self.t512 = k.sbpool(6, [128, 512], F32, "t512")
        self.rsp = k.sbpool(3, [128, 512], F32, "rsp")
        self.sgp = k.sbpool(3, [128, 512], F32, "sgp")
        self.h512 = k.sbpool(8, [128, 512], BF16, "h512")
        self.xpc = k.sbpool(3, [128, 512], F32, "xpc")
        self.xo = k.sbpool(3, [128, 512], F32, "xo")
        self.modT = k.sb([128, 96, 2], F32, "modT"); self.mod_b = Buf()
        self.gmix = k.sb([128, DEPTH, DC], F32, "gmix")
        self.gffn = k.sb([128, DEPTH, DC], F32, "gffn")
        self.gfin = k.sb([128, DC], F32, "gfin")
        self.AB = k.sb([128, 2, 2, 2, DC], F32, "AB"); self.AB_b = Buf()
        self.gate = k.sb([128, 2, 2, DC], F32, "gate")
        self.adabT = k.sb([128, DEPTH, 96], F32, "adabT")
        k.dma("sp", self.gmix[:, :, :], self.ins["norm_mix"].rearrange("l (c p) -> p l c", p=128), writes=[self.cb], nc_ok=True)
        k.dma("sp", self.gffn[:, :, :], self.ins["norm_ffn"].rearrange("l (c p) -> p l c", p=128), writes=[self.cb], nc_ok=True)
        k.dma("sp", self.gfin[:, :], self.ins["final_norm"].rearrange("(c p) -> p c", p=128), writes=[self.cb], nc_ok=True)
        k.dma("sp", self.adabT[:, :, :], self.ins["ada_b"].rearrange("l (j p) -> p l j", p=128), writes=[self.cb], nc_ok=True)
        self.condT = k.sb([128, DC, 2], F32, "condT")
        self.condh = k.sb([128, DC, 2], BF16, "condh")
        for r in range(2):
            k.dma("sp", self.condT[:, :, r], self.ins["cond"][r].rearrange("(c p) -> p c", p=128), writes=[self.cb], nc_ok=True)
        k.op("act", _act(self.condh[:, :, :], self.condT[:, :, :], AF.Silu), [self.cb], [self.cb])

    def alloc_gemm(self, nw=3, hT=True):
        k = self.k
        self.wpool = k.sbpool(nw, [128, 8192], BF16, "w")
        if hT:
            self.hTf = k.sb([128, 16384], BF16, "hT"); self.hT_b = Buf()
            self.hT = self.hTf[:, :].rearrange("p (k t) -> p k t", k=DC)
            self.hT32 = self.hTf[:, :].rearrange("p (k t) -> p k t", k=32)

    def xbufs(self, t0, n):
        b0 = t0 // 512
        return [self.xT_b[c][b0 + i] for c in range(DC) for i in range(n // 512)]

    def load_x_T(self):
        k = self.k
        with k.phase():
            stage = k.sbpool(2, [128, D], F32, "p0in")
            for tb in range(T // 128):
                st, sbuf = stage.next()
                k.dma("sp", st[:, :], self.ins["x_all"][tb * 128:(tb + 1) * 128, :], writes=[sbuf])
                for cg in range(4):
                    ps, pb = self.ps.next()
                    fns = [_tp(ps[:, j * 128:(j + 1) * 128], st[:, (cg * 4 + j) * 128:(cg * 4 + j + 1) * 128], self.ident[:, :]) for j in range(4)]
                    k.mm(fns, [sbuf, self.cb], [pb])
                    o, ob = self.xo.next()
                    if cg % 2 == 0:
                        k.op("act", _act(o[:, :], ps[:, :], AF.Copy), [pb], [ob])
                    else:
                        k.op("dve", _cp(o[:, :], ps[:, :]), [pb], [ob])
                    dst = self.xT[cg * 512:(cg + 1) * 512, tb * 128:(tb + 1) * 128].rearrange("(c p) t -> p c t", p=128)
                    wb = [self.xT_b[cg * 4 + j][tb // 4] for j in range(4)]
                    k.dma("sp", dst, o[:, :].rearrange("p (c t) -> p c t", c=4), reads=[ob], writes=wb)

    def rstd_from_sumsq(self, ps, pb, n, inv_n):
        k = self.k
        sd, sdb = self.t512.next()
        k.op("act", _act(sd[:, :n], ps[:, :n], AF.Sqrt, scale=inv_n, bias=self.epsb[:, 0:1]), [pb, self.cb], [sdb])
        rs, rsb = self.rsp.next()
        k.op("dve", lambda h: h.reciprocal(out=rs[:, :n], in_=sd[:, :n]), [sdb], [rsb])
        return rs, rsb

    def load_stats(self, tt0):
        k = self.k
        xi, xib = self.xin.next()
        src = self.xT[:, tt0:tt0 + 512].rearrange("(c p) t -> p c t", p=128)
        k.dma("sp", xi[:, :, :], src, reads=self.xbufs(tt0, 512), writes=[xib])
        ps, pb = self.ps.next()
        for c in range(DC):
            sq, sqb = self.h512.next()
            k.op("act", _act(sq[:, :], xi[:, c, :], AF.Square), [xib], [sqb])
            k.mm([_mmf(ps[:, :], self.ones_h[:, :], sq[:, :], start=(c == 0), stop=(c == DC - 1))], [sqb, self.cb], [pb])
        rs, rsb = self.rstd_from_sumsq(ps, pb, 512, 1.0 / D)
        return xi, xib, rs, rsb

    def norm_mod(self, t0, TT, which, r):
        k = self.k
        for blk in range(TT // 512):
            xi, xib, rs, rsb = self.load_stats(t0 + blk * 512)
            for c in range(DC):
                tm, tmb = self.t512.next()
                k.op("dve", _stt(tm[:, :], xi[:, c, :], self.AB[:, which, r, 0, c:c + 1], rs[:, :], ALU.mult, ALU.mult),
                     [xib, rsb, self.AB_b], [tmb])
                k.op("act", _act(self.hT[:, c, blk * 512:(blk + 1) * 512], tm[:, :], AF.Identity, bias=self.AB[:, which, r, 1, c:c + 1], scale=1.0),
                     [tmb, self.AB_b], [self.hT_b])

    def gemm_fm(self, act, act_b, KC, TT, W, groups, cb):
        k = self.k
        Wv = W.rearrange("(k p) n -> p k n", p=128)
        pending = []
        for gi, grp in enumerate(groups):
            G = len(grp)
            assert KC * G * 128 <= 8192
            wt, wb = self.wpool.next()
            wv = wt[:, 0:KC * G * 128].rearrange("p (k n) -> p k n", k=KC)
            s = 0
            while s < G:
                e = s
                while e + 1 < G and grp[e + 1] == grp[e] + 128:
                    e += 1
                n = (e - s + 1) * 128
                k.dma("pool", wv[:, :, s * 128:s * 128 + n], Wv[:, :, grp[s]:grp[s] + n], writes=[wb])
                s = e + 1
            for blk in range(TT // 512):
                for s, col in enumerate(grp):
                    ps, pb = self.ps.next()
                    fns = [_mmf(ps[:, :], wv[:, kk, s * 128:(s + 1) * 128], act[:, kk, blk * 512:(blk + 1) * 512],
                                start=(kk == 0), stop=(kk == KC - 1)) for kk in range(KC)]
                    k.mm(fns, [wb, act_b], [pb])
                    for gen in list(pending):
                        try:
                            next(gen)
                        except StopIteration:
                            pending.remove(gen)
                    r = cb(gi, s, col, blk, ps, pb)
                    if r is not None:
                        try:
                            next(r)
                            pending.append(r)
                        except StopIteration:
                            pass
        while pending:
            for gen in list(pending):
                try:
                    next(gen)
                except StopIteration:
                    pending.remove(gen)

    def resid_cb(self, t0, gate_ap_fn):
        k = self.k

        def cb(gi, s, col, blk, ps, pb):
            c = col // 128
            tt0 = t0 + blk * 512
            xb = self.xT_b[c][tt0 // 512]
            xp, xpb = self.xpc.next()
            k.dma("sp", xp[:, :], self.xT[c * 128:(c + 1) * 128, tt0:tt0 + 512], reads=[xb], writes=[xpb])
            o, ob = self.xo.next()
            k.op("dve", _stt(o[:, :], ps[:, :], gate_ap_fn(c), xp[:, :], ALU.mult, ALU.add), [pb, xpb, self.AB_b], [ob])
            k.dma("sp", self.xT[c * 128:(c + 1) * 128, tt0:tt0 + 512], o[:, :], reads=[ob], writes=[xb])
        return cb

    def adaln(self, l):
        k = self.k
        Wv = self.ins["ada_w"][l].rearrange("(k p) n -> p k n", p=128)
        psT, psTb = self.psacc.next()
        ph = k.phase(); ph.__enter__()
        self.alloc_gemm(hT=False)
        for cb_ in range(24):
            wt, wb = self.wpool.next()
            wv = wt[:, :].rearrange("p (k n) -> p k n", k=DC)
            k.dma("pool", wv[:, :, :], Wv[:, :, cb_ * 512:(cb_ + 1) * 512], writes=[wb])
            ps, pb = self.ps.next()
            fns = [_mmf(ps[0:2, :], self.condh[:, kk, :], wv[:, kk, :], start=(kk == 0), stop=(kk == DC - 1)) for kk in range(DC)]
            k.mm(fns, [wb, self.cb], [pb])
            row, rowb = self.t512.next()
            k.op("dve", _cp(row[0:2, :], ps[0:2, :]), [pb], [rowb])
            fns = [_tp(psT[:, (cb_ * 4 + j) * 2:(cb_ * 4 + j + 1) * 2], row[0:2, j * 128:(j + 1) * 128], self.ident[0:2, 0:2]) for j in range(4)]
            k.mm(fns, [rowb, self.cb], [psTb])
        for r in range(2):
            k.op("dve", _tt(self.modT[:, :, r], psT[:, 0:192].rearrange("p (j r) -> p j r", r=2)[:, :, r], self.adabT[:, l, :], ALU.add),
                 [psTb, self.cb], [self.mod_b, self.AB_b])
        for which, (gt, o_sh, o_sc, o_g) in enumerate(((self.gmix, 0, 16, 32), (self.gffn, 48, 64, 80))):
            for r in range(2):
                k.op("dve", _stt(self.AB[:, which, r, 0, :], self.modT[:, o_sc:o_sc + 16, r], 1.0, gt[:, l, :], ALU.add, ALU.mult),
                     [self.mod_b, self.cb], [self.AB_b])
                k.op("dve", _cp(self.AB[:, which, r, 1, :], self.modT[:, o_sh:o_sh + 16, r]), [self.mod_b], [self.AB_b])
                k.op("dve", _cp(self.gate[:, which, r, :], self.modT[:, o_g:o_g + 16, r]), [self.mod_b], [self.AB_b])
        ph.__exit__(None, None, None)

    def ffn(self, l):
        k = self.k
        Win = self.ins["ffn_w_in"][l]
        Wout = self.ins["ffn_w_out"][l]
        with k.phase():
            self.alloc_gemm()
            self.xin = k.sbpool(1, [128, DC, 512], F32, "xin")
            big = k.sb([128, 22 * 1024], BF16, "big")
            big_b = Buf()
            for (t0, TT) in TILES:
                r = 1 if t0 >= S_LEN else 0
                self.norm_mod(t0, TT, 1, r)
                if self.cfg.get("dbg_hT"):
                    for c in range(DC):
                        tm, tmb = self.t512.next()
                        k.op("dve", _cp(tm[:, :], self.hT[:, c, 0:512]), [self.hT_b], [tmb])
                        k.dma("sp", self.outs["dbg_xT"][c * 128:(c + 1) * 128, t0:t0 + 512], tm[:, :], reads=[tmb])
                    continue
                hid = big[:, 0:22 * TT].rearrange("p (j t) -> p j t", j=22)
                for half in range(2):
                    state = {}

                    def cb_in(gi, s, col, blk, ps, pb, state=state, hid=hid):
                        if s == 0:
                            sg, sgb = self.sgp.next()
                            k.op("act", _act(sg[:, :], ps[:, :], AF.Silu), [pb], [sgb])
                            state["sg"] = (sg, sgb)
                        else:
                            sg, sgb = state["sg"]
                            k.op("dve", _tt(hid[:, gi, blk * 512:(blk + 1) * 512], ps[:, :], sg[:, :], ALU.mult), [pb, sgb], [big_b])
                    groups = [[(half * 22 + j) * 128, DFF + (half * 22 + j) * 128] for j in range(22)]
                    self.gemm_fm(self.hT, self.hT_b, DC, TT, Win, groups, cb_in)
                    Wo = Wout[half * 2816:(half + 1) * 2816, :]
                    groups = [[c * 128, (c + 1) * 128] for c in range(0, DC, 2)]
                    self.gemm_fm(hid, big_b, 22, TT, Wo, groups, self.resid_cb(t0, lambda c, r=r: self.gate[:, 1, r, c:c + 1]))

    def final(self):
        k = self.k
        with k.phase():
            self.xin = k.sbpool(1, [128, DC, 512], F32, "xin")
            p0out = k.sbpool(2, [128, D], F32, "p0out")
            for blk in range(NBLK):
                tt0 = blk * 512
                xi, xib, rs, rsb = self.load_stats(tt0)
                for c in range(DC):
                    k.op("dve", _stt(xi[:, c, :], xi[:, c, :], self.gfin[:, c:c + 1], rs[:, :], ALU.mult, ALU.mult), [xib, rsb, self.cb], [xib])
                for tb in range(4):
                    o, ob = p0out.next()
                    for cg in range(4):
                        ps, pb = self.ps.next()
                        fns = [_tp(ps[:, j * 128:(j + 1) * 128], xi[:, cg * 4 + j, tb * 128:(tb + 1) * 128], self.ident[:, :]) for j in range(4)]
                        k.mm(fns, [xib, self.cb], [pb])
                        if cg % 2 == 0:
                            k.op("act", _act(o[:, cg * 512:(cg + 1) * 512], ps[:, :], AF.Copy), [pb], [ob])
                        else:
                            k.op("dve", _cp(o[:, cg * 512:(cg + 1) * 512], ps[:, :]), [pb], [ob])
                    k.dma("sp", self.outs["y_all"][tt0 + tb * 128:tt0 + (tb + 1) * 128, :], o[:, :], reads=[ob])

    def dbg_dump(self):
        k = self.k
        with k.phase():
            self.xin = k.sbpool(1, [128, DC, 512], F32, "xin")
            for blk in range(NBLK):
                tt0 = blk * 512
                xi, xib = self.xin.next()
                k.dma("sp", xi[:, :, :], self.xT[:, tt0:tt0 + 512].rearrange("(c p) t -> p c t", p=128), reads=self.xbufs(tt0, 512), writes=[xib])
                k.dma("sp", self.outs["dbg_xT"][:, tt0:tt0 + 512].rearrange("(c p) t -> p c t", p=128), xi[:, :, :], reads=[xib])

    def build(self):
        cfg = self.cfg
        k = self.k
        self.load_x_T()
        for l in cfg.get("layers", range(DEPTH)):
            self.adaln(l)
            if cfg.get("mixers", True):
                kind = l % 3
                if kind == 0:
                    self.attention(l)
                elif kind == 1:
                    self.ssd(l)
                else:
                    self.lru(l)
            if cfg.get("ffn", True):
                self.ffn(l)
        if cfg.get("dbg") and not cfg.get("dbg_hT"):
            self.dbg_dump()
        self.final()
        return k.finish()

    def attention(self, l):
        k = self.k
        jl = l // 3
        Wqkv = self.ins["attn_w_qkv"][jl]
        Wo = self.ins["attn_w_o"][jl]
        nk_o, nv_o = self.outs["nk"], self.outs["nv"]
        with k.phase():
            self.alloc_gemm()
            self.xin = k.sbpool(1, [128, DC, 512], F32, "xin")
            cosT = k.sb([128, S_LEN], F32, "cosT")
            sinT = k.sb([128, S_LEN], F32, "sinT")
            qkg = k.sb([128, 2], F32, "qkg")
            tb_ = Buf()
            k.dma("sp", cosT[:, :], self.ins["c_cos"][:, :], writes=[tb_])
            k.dma("sp", sinT[:, :], self.ins["c_sin"][:, :], writes=[tb_])
            k.dma("sp", qkg[:, 0:1], self.ins["attn_q_norm"][jl].rearrange("(p o) -> p o", o=1), writes=[tb_], nc_ok=True)
            k.dma("sp", qkg[:, 1:2], self.ins["attn_k_norm"][jl].rearrange("(p o) -> p o", o=1), writes=[tb_], nc_ok=True)
            qn_p = k.sbpool(2, [128, 512], F32, "qn")
            for (t0, TT) in self.cfg.get("tiles", TILES):
                prompt = t0 >= S_LEN
                r = 1 if prompt else 0
                self.norm_mod(t0, TT, 0, r)

                def cb(gi, s, col, blk, ps, pb, t0=t0, prompt=prompt):
                    j = col // 128
                    tt0 = t0 + blk * 512
                    if j < 20:
                        sq, sqb = self.h512.next()
                        k.op("act", _act(sq[:, :], ps[:, :], AF.Square), [pb], [sqb])
                        yield
                        ps2, pb2 = self.psacc.next()
                        k.mm([_mmf(ps2[:, :], self.ones_h[:, :], sq[:, :])], [sqb, self.cb], [pb2])
                        rs, rsb = self.rstd_from_sumsq(ps2, pb2, 512, 1.0 / 128)
                        qn, qnb = qn_p.next()
                        gi_ = 0 if j < 16 else 1
                        k.op("dve", _stt(qn[:, :], ps[:, :], qkg[:, gi_:gi_ + 1], rs[:, :], ALU.mult, ALU.mult), [pb, rsb, tb_], [qnb])
                        o, ob = self.h512.next()
                        if not prompt:
                            qh, qhb = self.h512.next()
                            k.op("act", _act(qh[:, :], qn[:, :], AF.Copy), [qnb], [qhb])
                            yield
                            ps3, pb3 = self.psacc.next()
                            k.mm([_mmf(ps3[:, :], self.rot_h[:, :], qh[:, :])], [qhb, self.cb], [pb3])
                            t1, t1b = self.t512.next()
                            k.op("dve", _tt(t1[:, :], qn[:, :], cosT[:, tt0:tt0 + 512], ALU.mult), [qnb, tb_], [t1b])
                            t2, t2b = self.t512.next()
                            k.op("dve", _tt(t2[:, :], ps3[:, :], sinT[:, tt0:tt0 + 512], ALU.mult), [pb3, tb_], [t2b])
                            k.op("dve", _tt(o[:, :], t1[:, :], t2[:, :], ALU.add), [t1b, t2b], [ob])
                        else:
                            k.op("act", _act(o[:, :], qn[:, :], AF.Copy), [qnb], [ob])
                            if j >= 16:
                                hh = j - 16
                                ps4, pb4 = self.psacc.next()
                                k.mm([_tp(ps4[:, i * 128:(i + 1) * 128], qn[:, i * 128:(i + 1) * 128], self.ident[:, :]) for i in range(4)], [qnb, self.cb], [pb4])
                                kf, kfb = self.t512.next()
                                k.op("act", _act(kf[:, :], ps4[:, :], AF.Copy), [pb4], [kfb])
                                for sq_ in range(2):
                                    k.dma("sp", nk_o[sq_, jl, :, hh * 128:(hh + 1) * 128].rearrange("(b p) f -> p b f", p=128),
                                          kf[:, sq_ * 256:(sq_ + 1) * 256].rearrange("p (b f) -> p b f", b=2), reads=[kfb])
                        dst = self.qT[j * 128:(j + 1) * 128, tt0:tt0 + 512] if j < 16 else self.kT[(j - 16) * 128:(j - 15) * 128, tt0:tt0 + 512]
                        k.dma("sp", dst, o[:, :], reads=[ob])
                    else:
                        hh = j - 20
                        vt, vtb = self.t512.next()
                        k.op("act", _act(vt[:, :], ps[:, :], AF.Copy), [pb], [vtb])
                        ps4, pb4 = self.ps.next()
                        k.mm([_tp(ps4[:, i * 128:(i + 1) * 128], vt[:, i * 128:(i + 1) * 128], self.ident[:, :]) for i in range(4)], [vtb, self.cb], [pb4])
                        vb, vbb = self.h512.next()
                        k.op("dve", _cp(vb[:, :], ps4[:, :]), [pb4], [vbb])
                        k.dma("sp", self.vtok[tt0:tt0 + 512, hh * 128:(hh + 1) * 128].rearrange("(b p) f -> p b f", p=128),
                              vb[:, :].rearrange("p (b f) -> p b f", b=4), reads=[vbb])
                        if prompt and self.cfg.get("vmode", 2) > 0:
                            vf, vfb = self.t512.next()
                            if self.cfg.get("vmode", 2) == 1:
                                k.op("act", _act(vf[:, :], ps4[:, :], AF.Copy), [pb4], [vfb])
                            else:
                                k.op("dve", _cp(vf[:, :], ps4[:, :]), [pb4], [vfb])
                            for sq_ in range(2):
                                k.dma("sp", nv_o[sq_, jl, :, hh * 128:(hh + 1) * 128].rearrange("(b p) f -> p b f", p=128),
                                      vf[:, sq_ * 256:(sq_ + 1) * 256].rearrange("p (b f) -> p b f", b=2), reads=[vfb])
                groups = [[(g4 * 4 + i) * 128 for i in range(4)] for g4 in self.cfg.get("qkv_groups", range(6))]
                self.gemm_fm(self.hT, self.hT_b, DC, TT, Wqkv, groups, cb)
        if self.cfg.get("attn_stop") == 1:
            return
        with k.phase():
            KTp = k.sbpool(2, [128, S_LEN + PAST], BF16, "KT")
            Vp = k.sbpool(2, [128, 36, 128], BF16, "Vt")
            ck = k.sb([128, 4, 512], F32, "ck")
            ckb = Buf()
            k.dma("sp", ck[:, :, :], self.ins["cache_k"][jl].rearrange("(b p) f -> p b f", p=128), writes=[ckb])
            QTp = k.sbpool(3, [128, 512], BF16, "QT")
            PTp = k.sbpool(4, [128, 512], BF16, "PT")
            scale = 1.0 / np.sqrt(128.0)
            jobs = [(0, 32, 0, S_LEN, True), (S_LEN, 2, S_LEN, P_LEN, False), (S_LEN + P_LEN, 2, S_LEN + P_LEN, P_LEN, False)]
            for (k0, nkb0, q0, qlen, ctx) in jobs:
                nkb = nkb0 + (4 if ctx else 0)
                QB = min(512, qlen)
                for g in range(4):
                    KT, KTb = KTp.next()
                    V, Vb = Vp.next()
                    k.dma("sp", KT[:, 0:nkb0 * 128], self.kT[g * 128:(g + 1) * 128, k0:k0 + nkb0 * 128], writes=[KTb])
                    k.dma("sp", V[:, 0:nkb0, :], self.vtok[k0:k0 + nkb0 * 128, g * 128:(g + 1) * 128].rearrange("(b p) f -> p b f", p=128), writes=[Vb])
                    if ctx:
                        ps4, pb4 = self.ps.next()
                        k.mm([_tp(ps4[:, i * 128:(i + 1) * 128], ck[:, i, g * 128:(g + 1) * 128], self.ident[:, :]) for i in range(4)], [ckb, self.cb], [pb4])
                        k.op("act", _act(KT[:, nkb0 * 128:nkb * 128], ps4[:, :], AF.Copy), [pb4], [KTb])
                        k.dma("pool", V[:, nkb0:nkb, :], self.ins["cache_v"][jl][:, g * 128:(g + 1) * 128].rearrange("(b p) f -> p b f", p=128), writes=[Vb])
                    its = [(h, qb) for h in range(4) for qb in range(qlen // QB)]

                    def load_q(h, qb, g=g, q0=q0, QB=QB):
                        QT, QTb = QTp.next()
                        k.dma("sp", QT[:, 0:QB], self.qT[(g * 4 + h) * 128:(g * 4 + h + 1) * 128, q0 + qb * QB:q0 + (qb + 1) * QB], writes=[QTb])
                        return QT, QTb
                    nq = load_q(*its[0])
                    for ii, (h, qb) in enumerate(its):
                        hd = g * 4 + h
                        if True:
                            qq0 = q0 + qb * QB
                            QT, QTb = nq
                            if ii + 1 < len(its):
                                nq = load_q(*its[ii + 1])
                            psO, pbO = self.psacc.next()
                            psD, pbD = self.psacc.next()
                            prev = None
                            for kb in range(nkb + 1):
                                cur = None
                                if kb < nkb:
                                    psS, pbS = self.ps.next()
                                    k.mm([_mmf(psS[:, 0:QB], KT[:, kb * 128:(kb + 1) * 128], QT[:, 0:QB])], [KTb, QTb], [pbS])
                                    PT, PTb = PTp.next()
                                    k.op("act", _act(PT[:, 0:QB], psS[:, 0:QB], AF.Exp, scale=float(scale)), [pbS], [PTb])
                                    cur = (PT, PTb, kb)
                                if prev is not None:
                                    PTq, PTqb, kq = prev
                                    k.mm([_mmf(psO[:, 0:QB], V[:, kq, :], PTq[:, 0:QB], start=(kq == 0), stop=(kq == nkb - 1)),
                                          _mmf(psD[:, 0:QB], self.ones_h[:, :], PTq[:, 0:QB], start=(kq == 0), stop=(kq == nkb - 1))],
                                         [Vb, PTqb, self.cb], [pbO, pbD])
                                prev = cur
                            rc, rcb = self.t512.next()
                            k.op("dve", lambda h, rc=rc, psD=psD, QB=QB: h.reciprocal(out=rc[:, 0:QB], in_=psD[:, 0:QB]), [pbD], [rcb])
                            o, ob = self.h512.next()
                            k.op("dve", _tt(o[:, 0:QB], psO[:, 0:QB], rc[:, 0:QB], ALU.mult), [pbO, rcb], [ob])
                            k.dma("sp", self.oT[hd * 128:(hd + 1) * 128, qq0:qq0 + QB], o[:, 0:QB], reads=[ob])
        if self.cfg.get("attn_stop") == 2:
            return
        self.out_proj(Wo, DC)

    def out_proj(self, Wo, KC):
        k = self.k
        with k.phase():
            self.alloc_gemm()
            TT = 16384 // KC
            G = 8192 // (KC * 128)
            act = self.hT if KC == DC else self.hT32
            for t0 in range(0, T, TT):
                r = 1 if t0 >= S_LEN else 0
                TTl = min(TT, T - t0)
                for h2 in range(KC // DC):
                    k.dma("sp", act[:, h2 * DC:(h2 + 1) * DC, 0:TTl],
                          self.oT[h2 * D:(h2 + 1) * D, t0:t0 + TTl].rearrange("(c p) t -> p c t", p=128), writes=[self.hT_b])
                groups = [[(g4 * G + i) * 128 for i in range(G)] for g4 in range(DC // G)]
                self.gemm_fm(act, self.hT_b, KC, TTl, Wo, groups, self.resid_cb(t0, lambda c, r=r: self.gate[:, 0, r, c:c + 1]))

    def conv_chunk(self, src, pin, pinb, acc, accb, cw, cbias, c, prm):
        k = self.k
        if src is not None:
            for (t0, L, q0) in SEGQ:
                k.dma("sp", pin[:, q0:q0 + L], src[:, t0:t0 + L], writes=[pinb])
        N = PADT - 3
        k.op("dve", _ts(acc[:, 1:1 + N], pin[:, 0:N], cw[:, 0, c:c + 1], cbias[:, c:c + 1], ALU.mult, ALU.add), [pinb, prm], [accb])
        for kk in range(1, 4):
            k.op("dve", _stt(acc[:, 1:1 + N], pin[:, kk:kk + N], cw[:, kk, c:c + 1], acc[:, 1:1 + N], ALU.mult, ALU.add), [pinb, prm, accb], [accb])

    def conv_params(self, wname, bname, jl, nch, prm):
        k = self.k
        cw = k.sb([128, 4, nch], F32, "cw")
        cbias = k.sb([128, nch], F32, "cbias")
        for kk in range(4):
            k.dma("sp", cw[:, kk, :], self.ins[wname][jl, kk].rearrange("(c p) -> p c", p=128), writes=[prm], nc_ok=True)
        k.dma("sp", cbias[:, :], self.ins[bname][jl].rearrange("(c p) -> p c", p=128), writes=[prm], nc_ok=True)
        return cw, cbias

    def lru(self, l):
        k = self.k
        jl = l // 3
        Win = self.ins["lru_w_in"][jl]
        Wout = self.ins["lru_w_out"][jl]
        recT, gT = self.xbcT, self.zsT
        with k.phase():
            self.alloc_gemm()
            self.xin = k.sbpool(1, [128, DC, 512], F32, "xin")
            for (t0, TT) in TILES:
                r = 1 if t0 >= S_LEN else 0
                self.norm_mod(t0, TT, 0, r)

                def cb(gi, s, col, blk, ps, pb, t0=t0):
                    j = col // 128
                    tt0 = t0 + blk * 512
                    if j < 16:
                        o, ob = self.h512.next()
                        k.op("act", _act(o[:, :], ps[:, :], AF.Gelu_apprx_tanh), [pb], [ob])
                        k.dma("sp", gT[j * 128:(j + 1) * 128, tt0:tt0 + 512], o[:, :], reads=[ob])
                    else:
                        o, ob = self.xo.next()
                        k.op("dve", _cp(o[:, :], ps[:, :]), [pb], [ob])
                        k.dma("sp", recT[(j - 16) * 128:(j - 15) * 128, tt0:tt0 + 512], o[:, :], reads=[ob])
                groups = [[(g4 * 4 + i) * 128 for i in range(4)] for g4 in range(8)]
                self.gemm_fm(self.hT, self.hT_b, DC, TT, Win, groups, cb)
        with k.phase():
            N = PADT - 3
            prm = Buf()
            pin = k.sb([128, PADT], F32, "pin"); pinb = Buf()
            k.op("dve", lambda h: h.memset(pin[:, :], 0.0), writes=[pinb])
            acc = k.sb([128, PADT], F32, "acc"); accb = Buf()
            recb = k.sb([128, PADT], BF16, "recb"); recbb = Buf()
            R = k.sb([128, PADT], F32, "R"); Rb = Buf()
            I = k.sb([128, PADT], F32, "I"); Ib = Buf()
            Hf = k.sb([128, PADT], F32, "Hf"); Hfb = Buf()
            Hb = k.sb([128, PADT], F32, "Hb"); Hbb = Buf()
            gpad = k.sb([128, PADT], BF16, "gpad"); gpb = Buf()
            yb = k.sb([128, PADT], BF16, "yb"); ybb = Buf()
            cw, cbias = self.conv_params("lru_conv_w", "lru_conv_b", jl, 16, prm)
            ba = k.sb([128, 2, 16], F32, "ba"); bi = k.sb([128, 2, 16], F32, "bi")
            apm = k.sb([128, 32], F32, "apm"); h0 = k.sb([128, 2, 16], F32, "h0")
            wa = k.sb([128, 32, 128], BF16, "wa"); wi = k.sb([128, 32, 128], BF16, "wi")
            for d in range(2):
                k.dma("sp", ba[:, d, :], self.ins["lru_b_a"][jl, d].rearrange("(c p) -> p c", p=128), writes=[prm], nc_ok=True)
                k.dma("sp", bi[:, d, :], self.ins["lru_b_i"][jl, d].rearrange("(c p) -> p c", p=128), writes=[prm], nc_ok=True)
                k.dma("sp", apm[:, d * 16:(d + 1) * 16], self.ins["lru_a_param"][jl, d].rearrange("(c p) -> p c", p=128), writes=[prm], nc_ok=True)
                k.dma("sp", h0[:, d, :], self.ins["state_lru"][d].rearrange("(c p) -> p c", p=128), writes=[prm], nc_ok=True)
                k.dma("pool", wa[:, d * 16:(d + 1) * 16, :], self.ins["lru_w_a"][jl, d].rearrange("c w v -> w c v"), writes=[prm])
                k.dma("pool", wi[:, d * 16:(d + 1) * 16, :], self.ins["lru_w_i"][jl, d].rearrange("c w v -> w c v"), writes=[prm])
            t1 = k.sb([128, 32], F32, "lt1"); t2 = k.sb([128, 32], F32, "lt2")
            nsp8 = k.sb([128, 32], F32, "nsp8"); nsp16 = k.sb([128, 32], F32, "nsp16")
            k.op("act", _act(t1[:, :], apm[:, :], AF.Abs), [prm], [prm])
            k.op("act", _act(t1[:, :], t1[:, :], AF.Exp, scale=-1.0), [prm], [prm])
            k.op("act", _act(t1[:, :], t1[:, :], AF.Ln, bias=self.ones_f[:, 0:1], scale=1.0), [prm, self.cb], [prm])
            k.op("dve", _ts(t2[:, :], apm[:, :], -1.0, 0.0, ALU.mult, ALU.max), [prm], [prm])
            k.op("dve", _tt(t2[:, :], t2[:, :], t1[:, :], ALU.add), [prm], [prm])
            k.op("dve", _ts(nsp8[:, :], t2[:, :], -8.0, None, ALU.mult), [prm], [prm])
            k.op("dve", _ts(nsp16[:, :], t2[:, :], -16.0, None, ALU.mult), [prm], [prm])
            fin = k.sb([128, 2, 2, 16], F32, "fin"); finb = Buf()
            for c in range(16):
                self.conv_chunk(recT[c * 128:(c + 1) * 128, :], pin, pinb, acc, accb, cw, cbias, c, prm)
                k.op("act", _act(recb[:, 1:1 + N], acc[:, 1:1 + N], AF.Copy), [accb], [recbb])
                for (t0, L, q0) in SEGQ:
                    k.dma("sp", gpad[:, q0:q0 + L], gT[c * 128:(c + 1) * 128, t0:t0 + L], writes=[gpb])
                for d in range(2):
                    dc_ = d * 16 + c
                    for q in range(1, 1 + N, 512):
                        w = min(512, 1 + N - q)
                        ps, pb = self.ps.next()
                        k.mm([_mmf(ps[:, 0:w], wa[:, dc_, :], recb[:, q:q + w])], [prm, recbb], [pb])
                        k.op("act", _act(R[:, q:q + w], ps[:, 0:w], AF.Sigmoid, bias=ba[:, d, c:c + 1], scale=1.0), [pb, prm], [Rb])
                        ps2, pb2 = self.ps.next()
                        k.mm([_mmf(ps2[:, 0:w], wi[:, dc_, :], recb[:, q:q + w])], [prm, recbb], [pb2])
                        k.op("act", _act(I[:, q:q + w], ps2[:, 0:w], AF.Sigmoid, bias=bi[:, d, c:c + 1], scale=1.0), [pb2, prm], [Ib])
                    S, Sb = (Hf, Hfb) if d == 0 else (Hb, Hbb)
                    k.op("act", _act(S[:, 1:1 + N], R[:, 1:1 + N], AF.Exp, scale=nsp16[:, dc_:dc_ + 1]), [Rb, prm], [Sb])
                    k.op("act", _act(S[:, 1:1 + N], S[:, 1:1 + N], AF.Sqrt, bias=self.ones_f[:, 0:1], scale=-1.0), [Sb, self.cb], [Sb])
                    k.op("act", _act(R[:, 1:1 + N], R[:, 1:1 + N], AF.Exp, scale=nsp8[:, dc_:dc_ + 1]), [Rb, prm], [Rb])
                    k.op("dve", _tt(I[:, 1:1 + N], I[:, 1:1 + N], S[:, 1:1 + N], ALU.mult), [Ib, Sb], [Ib])
                    k.op("dve", _tt(I[:, 1:1 + N], I[:, 1:1 + N], acc[:, 1:1 + N], ALU.mult), [Ib, accb], [Ib])
                    H, Hbuf = (Hf, Hfb) if d == 0 else (Hb, Hbb)
                    for si, (t0, L, q0) in enumerate(SEGQ):
                        init = h0[:, d, c:c + 1] if si == 0 else self.zcol[:, 0:1]
                        if d == 0:
                            o_, a_, u_ = H[:, q0:q0 + L], R[:, q0:q0 + L], I[:, q0:q0 + L]
                        else:
                            o_, a_, u_ = H[:, q0 + L - 1:q0 - 1:-1], R[:, q0 + L - 1:q0 - 1:-1], I[:, q0 + L - 1:q0 - 1:-1]
                        k.op("dve", lambda h, o_=o_, a_=a_, u_=u_, init=init: h.tensor_tensor_scan(
                            out=o_, data0=a_, data1=u_, initial=init, op0=ALU.mult, op1=ALU.add), [Rb, Ib, prm, self.cb], [Hbuf])
                        if si > 0:
                            src = H[:, q0 + L - 1:q0 + L] if d == 0 else H[:, q0:q0 + 1]
                            k.op("dve", _cp(fin[:, si - 1, d, c:c + 1], src), [Hbuf], [finb])
                k.op("dve", _tt(Hf[:, 1:1 + N], Hf[:, 1:1 + N], Hb[:, 1:1 + N], ALU.add), [Hfb, Hbb], [Hfb])
                k.op("dve", _tt(yb[:, 1:1 + N], Hf[:, 1:1 + N], gpad[:, 1:1 + N], ALU.mult), [Hfb, gpb], [ybb])
                for (t0, L, q0) in SEGQ:
                    k.dma("sp", self.oT[c * 128:(c + 1) * 128, t0:t0 + L], yb[:, q0:q0 + L], reads=[ybb])
            for pr in range(2):
                for d in range(2):
                    k.dma("sp", self.outs["nlru"][pr, d].rearrange("(c p) -> p c", p=128), fin[:, pr, d, :], reads=[finb], nc_ok=True)
        self.out_proj(Wout, DC)

    def ssd(self, l):
        k = self.k
        jl = l // 3
        Win = self.ins["ssd_w_in"][jl]
        Wout = self.ins["ssd_w_out"][jl]
        N = PADT - 3

        def v3(ap, a):
            return ap.rearrange("p (a b) -> p a b", a=a)

        with k.phase():
            self.alloc_gemm()
            self.xin = k.sbpool(1, [128, DC, 512], F32, "xin")
            prm = Buf()
            dtb = k.sb([128, 1], F32, "dtb"); nea = k.sb([128, 1], F32, "nea")
            k.dma("sp", dtb[:, :], self.ins["ssd_dt_bias"][jl].rearrange("(p o) -> p o", o=1), writes=[prm], nc_ok=True)
            k.dma("sp", nea[:, :], self.ins["ssd_a_log"][jl].rearrange("(p o) -> p o", o=1), writes=[prm], nc_ok=True)
            k.op("act", _act(nea[:, :], nea[:, :], AF.Exp), [prm], [prm])
            for (t0, TT) in TILES:
                r = 1 if t0 >= S_LEN else 0
                self.norm_mod(t0, TT, 0, r)

                def cb(gi, s, col, blk, ps, pb, t0=t0):
                    j = col // 128
                    tt0 = t0 + blk * 512
                    if j < 32:
                        o, ob = self.h512.next()
                        k.op("act", _act(o[:, :], ps[:, :], AF.Silu), [pb], [ob])
                        k.dma("sp", self.zsT[j * 128:(j + 1) * 128, tt0:tt0 + 512], o[:, :], reads=[ob])
                    elif j < 80:
                        o, ob = self.xo.next()
                        k.op("dve", _cp(o[:, :], ps[:, :]), [pb], [ob])
                        k.dma("sp", self.xbcT[(j - 32) * 128:(j - 31) * 128, tt0:tt0 + 512], o[:, :], reads=[ob])
                    else:
                        xv, xvb = self.t512.next()
                        k.op("act", _act(xv[:, :], ps[:, :], AF.Identity, bias=dtb[:, 0:1], scale=1.0), [pb, prm], [xvb])
                        ax, axb = self.t512.next()
                        k.op("act", _act(ax[:, :], xv[:, :], AF.Abs), [xvb], [axb])
                        k.op("act", _act(ax[:, :], ax[:, :], AF.Exp, scale=-1.0), [axb], [axb])
                        k.op("act", _act(ax[:, :], ax[:, :], AF.Ln, bias=self.ones_f[:, 0:1], scale=1.0), [axb, self.cb], [axb])
                        dtv, dtvb = self.t512.next()
                        k.op("dve", _stt(dtv[:, :], xv[:, :], 0.0, ax[:, :], ALU.max, ALU.add), [xvb, axb], [dtvb])
                        dav, davb = self.t512.next()
                        k.op("dve", _ts(dav[:, :], dtv[:, :], nea[:, 0:1], -1.0, ALU.mult, ALU.mult), [dtvb, prm], [davb])
                        for which, (src, srcb) in enumerate(((dtv, dtvb), (dav, davb))):
                            ps4, pb4 = self.ps.next()
                            k.mm([_tp(ps4[:, i * 128:(i + 1) * 128], src[:, i * 128:(i + 1) * 128], self.ident[:, :]) for i in range(4)], [srcb, self.cb], [pb4])
                            o, ob = self.xo.next()
                            k.op("dve", _cp(o[:, :], ps4[:, :]), [pb4], [ob])
                            k.dma("sp", self.dtk[tt0:tt0 + 512, which, :].rearrange("(b p) f -> p b f", p=128),
                                  o[:, :].rearrange("p (b f) -> p b f", b=4), reads=[ob])
                groups = [[(g4 * 4 + i) * 128 for i in range(4)] for g4 in range(20)] + [[80 * 128]]
                self.gemm_fm(self.hT, self.hT_b, DC, TT, Win, groups, cb)
        if self.cfg.get("ssd_stop") == 1:
            return
        with k.phase():
            prm = Buf()
            pins = []
            for i_ in range(2):
                pin_ = k.sb([128, PADT], F32, "pin"); pinb_ = Buf()
                k.op("dve", lambda h, pin_=pin_: h.memset(pin_[:, :], 0.0), writes=[pinb_])
                pins.append((pin_, pinb_))
            acc = k.sb([128, PADT], F32, "acc"); accb = Buf()
            sil = k.sb([128, PADT], F32, "sil"); silb_ = Buf()
            silh = k.sb([128, PADT], BF16, "silh"); silhb = Buf()
            cw, cbias = self.conv_params("ssd_conv_w", "ssd_conv_b", jl, 48, prm)

            def pin_load(cc):
                pin_, pinb_ = pins[cc % 2]
                for (t0, L, q0) in SEGQ:
                    k.dma("sp", pin_[:, q0:q0 + L], self.xbcT[cc * 128:(cc + 1) * 128, t0:t0 + L], writes=[pinb_])
            pin_load(0)
            for cc in range(48):
                pin, pinb = pins[cc % 2]
                if cc + 1 < 48:
                    pin_load(cc + 1)
                self.conv_chunk(None, pin, pinb, acc, accb, cw, cbias, cc, prm)
                if cc < 40:
                    k.op("act", _act(sil[:, 1:1 + N], acc[:, 1:1 + N], AF.Silu), [accb], [silb_])
                    for grp in range(T // 512):
                        ps4, pb4 = self.ps.next()
                        fns = []
                        for i in range(4):
                            t = (grp * 4 + i) * 128
                            q = t + 1 + 2 * (0 if t < S_LEN else (1 if t < S_LEN + P_LEN else 2))
                            fns.append(_tp(ps4[:, i * 128:(i + 1) * 128], sil[:, q:q + 128], self.ident[:, :]))
                        k.mm(fns, [silb_, self.cb], [pb4])
                        o, ob = self.h512.next()
                        if grp % 2 == 0:
                            k.op("dve", _cp(o[:, :], ps4[:, :]), [pb4], [ob])
                        else:
                            k.op("act", _act(o[:, :], ps4[:, :], AF.Copy), [pb4], [ob])
                        dst = self.xtok[grp * 512:(grp + 1) * 512, cc * 128:(cc + 1) * 128] if cc < 32 else \
                            self.btok[grp * 512:(grp + 1) * 512, (cc - 32) * 128:(cc - 31) * 128]
                        k.dma("sp", dst.rearrange("(b p) f -> p b f", p=128), o[:, :].rearrange("p (b f) -> p b f", b=4), reads=[ob])
                if cc >= 32:
                    k.op("act", _act(silh[:, 1:1 + N], acc[:, 1:1 + N], AF.Silu), [accb], [silhb])
                    for (t0, L, q0) in SEGQ:
                        k.dma("sp", self.bcT[(cc - 32) * 128:(cc - 31) * 128, t0:t0 + L], silh[:, q0:q0 + L], reads=[silhb])
        if self.cfg.get("ssd_stop") == 2:
            return
        with k.phase():
            prm = Buf()
            tri = k.sb([128, 4, 128], F32, "tri")
            for i in range(4):
                k.dma("sp", tri[:, i, :], self.ins["c_tri"][i], writes=[prm])
            stT = k.sb([128, 4096], F32, "stT"); stTb = [Buf() for _ in range(8)]
            stH = k.sb([128, 4096], BF16, "stH"); stHb = [Buf() for _ in range(8)]
            hst = k.sb([128, 32, 128], F32, "hst"); hstb = Buf()
            xt_p = k.sbpool(3, [128, 4096], BF16, "xt")
            xdt_p = k.sbpool(2, [128, 4096], BF16, "xdt")
            xw_p = k.sbpool(2, [128, 4096], BF16, "xw")
            bt_p = k.sbpool(3, [128, 1024], BF16, "bt")
            bT_p = k.sbpool(3, [128, 8, 128], BF16, "bT")
            cT_p = k.sbpool(3, [128, 8, 128], BF16, "cT")
            dd_p = k.sbpool(3, [128, 2, 128], F32, "dd")
            sc_p = k.sbpool(2, [128, 192], F32, "sc")
            dU_p = k.sbpool(4, [128, 512], F32, "dU")
            E_p = k.sbpool(6, [128, 512], F32, "E")
            MT_p = k.sbpool(4, [128, 512], BF16, "MT")
            SM_p = k.sbpool(4, [128, 128], F32, "SM")
            y_p = k.sbpool(3, [128, 512], F32, "yo")

            def issue_loads(t0):
                xt, xtb = xt_p.next()
                k.dma("sp", xt[:, :], self.xtok[t0:t0 + 128, :], writes=[xtb])
                bt, btb = bt_p.next()
                k.dma("sp", bt[:, :], self.btok[t0:t0 + 128, :], writes=[btb])
                bT, bTb = bT_p.next()
                k.dma("sp", bT[:, :, :], self.bcT[0:1024, t0:t0 + 128].rearrange("(g p) t -> p g t", p=128), writes=[bTb])
                cT, cTb = cT_p.next()
                k.dma("sp", cT[:, :, :], self.bcT[1024:2048, t0:t0 + 128].rearrange("(g p) t -> p g t", p=128), writes=[cTb])
                dd, ddb = dd_p.next()
                k.dma("sp", dd[:, :, :], self.dtk[t0:t0 + 128, :, :], writes=[ddb])
                return dict(xt=xt, xtb=xtb, bt=bt, btb=btb, bT=bT, bTb=bTb, cT=cT, cTb=cTb, dd=dd, ddb=ddb, t0=t0)

            for d in range(2):
                Lm = tri[:, 0 if d == 0 else 2, :]
                Um = tri[:, 1 if d == 0 else 3, :]

                def prologue(c, d=d, Lm=Lm, Um=Um):
                    c["dt"] = c["dd"][:, 0, d * 64:(d + 1) * 64]
                    c["dta"] = c["dd"][:, 1, d * 64:(d + 1) * 64]
                    psc, pscb = self.ps.next()
                    k.mm([_mmf(psc[:, 0:64], Um, c["dta"]), _mmf(psc[:, 64:128], Lm, c["dta"]), _mmf(psc[:, 128:192], self.ones_f[:, :], c["dta"])],
                         [prm, c["ddb"], self.cb], [pscb])
                    sc, scb = sc_p.next()
                    k.op("act", _act(sc[:, :], psc[:, 0:192], AF.Exp), [pscb], [scb])
                    xdt, xdtb = xdt_p.next()
                    k.op("dve", _tt(v3(xdt[:, :], 64), v3(c["xt"][:, :], 64), c["dt"].unsqueeze(2).broadcast_to([128, 64, 64]), ALU.mult), [c["xtb"], c["ddb"]], [xdtb])
                    xw, xwb = xw_p.next()
                    k.op("dve", _tt(v3(xw[:, :], 64), v3(xdt[:, :], 64), sc[:, 64:128].unsqueeze(2).broadcast_to([128, 64, 64]), ALU.mult), [xdtb, scb], [xwb])
                    c.update(sc=sc, scb=scb, xdt=xdt, xdtb=xdtb, xw=xw, xwb=xwb)

                def stage_a(c, g, Lm=Lm, Um=Um):
                    psS, psSb = self.ps.next()
                    k.mm([_mmf(psS[:, 0:128], c["bT"][:, g, :], c["cT"][:, g, :])], [c["bTb"], c["cTb"]], [psSb])
                    SM, SMb = SM_p.next()
                    k.op("dve", _tt(SM[:, :], psS[:, 0:128], Um, ALU.mult), [psSb, prm], [SMb])
                    Es = []
                    for hf in range(2):
                        h0_ = g * 8 + hf * 4
                        dU, dUb = dU_p.next()
                        k.op("dve", _tt(v3(dU[:, :], 4), Um.unsqueeze(1).broadcast_to([128, 4, 128]),
                                        c["dta"][:, h0_:h0_ + 4].unsqueeze(2).broadcast_to([128, 4, 128]), ALU.mult), [prm, c["ddb"]], [dUb])
                        psE, psEb = self.ps.next()
                        k.mm([_mmf(psE[:, :], Lm, dU[:, :])], [prm, dUb], [psEb])
                        E, Eb = E_p.next()
                        k.op("act", _act(E[:, :], psE[:, :], AF.Exp), [psEb], [Eb])
                        Es.append((E, Eb))
                    return dict(SM=SM, SMb=SMb, Es=Es)

                def stage_b(c, g, A):
                    gs = slice(g * 512, (g + 1) * 512)
                    MTs = []
                    for hf in range(2):
                        E, Eb = A["Es"][hf]
                        MT, MTb = MT_p.next()
                        k.op("dve", _tt(v3(MT[:, :], 4), v3(E[:, :], 4), A["SM"][:, :].unsqueeze(1).broadcast_to([128, 4, 128]), ALU.mult), [Eb, A["SMb"]], [MTb])
                        MTs.append((MT, MTb))
                    psY, psYb = self.psacc.next()
                    fns = [_mmf(psY[:, h * 64:(h + 1) * 64], MTs[h // 4][0][:, (h % 4) * 128:(h % 4 + 1) * 128],
                                c["xdt"][:, (g * 8 + h) * 64:(g * 8 + h + 1) * 64]) for h in range(8)]
                    k.mm(fns, [MTs[0][1], MTs[1][1], c["xdtb"]], [psYb])
                    psI, psIb = self.psacc.next()
                    k.mm([_mmf(psI[:, :], c["cT"][:, g, :], stH[:, gs])], [c["cTb"], stHb[g]], [psIb])
                    A.update(psY=psY, psYb=psYb, psI=psI, psIb=psIb)

                def stage_c(c, g, A, d=d):
                    gs = slice(g * 512, (g + 1) * 512)
                    sc, scb = c["sc"], c["scb"]
                    psD, psDb = self.ps.next()
                    k.mm([_mmf(psD[:, :], c["bt"][:, g * 128:(g + 1) * 128], c["xw"][:, gs])], [c["btb"], c["xwb"]], [psDb])
                    yo, yob = y_p.next()
                    k.op("dve", _tt(v3(yo[:, :], 8), v3(A["psI"][:, :], 8), sc[:, g * 8:(g + 1) * 8].unsqueeze(2).broadcast_to([128, 8, 64]), ALU.mult),
                         [A["psIb"], scb], [yob])
                    k.op("dve", _tt(yo[:, :], yo[:, :], A["psY"][:, :], ALU.add), [yob, A["psYb"]], [yob])
                    k.dma("sp", self.ytok[d, c["t0"]:c["t0"] + 128, gs], yo[:, :], reads=[yob])
                    k.op("dve", _tt(v3(stT[:, gs], 8), v3(stT[:, gs], 8), sc[:, 128 + g * 8:128 + (g + 1) * 8].unsqueeze(2).broadcast_to([128, 8, 64]), ALU.mult),
                         [scb, stTb[g]], [stTb[g]])
                    k.op("dve", _tt(stT[:, gs], stT[:, gs], psD[:, :], ALU.add), [stTb[g], psDb], [stTb[g]])
                    k.op("act", _act(stH[:, gs], stT[:, gs], AF.Copy), [stTb[g]], [stHb[g]])

                for si, (s0, SL, _) in enumerate(SEGS):
                    nch = SL // 128
                    order = list(range(nch)) if d == 0 else list(range(nch - 1, -1, -1))
                    if self.cfg.get("ssd_nch"):
                        order = order[:self.cfg["ssd_nch"]]
                    t0s = [s0 + ci * 128 for ci in order]
                    ctxs = {0: issue_loads(t0s[0])}
                    if len(t0s) > 1:
                        ctxs[1] = issue_loads(t0s[1])
                    if si == 0:
                        for half in range(2):
                            k.dma("sp", hst[:, half * 16:(half + 1) * 16, :],
                                  self.ins["state_ssm"][d, half * 2048:(half + 1) * 2048, :].rearrange("(b p) n -> p b n", p=128), writes=[hstb])
                        for r4 in range(8):
                            ps, pb = self.ps.next()
                            k.mm([_tp(ps[:, i * 128:(i + 1) * 128], hst[:, r4 * 4 + i, :], self.ident[:, :]) for i in range(4)], [hstb, self.cb], [pb])
                            k.op("dve", _cp(stT[:, r4 * 512:(r4 + 1) * 512], ps[:, :]), [pb], [stTb[r4]])
                            k.op("act", _act(stH[:, r4 * 512:(r4 + 1) * 512], stT[:, r4 * 512:(r4 + 1) * 512], AF.Copy), [stTb[r4]], [stHb[r4]])
                    else:
                        for g in range(8):
                            k.op("dve", lambda h, g=g: h.memset(stT[:, g * 512:(g + 1) * 512], 0.0), writes=[stTb[g]])
                            k.op("dve", lambda h, g=g: h.memset(stH[:, g * 512:(g + 1) * 512], 0.0), writes=[stHb[g]])
                    prologue(ctxs[0])
                    items = [(vi, g) for vi in range(len(t0s)) for g in range(8)]
                    As = {}
                    for step in range(len(items) + 2):
                        if step < len(items):
                            vi, g = items[step]
                            if g == 4 and vi + 1 < len(t0s):
                                if vi + 2 < len(t0s):
                                    ctxs[vi + 2] = issue_loads(t0s[vi + 2])
                                prologue(ctxs[vi + 1])
                            As[step] = stage_a(ctxs[vi], g)
                        if 0 <= step - 1 < len(items):
                            vi, g = items[step - 1]
                            stage_b(ctxs[vi], g, As[step - 1])
                        if 0 <= step - 2 < len(items):
                            vi, g = items[step - 2]
                            stage_c(ctxs[vi], g, As[step - 2])
                            del As[step - 2]
                    if si > 0:
                        for r4 in range(8):
                            ps, pb = self.ps.next()
                            k.mm([_tp(ps[:, i * 128:(i + 1) * 128], stT[:, (r4 * 4 + i) * 128:(r4 * 4 + i + 1) * 128], self.ident[:, :]) for i in range(4)],
                                 [stTb[r4], self.cb], [pb])
                            o, ob = y_p.next()
                            k.op("dve", _cp(o[:, :], ps[:, :]), [pb], [ob])
                            k.dma("sp", self.outs["nssm"][si - 1, d, r4 * 512:(r4 + 1) * 512, :].rearrange("(b p) n -> p b n", p=128),
                                  o[:, :].rearrange("p (b n) -> p b n", b=4), reads=[ob])
        if self.cfg.get("ssd_stop") == 3:
            return
        with k.phase():
            prm = Buf()
            dsk = k.sb([128, 64], F32, "dsk")
            k.dma("sp", dsk[:, :], self.ins["ssd_d"][jl].partition_broadcast(128), writes=[prm], nc_ok=True)
            nw = k.sb([128, 32], F32, "nw")
            k.dma("sp", nw[:, :], self.ins["ssd_norm"][jl].rearrange("(c p) -> p c", p=128), writes=[prm], nc_ok=True)
            yf_p = k.sbpool(2, [128, 4096], F32, "yf")
            yb_p = k.sbpool(2, [128, 4096], F32, "yb2")
            xt_p = k.sbpool(2, [128, 4096], BF16, "xt2")
            zs_p = k.sbpool(2, [128, 32, 128], BF16, "zs")
            gt_p = k.sbpool(1, [128, 32, 128], F32, "gt")
            yo_p = k.sbpool(2, [128, 32, 128], BF16, "yo2")
            def load_blk(t0):
                yf, yfb = yf_p.next()
                k.dma("sp", yf[:, :], self.ytok[0, t0:t0 + 128, :], writes=[yfb])
                y2, y2b = yb_p.next()
                k.dma("sp", y2[:, :], self.ytok[1, t0:t0 + 128, :], writes=[y2b])
                xt, xtb = xt_p.next()
                k.dma("sp", xt[:, :], self.xtok[t0:t0 + 128, :], writes=[xtb])
                zs, zsb = zs_p.next()
                for half in range(2):
                    k.dma("sp", zs[:, half * 16:(half + 1) * 16, :],
                          self.zsT[half * 2048:(half + 1) * 2048, t0:t0 + 128].rearrange("(c p) t -> p c t", p=128), writes=[zsb])
                return (yf, yfb, y2, y2b, xt, xtb, zs, zsb)
            nblk = load_blk(0)
            for b_ in range(T // 128):
                t0 = b_ * 128
                (yf, yfb, y2, y2b, xt, xtb, zs, zsb) = nblk
                if b_ + 1 < T // 128:
                    nblk = load_blk(t0 + 128)
                k.op("dve", _tt(yf[:, :], yf[:, :], y2[:, :], ALU.add), [yfb, y2b], [yfb])
                k.op("dve", _tt(v3(y2[:, :], 64), v3(xt[:, :], 64), dsk[:, :].unsqueeze(2).broadcast_to([128, 64, 64]), ALU.mult), [xtb, prm], [y2b])
                k.op("dve", _tt(yf[:, :], yf[:, :], y2[:, :], ALU.add), [yfb, y2b], [yfb])
                gt, gtb = gt_p.next()
                psQ, psQb = self.psacc.next()
                for c4 in range(8):
                    ps, pb = self.ps.next()
                    k.mm([_tp(ps[:, i * 128:(i + 1) * 128], yf[:, (c4 * 4 + i) * 128:(c4 * 4 + i + 1) * 128], self.ident[:, :]) for i in range(4)],
                         [yfb, self.cb], [pb])
                    k.op("dve", _tt(gt[:, c4 * 4:(c4 + 1) * 4, :], v3(ps[:, :], 4), zs[:, c4 * 4:(c4 + 1) * 4, :], ALU.mult), [pb, zsb], [gtb])
                    sq, sqb = self.h512.next()
                    k.op("act", _act(v3(sq[:, :], 4), gt[:, c4 * 4:(c4 + 1) * 4, :], AF.Square), [gtb], [sqb])
                    k.mm([_mmf(psQ[:, 0:128], self.ones_h[:, :], sq[:, i * 128:(i + 1) * 128], start=(c4 == 0 and i == 0), stop=(c4 == 7 and i == 3))
                          for i in range(4)], [sqb, self.cb], [psQb])
                rs, rsb = self.rstd_from_sumsq(psQ, psQb, 128, 1.0 / 4096)
                k.op("dve", _tt(gt[:, :, :], gt[:, :, :], nw[:, :].unsqueeze(2).broadcast_to([128, 32, 128]), ALU.mult), [gtb, prm], [gtb])
                yo, yob = yo_p.next()
                k.op("dve", _tt(yo[:, :, :], gt[:, :, :], rs[:, 0:128].unsqueeze(1).broadcast_to([128, 32, 128]), ALU.mult), [gtb, rsb], [yob])
                for half in range(2):
                    k.dma("sp", self.oT[half * 2048:(half + 1) * 2048, t0:t0 + 128].rearrange("(c p) t -> p c t", p=128),
                          yo[:, half * 16:(half + 1) * 16, :], reads=[yob])
        self.out_proj(Wout, 32)


def host_consts():
    ident = np.eye(128, dtype=np.float32)
    rot = np.zeros((128, 128), np.float32)
    for m in range(64):
        rot[m + 64, m] = -1.0
        rot[m, m + 64] = 1.0
    rows = S_LEN // 64
    row = np.repeat(np.arange(rows, dtype=np.float32), 64)
    col = np.tile(np.arange(64, dtype=np.float32), rows)
    inv = (10000.0 ** (-np.arange(32, dtype=np.float32) / 32)).astype(np.float32)
    ang = np.concatenate([row[:, None] * inv, col[:, None] * inv], axis=-1).astype(np.float32)
    cos = np.cos(ang).T.astype(np.float32)
    sin = np.sin(ang).T.astype(np.float32)
    c_cos = np.ascontiguousarray(np.concatenate([cos, cos], 0))
    c_sin = np.ascontiguousarray(np.concatenate([sin, sin], 0))
    kk = np.arange(128)
    tri = np.stack([
        (kk[:, None] > kk[None, :]),
        (kk[:, None] <= kk[None, :]),
        (kk[:, None] < kk[None, :]),
        (kk[:, None] >= kk[None, :]),
    ]).astype(np.float32)
    return {"c_ident": ident, "c_rot": rot, "c_cos": c_cos, "c_sin": c_sin, "c_tri": tri}


def make_in_maps(inputs, cores):
    consts = host_consts()
    maps = []
    shared = {}
    for name in ("ada_w", "ada_b", "norm_mix", "norm_ffn", "attn_w_qkv", "attn_q_norm", "attn_k_norm", "attn_w_o",
                 "ssd_w_in", "ssd_conv_w", "ssd_conv_b", "ssd_d", "ssd_norm", "ssd_w_out",
                 "lru_w_in", "lru_conv_w", "lru_conv_b", "lru_w_a", "lru_b_a", "lru_w_i", "lru_b_i", "lru_a_param", "lru_w_out",
                 "ffn_w_in", "ffn_w_out", "final_norm"):
        shared[name] = np.ascontiguousarray(inputs[name], dtype=np.float32)
    shared["ssd_dt_bias"] = np.ascontiguousarray(inputs["ssd_dt_bias"], dtype=np.float32).reshape(1, 128)
    shared["ssd_a_log"] = np.ascontiguousarray(inputs["ssd_a_log"], dtype=np.float32).reshape(1, 128)
    shared.update(consts)
    for i in cores:
        m = dict(shared)
        m["x_all"] = np.ascontiguousarray(np.concatenate(
            [inputs["x_sample"][i], inputs["x_prompt"][2 * i], inputs["x_prompt"][2 * i + 1]], axis=0), dtype=np.float32)
        m["cache_k"] = np.ascontiguousarray(inputs["cache_k"][i].reshape(2, PAST, 512), dtype=np.float32)
        m["cache_v"] = np.ascontiguousarray(inputs["cache_v"][i].reshape(2, PAST, 512), dtype=np.float32)
        m["state_ssm"] = np.ascontiguousarray(inputs["state_ssm"][i, 0].reshape(2, 4096, 128), dtype=np.float32)
        m["state_lru"] = np.ascontiguousarray(inputs["state_lru"][i, 0], dtype=np.float32)
        m["cond"] = np.ascontiguousarray(np.stack([inputs["c"][i], inputs["c_ctx"]], 0), dtype=np.float32)
        maps.append(m)
    return maps


def run(inputs, cores, cfg):
    net = Net(cfg)
    nc = net.build()
    maps = make_in_maps(inputs, cores)
    res = run_bass_kernel_spmd(nc, maps, core_ids=list(range(len(cores))))
    return res


def kernel(**inputs):
    inputs = {k: np.asarray(v) for k, v in inputs.items()}
    cores = list(range(8))
    res = run(inputs, cores, {})
    B, BS = 16, 8
    y_prompt = np.zeros((B, P_LEN, D), np.float32)
    y_sample = np.zeros((BS, S_LEN, D), np.float32)
    nk = np.zeros((B, 2, P_LEN, 4, 128), np.float32)
    nv = np.zeros((B, 2, P_LEN, 4, 128), np.float32)
    nssm = np.zeros((B, 1, 2, 64, 64, 128), np.float32)
    nlru = np.zeros((B, 1, 2, D), np.float32)
    for i, r in enumerate(res.results):
        ya = r["y_all"]
        y_sample[i] = ya[:S_LEN]
        y_prompt[2 * i] = ya[S_LEN:S_LEN + P_LEN]
        y_prompt[2 * i + 1] = ya[S_LEN + P_LEN:]
        nk[2 * i:2 * i + 2] = r["nk"].reshape(2, 2, P_LEN, 4, 128)
        nv[2 * i:2 * i + 2] = r["nv"].reshape(2, 2, P_LEN, 4, 128)
        nssm[2 * i:2 * i + 2, 0] = r["nssm"].reshape(2, 2, 64, 64, 128)
        nlru[2 * i:2 * i + 2, 0] = r["nlru"].reshape(2, 2, D)
    return (y_prompt, y_sample, nk, nv, nssm, nlru)
```
